# Optimizing a Trainium2 kernel written in Bass

```python
import jax, jax.numpy as jnp
from jax import lax
import numpy as np

D_MODEL = 4096
BATCH = 4
SEQ = 4096
DEPTH = 1
DEC_BATCH = 1
DEC_SEQ = 8192
PAST_LEN = 128

HEAD_DIM = 128
N_Q_HEADS = 16
N_KV_HEADS = 4
GQA_GROUP = N_Q_HEADS // N_KV_HEADS
ATTN_WIDTH = N_Q_HEADS * HEAD_DIM
KV_WIDTH = N_KV_HEADS * HEAD_DIM
Q_BLOCK = 128
ROPE_THETA = 10000.0
AXIS_DIM = HEAD_DIM // 2
GRID_W = 64
D_RNN = D_MODEL // 2
LRU_BLOCKS = 16
LRU_BLOCK_W = D_RNN // LRU_BLOCKS
LRU_C = 8.0
CONV_W = 4
CONV_PAD_LO = 1
D_FF = 4 * D_MODEL
NORM_EPS = 1e-6
IN_WIDTH = ATTN_WIDTH + 2 * KV_WIDTH + 2 * D_RNN + 2 * D_MODEL

kernel_name = "hybrid_gated_rglru_axial_gqa_encoder"


def _rmsnorm(x, g):
    xf = x.astype(jnp.float32)
    y = xf * lax.rsqrt(jnp.mean(xf * xf, axis=-1, keepdims=True) + NORM_EPS)
    return (y * g.astype(jnp.float32)).astype(x.dtype)


def _axial_rope_tables(seq_len):
    n_rows = seq_len // GRID_W
    row = jnp.repeat(jnp.arange(n_rows), GRID_W).astype(jnp.float32)
    col = jnp.tile(jnp.arange(GRID_W), n_rows).astype(jnp.float32)
    inv = ROPE_THETA ** (-jnp.arange(0, AXIS_DIM, 2, dtype=jnp.float32) / AXIS_DIM)
    ang = jnp.concatenate([row[:, None] * inv, col[:, None] * inv], axis=-1)
    return jnp.cos(ang), jnp.sin(ang)


def _apply_rope(x, cos, sin):
    B, S, H, _ = x.shape
    xf = x.astype(jnp.float32).reshape(B, S, H, HEAD_DIM // 2, 2)
    c = cos[None, :, None, :]
    s = sin[None, :, None, :]
    x1, x2 = xf[..., 0], xf[..., 1]
    out = jnp.stack([x1 * c - x2 * s, x1 * s + x2 * c], axis=-1)
    return out.reshape(B, S, H, HEAD_DIM).astype(x.dtype)


def _block_attention(q, k, v):
    B, S = q.shape[:2]
    nblk = S // Q_BLOCK
    qb = q.reshape(B, nblk, Q_BLOCK, N_KV_HEADS, GQA_GROUP, HEAD_DIM).transpose(1, 0, 3, 4, 2, 5)
    scale = HEAD_DIM ** -0.5

    def one_block(qblk):
        s = jnp.einsum('bkgqd,bskd->bkgqs', qblk, k).astype(jnp.float32) * scale
        p = jax.nn.softmax(s, axis=-1).astype(v.dtype)
        return jnp.einsum('bkgqs,bskd->bkgqd', p, v)

    o = lax.map(one_block, qb)
    return o.transpose(1, 0, 4, 2, 3, 5).reshape(B, S, ATTN_WIDTH)


def _centred_dwconv(x, w, b):
    S = x.shape[1]
    xp = jnp.pad(x, ((0, 0), (CONV_PAD_LO, CONV_W - 1 - CONV_PAD_LO), (0, 0)))
    out = b
    for j in range(CONV_W):
        out = out + w[j] * xp[:, j:j + S]
    return out


def _lru_combine(left, right):
    a1, b1 = left
    a2, b2 = right
    return a1 * a2, a2 * b1 + b2


def _bidir_rg_lru(xc, w_a, b_a, w_i, b_i, lam):
    B, S, _ = xc.shape
    xf = xc.astype(jnp.float32)
    xb = xf.reshape(B, S, LRU_BLOCKS, LRU_BLOCK_W)
    r = jax.nn.sigmoid(jnp.einsum('bsnc,zncm->zbsnm', xb, w_a.astype(jnp.float32)).reshape(2, B, S, D_RNN)
                       + b_a.astype(jnp.float32)[:, None, None, :])
    i = jax.nn.sigmoid(jnp.einsum('bsnc,zncm->zbsnm', xb, w_i.astype(jnp.float32)).reshape(2, B, S, D_RNN)
                       + b_i.astype(jnp.float32)[:, None, None, :])
    log_a = -LRU_C * r * jax.nn.softplus(-lam.astype(jnp.float32))[:, None, None, :]
    a = jnp.exp(log_a)
    u = jnp.sqrt(-jnp.expm1(2.0 * log_a)) * (i * xf[None])
    _, h_fwd = lax.associative_scan(_lru_combine, (a[0], u[0]), axis=1)
    _, h_bwd = lax.associative_scan(_lru_combine, (a[1], u[1]), axis=1, reverse=True)
    return (h_fwd + h_bwd).astype(xc.dtype)


def _encoder_layer(x, norm_mix, w_in, q_norm, k_norm, conv_w, conv_b, lru_w_a, lru_b_a,
                   lru_w_i, lru_b_i, lru_lambda, w_attn_out, w_rnn_out, b_gate, w_out,
                   norm_mlp, w_up, w_down):
    B, S, _ = x.shape
    h = _rmsnorm(x, norm_mix)
    z = h @ w_in
    o1 = ATTN_WIDTH
    o2 = o1 + KV_WIDTH
    o3 = o2 + KV_WIDTH
    o4 = o3 + D_RNN
    o5 = o4 + D_RNN
    q, k, v, xr, yr, gl = jnp.split(z, [o1, o2, o3, o4, o5], axis=-1)

    cos, sin = _axial_rope_tables(S)
    q = _apply_rope(_rmsnorm(q.reshape(B, S, N_Q_HEADS, HEAD_DIM), q_norm), cos, sin)
    k = _apply_rope(_rmsnorm(k.reshape(B, S, N_KV_HEADS, HEAD_DIM), k_norm), cos, sin)
    v = v.reshape(B, S, N_KV_HEADS, HEAD_DIM)
    attn_branch = _block_attention(q, k, v) @ w_attn_out

    xc = _centred_dwconv(xr, conv_w, conv_b)
    rec = _bidir_rg_lru(xc, lru_w_a, lru_b_a, lru_w_i, lru_b_i, lru_lambda) * jax.nn.gelu(yr)
    rnn_branch = rec @ w_rnn_out

    gates = jax.nn.sigmoid(gl.reshape(B, S, 2, D_MODEL) + b_gate)
    merged = gates[:, :, 0] * attn_branch + gates[:, :, 1] * rnn_branch
    x = x + merged @ w_out

    hm = _rmsnorm(x, norm_mlp)
    x = x + jnp.square(jax.nn.relu(hm @ w_up)) @ w_down
    return x


def _trunk(x, norm_mix, w_in, q_norm, k_norm, conv_w, conv_b, lru_w_a, lru_b_a, lru_w_i,
           lru_b_i, lru_lambda, w_attn_out, w_rnn_out, b_gate, w_out, norm_mlp, w_up,
           w_down, norm_final):
    for l in range(DEPTH):
        x = _encoder_layer(x, norm_mix[l], w_in[l], q_norm[l], k_norm[l], conv_w[l], conv_b[l],
                           lru_w_a[l], lru_b_a[l], lru_w_i[l], lru_b_i[l], lru_lambda[l],
                           w_attn_out[l], w_rnn_out[l], b_gate[l], w_out[l], norm_mlp[l],
                           w_up[l], w_down[l])
    return _rmsnorm(x, norm_final)


def setup_inputs(seed: int = 0) -> dict:
    key = jax.random.key(seed)
    ks = jax.random.split(key, 24)
    f32 = jnp.float32

    def nrm(k, shape, fan_in):
        return jax.random.normal(k, shape, f32) * (fan_in ** -0.5)

    def gain(k, shape):
        return 1.0 + 0.02 * jax.random.normal(k, shape, f32)

    def small(k, shape):
        return 0.01 * jax.random.normal(k, shape, f32)

    u = jax.random.uniform(ks[13], (DEPTH, 2, D_RNN), f32, minval=0.9, maxval=0.999)
    a0 = u ** (1.0 / LRU_C)
    lru_lambda = jnp.log(a0) - jnp.log1p(-a0)
    return {
        "x_prompt": jax.random.normal(ks[0], (BATCH, SEQ, D_MODEL), f32),
        "x_sample": jax.random.normal(ks[1], (DEC_BATCH, DEC_SEQ, D_MODEL), f32),
        "norm_mix": gain(ks[2], (DEPTH, D_MODEL)),
        "w_in": nrm(ks[3], (DEPTH, D_MODEL, IN_WIDTH), D_MODEL),
        "q_norm": gain(ks[4], (DEPTH, HEAD_DIM)),
        "k_norm": gain(ks[5], (DEPTH, HEAD_DIM)),
        "conv_w": nrm(ks[6], (DEPTH, CONV_W, D_RNN), CONV_W),
        "conv_b": small(ks[7], (DEPTH, D_RNN)),
        "lru_w_a": nrm(ks[8], (DEPTH, 2, LRU_BLOCKS, LRU_BLOCK_W, LRU_BLOCK_W), LRU_BLOCK_W),
        "lru_b_a": small(ks[9], (DEPTH, 2, D_RNN)),
        "lru_w_i": nrm(ks[10], (DEPTH, 2, LRU_BLOCKS, LRU_BLOCK_W, LRU_BLOCK_W), LRU_BLOCK_W),
        "lru_b_i": small(ks[11], (DEPTH, 2, D_RNN)),
        "lru_lambda": lru_lambda,
        "w_attn_out": nrm(ks[14], (DEPTH, ATTN_WIDTH, D_MODEL), ATTN_WIDTH),
        "w_rnn_out": nrm(ks[15], (DEPTH, D_RNN, D_MODEL), D_RNN),
        "b_gate": small(ks[16], (DEPTH, 2, D_MODEL)),
        "w_out": nrm(ks[17], (DEPTH, D_MODEL, D_MODEL), D_MODEL),
        "norm_mlp": gain(ks[18], (DEPTH, D_MODEL)),
        "w_up": nrm(ks[19], (DEPTH, D_MODEL, D_FF), D_MODEL),
        "w_down": nrm(ks[20], (DEPTH, D_FF, D_MODEL), D_FF),
        "norm_final": gain(ks[21], (D_MODEL,)),
    }


def reference(x_prompt, x_sample, norm_mix, w_in, q_norm, k_norm, conv_w, conv_b, lru_w_a,
              lru_b_a, lru_w_i, lru_b_i, lru_lambda, w_attn_out, w_rnn_out, b_gate, w_out,
              norm_mlp, w_up, w_down, norm_final):
    y_prompt = _trunk(x_prompt, norm_mix, w_in, q_norm, k_norm, conv_w, conv_b, lru_w_a, lru_b_a,
                      lru_w_i, lru_b_i, lru_lambda, w_attn_out, w_rnn_out, b_gate, w_out,
                      norm_mlp, w_up, w_down, norm_final)
    y_sample = _trunk(x_sample, norm_mix, w_in, q_norm, k_norm, conv_w, conv_b, lru_w_a, lru_b_a,
                      lru_w_i, lru_b_i, lru_lambda, w_attn_out, w_rnn_out, b_gate, w_out,
                      norm_mlp, w_up, w_down, norm_final)
    return (y_prompt, y_sample)
```

```python
import math
import numpy as np
import concourse.bass as bass
import concourse.mybir as mybir
from concourse.bass_utils import run_bass_kernel_spmd

F32 = mybir.dt.float32
BF16 = mybir.dt.bfloat16
ALU = mybir.AluOpType
AF = mybir.ActivationFunctionType
AX = mybir.AxisListType
ENG = ("pe", "act", "dve", "pool", "sp")
T = 512
EPS = 1e-6

FULL = dict(D=4096, NQ=16, NKV=4, SP=4096, SS=8192)


class Buf:
    __slots__ = ("w", "r")

    def __init__(self):
        self.w = None
        self.r = {}


class Prog:
    def __init__(self, psem, dsems):
        self.ops = {e: [] for e in ENG}
        self.cnt = {e: 0 for e in ENG}
        self.psem = psem
        self.waited = {e: {} for e in ENG}
        self.dsems = dsems
        self.dval = [0] * len(dsems)
        self.dnext = 0

    def _deps(self, eng, reads, writes):
        need = {}

        def add(t):
            if t is None:
                return
            k = id(t[0])
            if k not in need or need[k][1] < t[1]:
                need[k] = t

        for b in reads:
            add(b.w)
        for b in writes:
            add(b.w)
            for t in b.r.values():
                add(t)
        waits = []
        for k, (sem, val) in need.items():
            if self.waited[eng].get(k, 0) < val:
                self.waited[eng][k] = val
                waits.append((sem, val))
        return waits

    def _mark(self, tok, reads, writes):
        for b in reads:
            b.r[id(tok[0])] = tok
        for b in writes:
            b.w = tok
            b.r = {}

    def op(self, eng, fn, reads=(), writes=()):
        waits = self._deps(eng, reads, writes)
        self.cnt[eng] += 1
        sem = self.psem[eng]
        tok = (sem, self.cnt[eng])

        def run(e, waits=waits, fn=fn, sem=sem):
            for ws, wv in waits:
                e.wait_ge(ws, wv)
            fn(e).then_inc(sem, 1)

        self.ops[eng].append(run)
        self._mark(tok, reads, writes)
        return tok

    def dma(self, q, out_ap, in_ap, reads=(), writes=()):
        k = self.dnext
        self.dnext = (self.dnext + 1) % len(self.dsems)
        sem = self.dsems[k]
        prev = self.dval[k]
        self.dval[k] += 16
        tok = (sem, self.dval[k])
        waits = self._deps(q, reads, writes)
        if prev > 0 and self.waited[q].get(id(sem), 0) < prev:
            self.waited[q][id(sem)] = prev
            waits.append((sem, prev))

        def run(e, waits=waits, sem=sem, out_ap=out_ap, in_ap=in_ap):
            for ws, wv in waits:
                e.wait_ge(ws, wv)
            e.dma_start(out=out_ap, in_=in_ap).then_inc(sem, 16)

        self.ops[q].append(run)
        self._mark(tok, reads, writes)
        return tok

    def barrier(self):
        toks = [(self.psem[f], self.cnt[f]) for f in ENG if self.cnt[f] > 0]
        toks += [(s, v) for s, v in zip(self.dsems, self.dval) if v > 0]
        for e in ENG:
            waits = []
            for sem, val in toks:
                if sem is self.psem[e]:
                    continue
                if self.waited[e].get(id(sem), 0) < val:
                    self.waited[e][id(sem)] = val
                    waits.append((sem, val))
            if waits:
                def run(eng, waits=waits):
                    for ws, wv in waits:
                        eng.wait_ge(ws, wv)
                self.ops[e].append(run)


def dims(cfg):
    cfg = {k: v for k, v in cfg.items() if k not in ('stop', 'arena', 'nds', 'nocast', 'dbg', 'pad')}
    D, NQ, NKV, SP, SS = cfg["D"], cfg["NQ"], cfg["NKV"], cfg["SP"], cfg["SS"]
    d = dict(cfg)
    d.update(KC=D // 128, DR=D // 2, NB=D // 256, DFF=4 * D, FC=4 * D // 128, QW=NQ * 128, KW=NKV * 128,
             G=NQ // NKV, nPo=SP // 2 // T, nSo=SS // 8 // T, nPc=SP // T, nSc=SS // T)
    d["IN"] = d["QW"] + 2 * d["KW"] + 2 * d["DR"] + 2 * D
    d["nOwn"] = d["nPo"] + d["nSo"]
    off = {}
    o = 0
    for name, n in (("gmix", d["KC"]), ("gmlp", d["KC"]), ("convw", d["NB"] * 4), ("convb", d["NB"]),
                    ("ba", 2 * d["NB"]), ("bi", 2 * d["NB"]), ("lam", 2 * d["NB"]), ("bg", 2 * d["KC"]),
                    ("qn", 128), ("kn", 128), ("iota", 32)):
        off[name] = (o, n)
        o += n
    d["ppoff"] = off
    d["NPP"] = o
    return d


def build(cfg):
    c = dims(cfg)
    D, NQ, NKV, SP, SS = c["D"], c["NQ"], c["NKV"], c["SP"], c["SS"]
    KC, DR, NB, DFF, FC, QW, KW, G, IN = c["KC"], c["DR"], c["NB"], c["DFF"], c["FC"], c["QW"], c["KW"], c["G"], c["IN"]
    nPo, nSo, nPc, nSc, nOwn = c["nPo"], c["nSo"], c["nPc"], c["nSc"], c["nOwn"]
    NPP = c["NPP"]
    ppoff = c["ppoff"]
    nc = bass.Bass("TRN2", target_bir_lowering=False)

    def din(name, shape, dt=F32):
        return nc.dram_tensor(name, list(shape), dt, kind="ExternalInput").ap()

    xc = [din("xc_p", [SP, D]), din("xc_s", [SS, D])]
    xo = din("xo", [nOwn * T, D])
    xh = din("xh", [nOwn * 4, D])
    posc = [din("pos_p", [128, SP // 64]), din("pos_s", [128, SS // 64])]
    poso = din("pos_o", [128, nOwn * 8])
    selc = [din("sel_p", [128, nPo * nPc]), din("sel_s", [128, nSo * nSc])]
    w_in = din("w_in", [D, IN])
    w_ao = din("w_ao", [QW, D])
    w_ro = din("w_ro", [DR, D])
    w_out = din("w_out", [D, D])
    w_up = din("w_up", [D, DFF])
    w_dn = din("w_dn", [DFF, D])
    lwa = din("lru_wa", [2, NB, 128, 128])
    lwi = din("lru_wi", [2, NB, 128, 128])
    pp = din("pp", [128, NPP])
    gfin = din("gfin", [128, D])
    yout = nc.dram_tensor("y", [nOwn * T, D], F32, kind="ExternalOutput").ap()

    def dscr(name, shape, dt):
        return nc.dram_tensor(name, list(shape), dt).ap()

    wb_in = dscr("wb_in", [D, IN], BF16)
    wb_ao = dscr("wb_ao", [QW, D], BF16)
    wb_ro = dscr("wb_ro", [DR, D], BF16)
    wb_out = dscr("wb_out", [D, D], BF16)
    wb_up = dscr("wb_up", [D, DFF], BF16)
    wb_dn = dscr("wb_dn", [DFF, D], BF16)
    Sseq = [SP, SS]
    KTs = [dscr(f"KT{s}", [NKV, 128, Sseq[s]], BF16) for s in range(2)]
    Vs = [dscr(f"V{s}", [NKV, 128, Sseq[s] // 128, 128], BF16) for s in range(2)]
    x1s = dscr("x1s", [T, D], F32)

    ARENA = cfg.get('arena', 53000)
    import contextlib
    es = contextlib.ExitStack()
    arena = es.enter_context(nc.sbuf_tensor("arena", [128, ARENA], F32))
    psum = es.enter_context(nc.psum_tensor("psum", [128, 8, 512], F32))
    sem_objs = {e: es.enter_context(nc.semaphore("p_" + e)) for e in ENG}
    dsems = [es.enter_context(nc.semaphore(f"d{i}")) for i in range(cfg.get('nds', 8))]
    P = Prog(sem_objs, dsems)

    HOLE_LO, HOLE_HI = 12160, 12864

    class Alloc:
        def __init__(self, base):
            if isinstance(base, tuple):
                self.o1, self.o2 = base
            else:
                self.o1 = base
                self.o2 = max(base, HOLE_HI)

        @property
        def o(self):
            return (self.o1, self.o2)

        def f32(self, n):
            if self.o1 + n <= HOLE_LO:
                ap = arena[:, self.o1:self.o1 + n]
                self.o1 += n
                return ap
            ap = arena[:, self.o2:self.o2 + n]
            self.o2 += n
            assert self.o2 <= ARENA, (self.o2, ARENA)
            return ap

        def bf16(self, n):
            w = (n + 1) // 2
            return self.f32(w).bitcast(BF16)[:, 0:n]

    A0 = Alloc(0)
    pp_sb = A0.f32(NPP)
    ident = A0.bf16(128)
    ones_bf = A0.bf16(128)
    der = A0.f32(10 * NB + 2 * KC)
    NTc = [nPc, nSc]
    csel = A0.f32(2 * NB * nOwn)
    sel_sb = [A0.f32(nPo * nPc), A0.f32(nSo * nSc)]
    inv_sb = A0.f32(32)
    RING_N = 2
    KCP = 8
    ring = [A0.bf16(KCP * 512) for _ in range(RING_N)]
    ringB = [Buf() for _ in range(RING_N)]
    ring_i = [0]
    BASE = A0.o
    bankB = [Buf() for _ in range(8)]
    constB = Buf()

    junk = A0.f32(16)
    BASE = A0.o
    junkB = Buf()
    for apx in [xc[0], xc[1], xo, xh, posc[0], posc[1], poso, selc[0], selc[1], w_in, w_ao, w_ro, w_out, w_up, w_dn, pp, gfin]:
        P.dma("sp", junk[0:1, 0:8], apx[0:1, 0:8], writes=[junkB])
    for apx in [lwa, lwi]:
        P.dma("sp", junk[0:1, 0:8], apx[0, 0, 0:1, 0:8], writes=[junkB])

    def ppv(name, i=0, n=1):
        o, _ = ppoff[name]
        return pp_sb[:, o + i:o + i + n]

    def v3(ap, a):
        return ap.rearrange("p (a b) -> p a b", a=a)

    P.dma("sp", pp_sb, pp, writes=[constB])
    P.op("pool", lambda e: e.memset(ones_bf, 1.0), writes=[constB])
    P.op("pool", lambda e: e.memset(chalf, 0.5), writes=[constB])
    P.op("pool", lambda e: e.memset(cmhalf, -0.5), writes=[constB])
    P.op("pool", lambda e: e.memset(c64, 64.0), writes=[constB])
    P.op("pool", lambda e: e.memset(c2pi, 2 * math.pi), writes=[constB])
    idf = A0.f32(128)
    chalf = A0.f32(512)
    cmhalf = A0.f32(16)
    c64 = A0.f32(4)
    c2pi = A0.f32(256)
    ncg = (D + 511) // 512
    ssqP = A0.f32(4 * ncg)
    ssqP3 = ssqP.rearrange("p (t c) -> p t c", t=4)
    ssqB = Buf()
    scrX = Buf()
    outB = Buf()
    sumB = Buf()
    cselB = Buf()
    identf = idf
    BASE = A0.o
    ident_in = din("ident", [128, 128])
    P.dma("sp", idf, ident_in, writes=[constB])
    P.op("dve", lambda e: e.tensor_copy(out=ident, in_=idf), reads=[constB], writes=[constB])
    for s in range(2):
        P.dma("sp", sel_sb[s], selc[s], writes=[constB])
    def load_lw(al):
        lw_sb = al.bf16(4 * NB * 128)
        lw4 = lw_sb.rearrange("p (g z n m) -> p g z n m", g=2, z=2, n=NB)
        lwB = Buf()
        for gi, src in enumerate((lwa, lwi)):
            for z in range(2):
                P.dma("pool", lw4[:, gi, z], src[z].rearrange("n c m -> c n m"), writes=[lwB])
        return lw4, lwB
    o_hc, o_nhc, o_hba, o_hbi, o_hbg = 0, 2 * NB, 4 * NB, 6 * NB, 8 * NB
    hc = der[:, o_hc:o_hc + 2 * NB]
    nhc = der[:, o_nhc:o_nhc + 2 * NB]
    hba = der[:, o_hba:o_hba + 2 * NB]
    hbi = der[:, o_hbi:o_hbi + 2 * NB]
    hbg = der[:, o_hbg:o_hbg + 2 * KC]
    tmpd = der[:, o_hbg + 2 * KC:o_hbg + 2 * KC + 2 * NB]
    lam = ppv("lam", 0, 2 * NB)
    P.op("act", lambda e: e.activation(out=tmpd, in_=lam, func=AF.Exp, scale=-1.0), reads=[constB], writes=[constB])
    P.op("act", lambda e: e.activation(out=tmpd, in_=tmpd, func=AF.Ln, bias=1.0), reads=[constB], writes=[constB])
    P.op("dve", lambda e: e.tensor_scalar_mul(out=hc, in0=tmpd, scalar1=-4.0), reads=[constB], writes=[constB])
    P.op("dve", lambda e: e.tensor_scalar_mul(out=nhc, in0=tmpd, scalar1=4.0), reads=[constB], writes=[constB])
    P.op("dve", lambda e: e.tensor_scalar_mul(out=hba, in0=ppv("ba", 0, 2 * NB), scalar1=0.5), reads=[constB], writes=[constB])
    P.op("dve", lambda e: e.tensor_scalar_mul(out=hbi, in0=ppv("bi", 0, 2 * NB), scalar1=0.5), reads=[constB], writes=[constB])
    P.op("dve", lambda e: e.tensor_scalar_mul(out=hbg, in0=ppv("bg", 0, 2 * KC), scalar1=0.5), reads=[constB], writes=[constB])
    P.op("act", lambda e: e.activation(out=inv_sb, in_=ppv("iota", 0, 32), func=AF.Exp, scale=-math.log(10000.0) / 32.0),
         reads=[constB], writes=[constB])
    wB = {}
    for name, src, dst in (("in", w_in, wb_in), ("ao", w_ao, wb_ao), ("ro", w_ro, wb_ro), ("out", w_out, wb_out),
                           ("up", w_up, wb_up), ("dn", w_dn, wb_dn)):
        b = Buf()
        wB[name] = b
        R = src.shape[0]
        step = max(128, min(R, (8 << 20) // src.shape[1] // 128 * 128))
        for r0 in range(0, R, step):
            r1 = min(R, r0 + step)
            if not cfg.get('nocast'):
                P.dma("pool", dst[r0:r1, :], src[r0:r1, :], writes=[b])

    def finish():
        P.barrier()
        with nc.Block() as block:
            @block.tensor
            def _(e):
                for f in P.ops["pe"]:
                    f(e)

            @block.scalar
            def _(e):
                for f in P.ops["act"]:
                    f(e)

            @block.vector
            def _(e):
                for f in P.ops["dve"]:
                    f(e)

            @block.gpsimd
            def _(e):
                for f in P.ops["pool"]:
                    f(e)

            @block.sync
            def _(e):
                for f in P.ops["sp"]:
                    f(e)
        es.close()
        return nc


    dbgB = Buf()

    def dump(name, ap, bufs, dt=None):
        if not cfg.get('dbg'):
            return
        o = nc.dram_tensor("dbg_" + name, list(ap.shape), dt or ap.dtype, kind="ExternalOutput").ap()
        P.dma("sp", o, ap, reads=list(bufs), writes=[dbgB])

    if cfg.get('stop') == 'setup':
        return finish()
    def load_piece(Wb, wbuf, k0, nk, c0, cw):
        i = ring_i[0] % RING_N
        ring_i[0] += 1
        slot = v3(ring[i], KCP)
        P.dma("sp", slot[:, 0:nk, 0:cw], Wb[k0 * 128:(k0 + nk) * 128, c0:c0 + cw].rearrange("(k p) c -> p k c", p=128),
              reads=[wbuf], writes=[ringB[i]])
        return slot, ringB[i]

    def proj_fm(Wb, wbuf, Kc, c0, cw, rhs_fn, rbufs, banks, N=T):
        nch = cw // 128
        for k0 in range(0, Kc, KCP):
            nk = min(KCP, Kc - k0)
            slot, sb = load_piece(Wb, wbuf, k0, nk, c0, cw)

            def fn(e, slot=slot, k0=k0, nk=nk):
                ins = None
                for kk in range(nk):
                    for cc in range(nch):
                        ins = e.matmul(psum[:, banks[cc], 0:N], lhsT=slot[:, kk, cc * 128:(cc + 1) * 128], rhs=rhs_fn(k0 + kk),
                                       start=(k0 + kk == 0), stop=(k0 + kk == Kc - 1))
                return ins
            P.op("pe", fn, reads=[sb] + rbufs, writes=[bankB[b] for b in banks[:nch]])

    def proj_tm(Wb, wbuf, Kc, c0, cw, lhs_fn, rbufs, banks, ntb=4, mrows=128):
        for k0 in range(0, Kc, KCP):
            nk = min(KCP, Kc - k0)
            slot, sb = load_piece(Wb, wbuf, k0, nk, c0, cw)

            def fn(e, slot=slot, k0=k0, nk=nk):
                ins = None
                for kk in range(nk):
                    for tb in range(ntb):
                        ins = e.matmul(psum[0:mrows, banks[tb], 0:cw], lhsT=lhs_fn(k0 + kk, tb), rhs=slot[:, kk, 0:cw],
                                       start=(k0 + kk == 0), stop=(k0 + kk == Kc - 1))
                return ins
            P.op("pe", fn, reads=[sb] + rbufs, writes=[bankB[b] for b in banks[:ntb]])

    def psbf(bank):
        return psum[:, bank, :].bitcast(BF16)

    tp_i = [0]

    def transpose_blocks(srcs, src_bufs, dst_ap, dst_buf):
        bank = 6 + (tp_i[0] % 2)
        tp_i[0] += 1
        n = len(srcs)
        pv = v3(psbf(bank)[:, 0:n * 128], n)

        def fn(e):
            ins = None
            for j, sap in enumerate(srcs):
                ins = e.transpose(out=pv[:, j, :], in_=sap, identity=ident)
            return ins
        P.op("pe", fn, reads=src_bufs + [constB], writes=[bankB[bank]])
        return pv, bank

    XH = D if D <= 2048 else D // 2
    NXH = D // XH

    def norm_T(xsrc_rows, gname, hT, hTB, S):
        xst, hn, ss, xstB, hnB = S["xst"], S["hn"], S["ss"], S["xstB"], S["hnB"]
        g_o, _ = ppoff[gname]
        for tb in range(4):
            src = xsrc_rows(tb)
            for hh in range(NXH):
                P.dma("sp", xst, src[:, hh * XH:(hh + 1) * XH], writes=[xstB])
                P.op("act", lambda e, hh=hh: e.activation(out=hn[:, hh * XH:(hh + 1) * XH], in_=xst, func=AF.Square, accum_out=ss[:, 4 + hh:5 + hh]),
                     reads=[xstB], writes=[hnB, S["ssB"]])
            if NXH == 1:
                P.op("dve", lambda e: e.tensor_scalar(out=ss[:, 1:2], in0=ss[:, 4:5], scalar1=1.0 / D, scalar2=EPS, op0=ALU.mult, op1=ALU.add),
                     reads=[S["ssB"]], writes=[S["ssB"]])
            else:
                P.op("dve", lambda e: e.tensor_tensor(out=ss[:, 0:1], in0=ss[:, 4:5], in1=ss[:, 5:6], op=ALU.add), reads=[S["ssB"]], writes=[S["ssB"]])
                P.op("dve", lambda e: e.tensor_scalar(out=ss[:, 1:2], in0=ss[:, 0:1], scalar1=1.0 / D, scalar2=EPS, op0=ALU.mult, op1=ALU.add),
                     reads=[S["ssB"]], writes=[S["ssB"]])
            P.op("pool", lambda e: e.tensor_tensor(out=ss[:, 2:3], in0=ss[:, 1:2], in1=cmhalf[:, 0:1], op=ALU.pow),
                 reads=[S["ssB"], constB], writes=[S["ssB"]])
            for hh in range(NXH):
                if NXH > 1:
                    P.dma("sp", xst, src[:, hh * XH:(hh + 1) * XH], writes=[xstB])
                P.op("dve", lambda e, hh=hh: e.tensor_scalar(out=hn[:, hh * XH:(hh + 1) * XH], in0=xst, scalar1=ss[:, 2:3], scalar2=None, op0=ALU.mult),
                     reads=[xstB, S["ssB"]], writes=[hnB])
            for k0 in range(0, KC, 4):
                n = min(4, KC - k0)
                pv, bank = transpose_blocks([hn[:, (k0 + j) * 128:(k0 + j + 1) * 128] for j in range(n)], [hnB], None, None)
                gb = pp_sb[:, g_o + k0:g_o + k0 + n].unsqueeze(2).to_broadcast([128, n, 128])
                P.op("dve", lambda e, pv=pv, gb=gb, k0=k0, n=n, tb=tb: e.tensor_tensor(
                    out=hT[:, k0:k0 + n, tb * 128:(tb + 1) * 128], in0=pv, in1=gb, op=ALU.mult),
                    reads=[bankB[bank], constB], writes=[hTB])

    def hn_junk(S):
        return S["junk"]

    def rope_tables(pos_ap, S):
        rt, rtB = S["rt"], S["rtB"]
        pos8 = rt[:, 0:8]
        ang = v3(rt[:, 16:16 + 256], 4)
        angf = rt[:, 16:16 + 256]
        sargf = rt[:, 272:272 + 256]
        qi = rt[:, 528:528 + 256].bitcast(mybir.dt.int32)
        sin_t = v3(rt[:, 1040:1040 + 256], 4)
        cos_t = v3(rt[:, 1296:1296 + 256], 4)
        qf = rt[:, 784:784 + 256]
        P.dma("sp", pos8, pos_ap, writes=[rtB])
        for b in range(4):
            P.op("dve", lambda e, b=b: e.tensor_scalar(out=ang[:, b, 0:32], in0=inv_sb, scalar1=pos8[:, 2 * b:2 * b + 1], scalar2=None, op0=ALU.mult),
                 reads=[rtB, constB], writes=[rtB])
            P.op("dve", lambda e, b=b: e.tensor_scalar(out=ang[:, b, 32:64], in0=inv_sb, scalar1=pos8[:, 2 * b + 1:2 * b + 2], scalar2=None, op0=ALU.mult),
                 reads=[rtB, constB], writes=[rtB])
        for shift, dst in ((0.0, rt[:, 1040:1040 + 256]), (0.5 * math.pi, rt[:, 1296:1296 + 256])):
            P.op("dve", lambda e, shift=shift: e.tensor_scalar_add(out=sargf, in0=angf, scalar1=shift), reads=[rtB], writes=[rtB])
            P.op("dve", lambda e: e.tensor_scalar_mul(out=qf, in0=sargf, scalar1=1.0 / (2 * math.pi)), reads=[rtB], writes=[rtB])
            P.op("dve", lambda e: e.tensor_copy(out=qi, in_=qf), reads=[rtB], writes=[rtB])
            P.op("dve", lambda e: e.tensor_copy(out=qf, in_=qi), reads=[rtB], writes=[rtB])
            P.op("dve", lambda e: e.scalar_tensor_tensor(out=sargf, in0=qf, scalar=-2 * math.pi, in1=sargf, op0=ALU.mult, op1=ALU.add), reads=[rtB], writes=[rtB])
            P.op("dve", lambda e: e.tensor_scalar(out=qf, in0=sargf, scalar1=math.pi, scalar2=2 * math.pi, op0=ALU.is_gt, op1=ALU.mult), reads=[rtB], writes=[rtB])
            P.op("dve", lambda e: e.tensor_tensor(out=sargf, in0=sargf, in1=qf, op=ALU.subtract), reads=[rtB], writes=[rtB])
            P.op("act", lambda e, dst=dst: e.activation(out=dst, in_=sargf, func=AF.Sin), reads=[rtB], writes=[rtB])
        return sin_t, cos_t

    def qk_norm_rope(bank, H, gname, sin_b, cos_b, S, out_bf):
        kq, sq, st, B = S["kq"], S["sq"], S["qst"], S["qkB"]
        n = H * 128
        kq3 = v3(kq[:, 0:n], H)
        sq3 = v3(sq[:, 0:n], H)
        g_o, _ = ppoff[gname]
        grow = pp_sb[:, g_o:g_o + 128].unsqueeze(1).to_broadcast([128, H, 128])
        P.op("act", lambda e: e.activation(out=kq[:, 0:n], in_=psum[:, bank, 0:n], func=AF.Identity), reads=[bankB[bank]], writes=[B])
        P.op("pool", lambda e: e.tensor_tensor(out=sq[:, 0:n], in0=kq[:, 0:n], in1=kq[:, 0:n], op=ALU.mult), reads=[B], writes=[B])
        P.op("dve", lambda e: e.tensor_reduce(out=st[:, 0:H], in_=sq3, axis=AX.X, op=ALU.add), reads=[B], writes=[B])
        P.op("dve", lambda e: e.tensor_scalar(out=st[:, 0:H], in0=st[:, 0:H], scalar1=1.0 / 128, scalar2=EPS, op0=ALU.mult, op1=ALU.add), reads=[B], writes=[B])
        P.op("pool", lambda e: e.tensor_tensor(out=st[:, 0:H], in0=st[:, 0:H], in1=cmhalf[:, 0:H], op=ALU.pow), reads=[B, constB], writes=[B])
        P.op("dve", lambda e: e.tensor_tensor(out=kq3, in0=kq3, in1=st[:, 0:H].unsqueeze(2).to_broadcast([128, H, 128]), op=ALU.mult), reads=[B], writes=[B])
        P.op("pool", lambda e: e.tensor_tensor(out=kq3, in0=kq3, in1=grow, op=ALU.mult), reads=[B, constB], writes=[B])
        x1 = kq3[:, :, 0::2]
        x2 = kq3[:, :, 1::2]
        cb = cos_b.unsqueeze(1).to_broadcast([128, H, 64])
        sb = sin_b.unsqueeze(1).to_broadcast([128, H, 64])
        t1 = v3(sq[:, 0:H * 64], H)
        t2 = v3(sq[:, H * 64:H * 128], H)
        o3 = v3(out_bf[:, 0:n], H)
        oB = S["qoB"]
        P.op("dve", lambda e: e.tensor_tensor(out=t1, in0=x1, in1=cb, op=ALU.mult), reads=[B, S["rtB"]], writes=[B])
        P.op("pool", lambda e: e.tensor_tensor(out=t2, in0=x2, in1=sb, op=ALU.mult), reads=[B, S["rtB"]], writes=[B])
        P.op("dve", lambda e: e.tensor_tensor(out=o3[:, :, 0::2], in0=t1, in1=t2, op=ALU.subtract), reads=[B], writes=[oB])
        P.op("dve", lambda e: e.tensor_tensor(out=t1, in0=x1, in1=sb, op=ALU.mult), reads=[B, S["rtB"]], writes=[B])
        P.op("pool", lambda e: e.tensor_tensor(out=t2, in0=x2, in1=cb, op=ALU.mult), reads=[B, S["rtB"]], writes=[B])
        P.op("dve", lambda e: e.tensor_tensor(out=o3[:, :, 1::2], in0=t1, in1=t2, op=ALU.add), reads=[B], writes=[oB])

    LNAMES = ("xc", "tr", "ti", "a", "th", "a2", "m2", "hs", "hf")

    def mk_L(al):
        S_L = {}
        for nm in LNAMES:
            S_L[nm] = (al.f32(T), Buf())
        S_L["iu"] = S_L["a2"]
        S_L["u"] = S_L["th"]
        S_L["xcb"] = (al.bf16(T), Buf())
        return S_L

    dbgXE = [None]

    dbgnames = []

    def lru_block(xe, XEBuf, blk, L, lw4, lwB, mode, extra):
        xcf, xcB = L["xc"]
        tr, trB = L["tr"]
        ti, tiB = L["ti"]
        a, aB = L["a"]
        th, thB = L["th"]
        a2, a2B = L["a2"]
        m2, m2B = L["m2"]
        iu, iuB = L["iu"]
        u, uB = L["u"]
        hs, hsB = L["hs"]
        hf, hfB = L["hf"]
        xcb, xcbB = L["xcb"]

        def _dd(i):
            if mode == 'own' and dbgXE[0] is not None and extra[0] == 0 and blk == 0:
                dump('Y%d_%d' % (i, len(dbgnames)), dbgXE[0], [XEBuf] + [L[n][1] for n in L])
                dbgnames.append(i)
        cw_o = ppoff["convw"][0]
        cb_o = ppoff["convb"][0]
        P.op("dve", lambda e: e.tensor_scalar(out=xcf, in0=xe[:, 0:T], scalar1=pp_sb[:, cw_o + blk * 4:cw_o + blk * 4 + 1],
                                              scalar2=pp_sb[:, cb_o + blk:cb_o + blk + 1], op0=ALU.mult, op1=ALU.add),
             reads=[XEBuf, constB], writes=[xcB])
        _dd(0)
        for j in range(1, 4):
            P.op("dve", lambda e, j=j: e.scalar_tensor_tensor(out=xcf, in0=xe[:, j:j + T], scalar=pp_sb[:, cw_o + blk * 4 + j:cw_o + blk * 4 + j + 1],
                                                          in1=xcf, op0=ALU.mult, op1=ALU.add),
                 reads=[XEBuf, constB, xcB], writes=[xcB])
            _dd(1)
        P.op("pool", lambda e: e.tensor_copy(out=xcb, in_=xcf), reads=[xcB], writes=[xcbB])
        _dd(2)
        for z in range(2):
            zi = z * NB + blk
            banks = (4, 5)

            def fn(e, z=z):
                e.matmul(psum[:, banks[0], 0:T], lhsT=lw4[:, 0, z, blk, :], rhs=xcb, start=True, stop=True)
                return e.matmul(psum[:, banks[1], 0:T], lhsT=lw4[:, 1, z, blk, :], rhs=xcb, start=True, stop=True)
            P.op("pe", fn, reads=[xcbB, lwB], writes=[bankB[banks[0]], bankB[banks[1]]])
            _dd(3)
            P.op("act", lambda e, zi=zi: e.activation(out=tr, in_=psum[:, banks[0], 0:T], func=AF.Tanh, scale=0.5, bias=hba[:, zi:zi + 1]),
                 reads=[bankB[banks[0]], constB], writes=[trB])
            _dd(4)
            P.op("act", lambda e, zi=zi: e.activation(out=ti, in_=psum[:, banks[1], 0:T], func=AF.Tanh, scale=0.5, bias=hbi[:, zi:zi + 1]),
                 reads=[bankB[banks[1]], constB], writes=[tiB])
            _dd(5)
            P.op("act", lambda e, zi=zi: e.activation(out=a, in_=tr, func=AF.Exp, scale=hc[:, zi:zi + 1], bias=hc[:, zi:zi + 1]),
                 reads=[trB, constB], writes=[aB])
            _dd(6)
            P.op("act", lambda e, zi=zi: e.activation(out=th, in_=tr, func=AF.Tanh, scale=nhc[:, zi:zi + 1], bias=nhc[:, zi:zi + 1]),
                 reads=[trB, constB], writes=[thB])
            _dd(7)
            P.op("pool", lambda e: e.tensor_tensor(out=a2, in0=a, in1=a, op=ALU.mult), reads=[aB], writes=[a2B])
            _dd(8)
            P.op("dve", lambda e: e.scalar_tensor_tensor(out=m2, in0=a2, scalar=1.0, in1=th, op0=ALU.add, op1=ALU.mult),
                 reads=[a2B, thB], writes=[m2B])
            _dd(9)
            P.op("dve", lambda e: e.tensor_scalar_max(out=m2, in0=m2, scalar1=0.0), reads=[m2B], writes=[m2B])
            _dd(10)
            P.op("pool", lambda e: e.tensor_tensor(out=m2, in0=m2, in1=chalf, op=ALU.pow), reads=[m2B, constB], writes=[m2B])
            _dd(11)
            P.op("dve", lambda e: e.scalar_tensor_tensor(out=iu, in0=ti, scalar=1.0, in1=xcf, op0=ALU.add, op1=ALU.mult),
                 reads=[tiB, xcB], writes=[iuB])
            _dd(12)
            P.op("dve", lambda e: e.scalar_tensor_tensor(out=u, in0=iu, scalar=0.5, in1=m2, op0=ALU.mult, op1=ALU.mult), reads=[iuB, m2B], writes=[uB])
            _dd(13)
            if mode == "ctx":
                sA4, sE4, t = extra
                if z == 0:
                    P.op("dve", lambda e: e.tensor_tensor_scan(out=hs, data0=a, data1=u, initial=0.0, op0=ALU.mult, op1=ALU.add),
                         reads=[aB, uB], writes=[hsB])
                    P.op("pool", lambda e: e.tensor_copy(out=sE4[:, 0, blk, t:t + 1], in_=hs[:, T - 1:T]), reads=[hsB], writes=[sumB])
                else:
                    P.op("dve", lambda e: e.tensor_tensor_scan(out=hs[:, ::-1], data0=a[:, ::-1], data1=u[:, ::-1], initial=0.0, op0=ALU.mult, op1=ALU.add),
                         reads=[aB, uB], writes=[hsB])
                    P.op("pool", lambda e: e.tensor_copy(out=sE4[:, 1, blk, t:t + 1], in_=hs[:, 0:1]), reads=[hsB], writes=[sumB])
                P.op("dve", lambda e, z=z: e.tensor_reduce(out=sA4[:, z, blk, t:t + 1], in_=a, axis=AX.X, op=ALU.mult),
                     reads=[aB], writes=[sumB])
            else:
                k, gy, gyB, rec, recB = extra
                cs4 = csel.rearrange("p (z n k) -> p z n k", z=2, n=NB)
                dd = (dbgXE[0] is not None and k == 0 and blk == 0)
                if z == 0:
                    if dd:
                        dump('X0', dbgXE[0], [XEBuf, uB, aB])
                    P.op("dve", lambda e: e.tensor_tensor_scan(out=hf, data0=a, data1=u, initial=cs4[:, 0, blk, k:k + 1], op0=ALU.mult, op1=ALU.add),
                         reads=[aB, uB, cselB], writes=[hfB])
                    if dd:
                        dump('X1', dbgXE[0], [XEBuf, hfB])
                else:
                    P.op("dve", lambda e: e.tensor_tensor_scan(out=hs[:, ::-1], data0=a[:, ::-1], data1=u[:, ::-1], initial=cs4[:, 1, blk, k:k + 1],
                                                               op0=ALU.mult, op1=ALU.add),
                         reads=[aB, uB, cselB], writes=[hsB])
                    if dd:
                        dump('X2', dbgXE[0], [XEBuf, hsB])
                    P.op("pool", lambda e: e.tensor_tensor(out=hs, in0=hs, in1=hf, op=ALU.add), reads=[hsB, hfB], writes=[hsB])
                    if dd:
                        dump('X3', dbgXE[0], [XEBuf, hsB])
                    P.op("dve", lambda e: e.tensor_tensor(out=rec[:, blk, :], in0=hs, in1=gy[:, blk, :], op=ALU.mult),
                         reads=[hsB, gyB], writes=[recB])
                    if dd:
                        dump('X4', dbgXE[0], [XEBuf, recB])

    def mk_S(al):
        S = {}
        S["xst"] = al.f32(XH); S["xstB"] = Buf()
        S["hn"] = al.bf16(D); S["hnB"] = Buf()
        S["junk"] = S["hn"]; S["junkB"] = S["hnB"]
        S["ss"] = al.f32(8); S["ssB"] = Buf()
        S["rt"] = al.f32(1552); S["rtB"] = Buf()
        S["kq"] = al.f32(512); S["sq"] = al.f32(512); S["qst"] = al.f32(16); S["qkB"] = Buf(); S["qoB"] = Buf()
        return S

    o_k, o_v, o_xr, o_yr, o_g = QW, QW + KW, QW + 2 * KW, QW + 2 * KW + DR, QW + 2 * KW + 2 * DR
    NTc = [nPc, nSc]
    scrB = [Buf(), Buf()]

    Actx = Alloc(BASE)
    hTc = v3(Actx.bf16(KC * T), KC); hTcB = Buf()
    Sc = mk_S(Actx)
    krb = Actx.bf16(512)
    KT_sb = v3(Actx.bf16(NKV * T), NKV); KT_B = Buf()
    v_sb = Actx.bf16(4 * KW); v_B = Buf()
    XE = [v3(Actx.f32(NB * (T + 4)), NB) for _ in range(2)]
    XEB = [Buf(), Buf()]
    Lc = mk_L(Actx)
    lw4c, lwBc = load_lw(Actx)
    sumA = [Actx.f32(2 * NB * NTc[s]).rearrange("p (z n t) -> p z n t", z=2, n=NB) for s in range(2)]
    sumE = [Actx.f32(2 * NB * NTc[s]).rearrange("p (z n t) -> p z n t", z=2, n=NB) for s in range(2)]
    carC = [Actx.f32(2 * NB * NTc[s]).rearrange("p (z n t) -> p z n t", z=2, n=NB) for s in range(2)]
    tmpc = Actx.f32(2 * NB * max(NTc))

    def ctx_lru(seq, t, last):
        xe = XE[t % 2]
        if t == 0:
            P.op("pool", lambda e: e.memset(xe[:, :, 0:1], 0.0), writes=[XEB[t % 2]])
        if last:
            P.op("pool", lambda e: e.memset(xe[:, :, T + 1:T + 3], 0.0), writes=[XEB[t % 2]])
        for blk in range(NB):
            lru_block(xe[:, blk, :], XEB[t % 2], blk, Lc, lw4c, lwBc, "ctx", (sumA[seq], sumE[seq], t))

    def ctx_tile(seq, t):
        S = Sc
        hT, hTB = hTc, hTcB
        norm_T(lambda tb: xc[seq][t * T + tb * 128:t * T + (tb + 1) * 128, :], "gmix", hT, hTB, S)
        sin_t, cos_t = rope_tables(posc[seq][:, t * 8:(t + 1) * 8], S)
        proj_tm(wb_in, wB["in"], KC, o_k, KW, lambda kc, tb: hT[:, kc, tb * 128:(tb + 1) * 128], [hTB], [0, 1, 2, 3])
        for tb in range(4):
            qk_norm_rope(tb, NKV, "kn", sin_t[:, tb, :], cos_t[:, tb, :], S, krb)
            pv, bank = transpose_blocks([krb[:, g * 128:(g + 1) * 128] for g in range(NKV)], [S["qoB"]], None, None)
            P.op("act", lambda e, pv=pv, tb=tb: e.activation(out=KT_sb[:, :, tb * 128:(tb + 1) * 128], in_=pv, func=AF.Identity),
                 reads=[bankB[bank]], writes=[KT_B])
        for g in range(NKV):
            P.dma("sp", KTs[seq][g][:, t * T:(t + 1) * T], KT_sb[:, g, :], reads=[KT_B], writes=[scrB[seq]])
        proj_tm(wb_in, wB["in"], KC, o_v, KW, lambda kc, tb: hT[:, kc, tb * 128:(tb + 1) * 128], [hTB], [0, 1, 2, 3])
        v3d = v3(v_sb, 4)
        for tb in range(4):
            P.op("act", lambda e, tb=tb: e.activation(out=v3d[:, tb, :], in_=psum[:, tb, 0:KW], func=AF.Identity), reads=[bankB[tb]], writes=[v_B])
        for g in range(NKV):
            P.dma("sp", Vs[seq][g][:, t * 4:(t + 1) * 4, :], v3d[:, :, g * 128:(g + 1) * 128], reads=[v_B], writes=[scrB[seq]])
        xe = XE[t % 2]
        for c0 in range(0, DR, 512):
            cw = min(512, DR - c0)
            proj_fm(wb_in, wB["in"], KC, o_xr + c0, cw, lambda kc: hT[:, kc, :], [hTB], [0, 1, 2, 3])
            for cc in range(cw // 128):
                P.op("act", lambda e, cc=cc, c0=c0: e.activation(out=xe[:, c0 // 128 + cc, 1:T + 1], in_=psum[:, cc, 0:T], func=AF.Identity),
                     reads=[bankB[cc]], writes=[XEB[t % 2]])
        if t > 0:
            xp = XE[(t - 1) % 2]
            P.op("pool", lambda e: e.tensor_copy(out=xp[:, :, T + 1:T + 3], in_=xe[:, :, 1:3]), reads=[XEB[t % 2]], writes=[XEB[(t - 1) % 2]])
            P.op("pool", lambda e: e.tensor_copy(out=xe[:, :, 0:1], in_=xp[:, :, T:T + 1]), reads=[XEB[(t - 1) % 2]], writes=[XEB[t % 2]])
            ctx_lru(seq, t - 1, False)

    for seq in range(2):
        for t in range(NTc[seq]):
            ctx_tile(seq, t)
            if cfg.get('stop') == 'ctx1':
                return finish()
        ctx_lru(seq, NTc[seq] - 1, True)
    if cfg.get('stop') == 'ctx':
        return finish()

    def carries(seq):
        nT = NTc[seq]
        A4, E4, C4 = sumA[seq], sumE[seq], carC[seq]
        P.op("dve", lambda e: e.memset(C4[:, 0, :, 0:1], 0.0), reads=[sumB], writes=[sumB])
        P.op("dve", lambda e: e.memset(C4[:, 1, :, nT - 1:nT], 0.0), reads=[sumB], writes=[sumB])
        for t in range(1, nT):
            P.op("dve", lambda e, t=t: e.tensor_tensor(out=C4[:, 0, :, t:t + 1], in0=A4[:, 0, :, t - 1:t], in1=C4[:, 0, :, t - 1:t], op=ALU.mult),
                 reads=[sumB], writes=[sumB])
            P.op("dve", lambda e, t=t: e.tensor_tensor(out=C4[:, 0, :, t:t + 1], in0=C4[:, 0, :, t:t + 1], in1=E4[:, 0, :, t - 1:t], op=ALU.add),
                 reads=[sumB], writes=[sumB])
        for t in range(nT - 2, -1, -1):
            P.op("dve", lambda e, t=t: e.tensor_tensor(out=C4[:, 1, :, t:t + 1], in0=A4[:, 1, :, t + 1:t + 2], in1=C4[:, 1, :, t + 1:t + 2], op=ALU.mult),
                 reads=[sumB], writes=[sumB])
            P.op("dve", lambda e, t=t: e.tensor_tensor(out=C4[:, 1, :, t:t + 1], in0=C4[:, 1, :, t:t + 1], in1=E4[:, 1, :, t + 1:t + 2], op=ALU.add),
                 reads=[sumB], writes=[sumB])

    def select_carry(k):
        seq = 0 if k < nPo else 1
        kk = k if k < nPo else k - nPo
        nT = NTc[seq]
        C3 = carC[seq].rearrange("p z n t -> p (z n) t")
        selk = sel_sb[seq][:, kk * nT:(kk + 1) * nT].unsqueeze(1).to_broadcast([128, 2 * NB, nT])
        t3 = tmpc[:, 0:2 * NB * nT].rearrange("p (zn t) -> p zn t", t=nT)
        P.op("dve", lambda e: e.tensor_tensor(out=t3, in0=C3, in1=selk, op=ALU.mult), reads=[sumB, constB], writes=[sumB])
        P.op("dve", lambda e: e.tensor_reduce(out=csel.rearrange("p (zn k) -> p zn k", k=nOwn)[:, :, k], in_=t3, axis=AX.X, op=ALU.add),
             reads=[sumB], writes=[cselB])

    for seq in range(2):
        carries(seq)
    for k in range(nOwn):
        select_carry(k)
    P.barrier()

    def own_tile(k):
        seq = 0 if k < nPo else 1
        Sq = Sseq[seq]
        nkb = Sq // 128
        Aa = Alloc(BASE)
        hT = v3(Aa.bf16(KC * T), KC); hTB = Buf()
        QT = v3(Aa.bf16(NQ * T), NQ); QTB = Buf()
        rec = v3(Aa.bf16(NB * T), NB); recB = Buf()
        KEEP = Aa.o
        S = mk_S(Aa)
        qrb = Aa.bf16(512)
        if cfg.get('pad'):
            Aa.f32(cfg['pad'])
        XEo = v3(Aa.f32(NB * (T + 4)), NB); XEoB = Buf()
        gy = v3(Aa.bf16(NB * T), NB); gyB = Buf()
        L = mk_L(Aa)
        lw4, lwB = load_lw(Aa)
        hhn = S["hn"]; hhB = S["hnB"]
        hhT = v3(Aa.bf16(KC * 4), KC); hhTB = Buf()
        xhs = S["xst"]; xhsB = S["xstB"]

        norm_T(lambda tb: xo[k * T + tb * 128:k * T + (tb + 1) * 128, :], "gmix", hT, hTB, S)
        sin_t, cos_t = rope_tables(poso[:, k * 8:(k + 1) * 8], S)
        xst4 = S["xst"][0:4, :]
        ss = S["ss"]
        for hh in range(NXH):
            P.dma("sp", xst4, xh[k * 4:(k + 1) * 4, hh * XH:(hh + 1) * XH], writes=[S["xstB"]])
            P.op("act", lambda e, hh=hh: e.activation(out=hhn[0:4, hh * XH:(hh + 1) * XH], in_=xst4, func=AF.Square, accum_out=ss[0:4, 4 + hh:5 + hh]),
                 reads=[S["xstB"]], writes=[hhB, S["ssB"]])
        if NXH == 1:
            P.op("dve", lambda e: e.tensor_scalar(out=ss[0:4, 1:2], in0=ss[0:4, 4:5], scalar1=1.0 / D, scalar2=EPS, op0=ALU.mult, op1=ALU.add), reads=[S["ssB"]], writes=[S["ssB"]])
        else:
            P.op("dve", lambda e: e.tensor_tensor(out=ss[0:4, 0:1], in0=ss[0:4, 4:5], in1=ss[0:4, 5:6], op=ALU.add), reads=[S["ssB"]], writes=[S["ssB"]])
            P.op("dve", lambda e: e.tensor_scalar(out=ss[0:4, 1:2], in0=ss[0:4, 0:1], scalar1=1.0 / D, scalar2=EPS, op0=ALU.mult, op1=ALU.add), reads=[S["ssB"]], writes=[S["ssB"]])
        P.op("pool", lambda e: e.tensor_tensor(out=ss[0:4, 2:3], in0=ss[0:4, 1:2], in1=cmhalf[0:4, 0:1], op=ALU.pow), reads=[S["ssB"], constB], writes=[S["ssB"]])
        for hh in range(NXH):
            if NXH > 1:
                P.dma("sp", xst4, xh[k * 4:(k + 1) * 4, hh * XH:(hh + 1) * XH], writes=[S["xstB"]])
            P.op("dve", lambda e, hh=hh: e.tensor_scalar(out=hhn[0:4, hh * XH:(hh + 1) * XH], in0=xst4, scalar1=ss[0:4, 2:3], scalar2=None, op0=ALU.mult),
                 reads=[S["xstB"], S["ssB"]], writes=[hhB])
        g_o = ppoff["gmix"][0]
        for k0 in range(0, KC, 4):
            n = min(4, KC - k0)
            bank = 6 + (tp_i[0] % 2)
            tp_i[0] += 1
            pv = v3(psbf(bank)[:, 0:n * 4], n)

            def fn(e, pv=pv, k0=k0, n=n):
                ins = None
                for j in range(n):
                    ins = e.transpose(out=pv[:, j, :], in_=hhn[0:4, (k0 + j) * 128:(k0 + j + 1) * 128], identity=ident[0:4, 0:4])
                return ins
            P.op("pe", fn, reads=[hhB, constB], writes=[bankB[bank]])
            gb = pp_sb[:, g_o + k0:g_o + k0 + n].unsqueeze(2).to_broadcast([128, n, 4])
            P.op("dve", lambda e, pv=pv, gb=gb, k0=k0, n=n: e.tensor_tensor(out=hhT[:, k0:k0 + n, :], in0=pv, in1=gb, op=ALU.mult),
                 reads=[bankB[bank], constB], writes=[hhTB])
        for c0 in range(0, QW, 512):
            cw = min(512, QW - c0)
            H = cw // 128
            proj_tm(wb_in, wB["in"], KC, c0, cw, lambda kc, tb: hT[:, kc, tb * 128:(tb + 1) * 128], [hTB], [0, 1, 2, 3])
            for tb in range(4):
                qk_norm_rope(tb, H, "qn", sin_t[:, tb, :], cos_t[:, tb, :], S, qrb)
                pv, bank = transpose_blocks([qrb[:, h * 128:(h + 1) * 128] for h in range(H)], [S["qoB"]], None, None)
                P.op("act", lambda e, pv=pv, tb=tb, c0=c0, H=H: e.activation(out=QT[:, c0 // 128:c0 // 128 + H, tb * 128:(tb + 1) * 128], in_=pv, func=AF.Identity),
                     reads=[bankB[bank]], writes=[QTB])
        for c0 in range(0, DR, 512):
            cw = min(512, DR - c0)
            proj_fm(wb_in, wB["in"], KC, o_xr + c0, cw, lambda kc: hT[:, kc, :], [hTB], [0, 1, 2, 3])
            for cc in range(cw // 128):
                P.op("act", lambda e, cc=cc, c0=c0: e.activation(out=XEo[:, c0 // 128 + cc, 1:T + 1], in_=psum[:, cc, 0:T], func=AF.Identity),
                     reads=[bankB[cc]], writes=[XEoB])
            proj_tm(wb_in, wB["in"], KC, o_xr + c0, cw, lambda kc, tb: hhT[:, kc, :], [hhTB], [4], ntb=1, mrows=4)
            P.op("act", lambda e, c0=c0, cw=cw: e.activation(out=xhs[0:4, c0:c0 + cw], in_=psum[0:4, 4, 0:cw], func=AF.Identity), reads=[bankB[4]], writes=[xhsB])
        for b0 in range(0, NB, 4):
            n = min(4, NB - b0)
            bank = 6 + (tp_i[0] % 2)
            tp_i[0] += 1
            pvf = v3(psum[:, bank, 0:n * 4], n)

            def fn(e, pvf=pvf, b0=b0, n=n):
                ins = None
                for j in range(n):
                    ins = e.transpose(out=pvf[:, j, :], in_=xhs[0:4, (b0 + j) * 128:(b0 + j + 1) * 128], identity=identf[0:4, 0:4])
                return ins
            P.op("pe", fn, reads=[xhsB, constB], writes=[bankB[bank]])
            P.op("act", lambda e, pvf=pvf, b0=b0, n=n: e.activation(out=XEo[:, b0:b0 + n, 0:1], in_=pvf[:, :, 0:1], func=AF.Identity), reads=[bankB[bank]], writes=[XEoB])
            P.op("act", lambda e, pvf=pvf, b0=b0, n=n: e.activation(out=XEo[:, b0:b0 + n, T + 1:T + 3], in_=pvf[:, :, 1:3], func=AF.Identity), reads=[bankB[bank]], writes=[XEoB])
        if k == 0:
            dump('XE', XEo, [XEoB])
        g_yv, g_yvB = L["xc"]
        g_y2, g_y2B = L["tr"]
        g_wv, g_wvB = L["ti"]
        g_tv, g_tvB = L["a"]
        for c0 in range(0, DR, 512):
            cw = min(512, DR - c0)
            proj_fm(wb_in, wB["in"], KC, o_yr + c0, cw, lambda kc: hT[:, kc, :], [hTB], [0, 1, 2, 3])
            for cc in range(cw // 128):
                blk = c0 // 128 + cc
                P.op("act", lambda e, cc=cc: e.activation(out=g_yv, in_=psum[:, cc, 0:T], func=AF.Identity), reads=[bankB[cc]], writes=[g_yvB])
                if k == 0 and blk == NB - 1:
                    dump('G0', XEo, [XEoB, g_yvB, g_y2B, g_wvB, g_tvB, gyB])
                P.op("dve", lambda e: e.tensor_tensor(out=g_y2, in0=g_yv, in1=g_yv, op=ALU.mult), reads=[g_yvB], writes=[g_y2B])
                if k == 0 and blk == NB - 1:
                    dump('G1', XEo, [XEoB, g_yvB, g_y2B, g_wvB, g_tvB, gyB])
                P.op("dve", lambda e: e.tensor_scalar(out=g_y2, in0=g_y2, scalar1=0.044715, scalar2=1.0, op0=ALU.mult, op1=ALU.add), reads=[g_y2B], writes=[g_y2B])
                if k == 0 and blk == NB - 1:
                    dump('G2', XEo, [XEoB, g_yvB, g_y2B, g_wvB, g_tvB, gyB])
                P.op("dve", lambda e: e.tensor_tensor(out=g_wv, in0=g_y2, in1=g_yv, op=ALU.mult), reads=[g_y2B, g_yvB], writes=[g_wvB])
                if k == 0 and blk == NB - 1:
                    dump('G3', XEo, [XEoB, g_yvB, g_y2B, g_wvB, g_tvB, gyB])
                P.op("act", lambda e: e.activation(out=g_tv, in_=g_wv, func=AF.Tanh, scale=math.sqrt(2.0 / math.pi)), reads=[g_wvB], writes=[g_tvB])
                if k == 0 and blk == NB - 1:
                    dump('G4', XEo, [XEoB, g_yvB, g_y2B, g_wvB, g_tvB, gyB])
                P.op("dve", lambda e: e.scalar_tensor_tensor(out=g_tv, in0=g_tv, scalar=1.0, in1=g_yv, op0=ALU.add, op1=ALU.mult), reads=[g_tvB, g_yvB], writes=[g_tvB])
                if k == 0 and blk == NB - 1:
                    dump('G5', XEo, [XEoB, g_yvB, g_y2B, g_wvB, g_tvB, gyB])
                P.op("dve", lambda e, blk=blk: e.tensor_scalar_mul(out=gy[:, blk, :], in0=g_tv, scalar1=0.5), reads=[g_tvB], writes=[gyB])
                if k == 0 and blk == NB - 1:
                    dump('G6', XEo, [XEoB, g_yvB, g_y2B, g_wvB, g_tvB, gyB])
        if k == 0:
            dump('XEb', XEo, [XEoB])
            dbgXE[0] = XEo
        for blk in range(NB):
            lru_block(XEo[:, blk, :], XEoB, blk, L, lw4, lwB, "own", (k, gy, gyB, rec, recB))
            if k == 0 and blk == 0:
                dump('XEc', XEo, [XEoB, L['hs'][1], recB])
        if k == 0:
            dump('XEd', XEo, [XEoB, recB])
        if k == 0:
            for nm in ('xc', 'tr', 'ti', 'a', 'm2', 'u', 'hs', 'hf'):
                dump('L_' + nm, L[nm][0], [L[nm][1]])
            dump('hT', hT, [hTB]); dump('QT', QT, [QTB]); dump('rec', rec, [recB]); dump('gy', gy, [gyB]); dump('csel', csel, [cselB])
            dump('KT', KTs[0], [scrB[0]]); dump('V', Vs[0], [scrB[0]])
        P.barrier()

        Ab = Alloc(KEEP)
        attn = v3(Ab.bf16(NQ * T), NQ); attnB = Buf()
        KEEP2 = Ab.o
        KTg = [Ab.bf16(Sq) for _ in range(2)]
        Vg = [Ab.bf16(Sq) for _ in range(2)]
        KVB = [Buf(), Buf()]
        PT = [Ab.bf16(T) for _ in range(3)]
        PTB = [Buf() for _ in range(3)]
        rcp = Ab.f32(T); rcpB = Buf()
        pi = 0
        for g in range(NKV):
            kt = KTg[g % 2]
            vg = v3(Vg[g % 2], nkb)
            P.dma("sp", kt, KTs[seq][g], reads=[scrB[seq]], writes=[KVB[g % 2]])
            P.dma("sp", vg, Vs[seq][g], reads=[scrB[seq]], writes=[KVB[g % 2]])
            for j in range(G):
                h = g * G + j
                for kb in range(nkb):
                    sbank = kb % 3
                    P.op("pe", lambda e, kt=kt, kb=kb, h=h, sbank=sbank: e.matmul(psum[:, sbank, 0:T], lhsT=kt[:, kb * 128:(kb + 1) * 128], rhs=QT[:, h, :], start=True, stop=True),
                         reads=[KVB[g % 2], QTB], writes=[bankB[sbank]])
                    pt = PT[pi % 3]
                    ptB = PTB[pi % 3]
                    pi += 1
                    P.op("act", lambda e, pt=pt, sbank=sbank: e.activation(out=pt, in_=psum[:, sbank, 0:T], func=AF.Exp, scale=1.0 / math.sqrt(128.0)),
                         reads=[bankB[sbank]], writes=[ptB])

                    def fn(e, vg=vg, kb=kb, pt=pt):
                        e.matmul(psum[:, 4, 0:T], lhsT=vg[:, kb, :], rhs=pt, start=(kb == 0), stop=(kb == nkb - 1))
                        return e.matmul(psum[:, 5, 0:T], lhsT=ones_bf, rhs=pt, start=(kb == 0), stop=(kb == nkb - 1))
                    P.op("pe", fn, reads=[KVB[g % 2], ptB, constB], writes=[bankB[4], bankB[5]])
                P.op("dve", lambda e: e.reciprocal(out=rcp, in_=psum[:, 5, 0:T]), reads=[bankB[5]], writes=[rcpB])
                P.op("dve", lambda e, h=h: e.tensor_tensor(out=attn[:, h, :], in0=psum[:, 4, 0:T], in1=rcp, op=ALU.mult), reads=[bankB[4], rcpB], writes=[attnB])
        if k == 0:
            dump('attn', attn, [attnB])
        P.barrier()

        Ac = Alloc(KEEP2)
        mg = v3(Ac.bf16(KC * T), KC); mgB = Buf()
        Asb = v3(Ac.f32(4 * T), 4); AsbB = Buf()
        Rsb = v3(Ac.f32(4 * T), 4); RsbB = Buf()
        tg = Ac.f32(T); tgB = Buf()
        xs = [Ac.f32(512) for _ in range(2)]; xsB = [Buf(), Buf()]
        for c0 in range(0, D, 512):
            cw = min(512, D - c0)
            nch = cw // 128
            proj_fm(wb_ao, wB["ao"], QW // 128, c0, cw, lambda kc: attn[:, kc, :], [attnB], [0, 1, 2, 3])
            for cc in range(nch):
                P.op("act", lambda e, cc=cc: e.activation(out=Asb[:, cc, :], in_=psum[:, cc, 0:T], func=AF.Identity), reads=[bankB[cc]], writes=[AsbB])
            proj_fm(wb_ro, wB["ro"], NB, c0, cw, lambda kc: rec[:, kc, :], [recB], [4, 5, 6, 7])
            for cc in range(nch):
                P.op("act", lambda e, cc=cc: e.activation(out=Rsb[:, cc, :], in_=psum[:, 4 + cc, 0:T], func=AF.Identity), reads=[bankB[4 + cc]], writes=[RsbB])
            for gi, (sbv, sbB) in enumerate(((Asb, AsbB), (Rsb, RsbB))):
                banks = [0, 1, 2, 3] if gi == 0 else [4, 5, 6, 7]
                proj_fm(wb_in, wB["in"], KC, o_g + gi * D + c0, cw, lambda kc: hT[:, kc, :], [hTB], banks)
                for cc in range(nch):
                    ch = c0 // 128 + cc
                    P.op("act", lambda e, cc=cc, ch=ch, gi=gi, banks=banks: e.activation(out=tg, in_=psum[:, banks[cc], 0:T], func=AF.Tanh, scale=0.5,
                                                                                      bias=hbg[:, gi * KC + ch:gi * KC + ch + 1]),
                         reads=[bankB[banks[cc]], constB], writes=[tgB])
                    P.op("dve", lambda e, cc=cc, sbv=sbv: e.scalar_tensor_tensor(out=sbv[:, cc, :], in0=tg, scalar=1.0, in1=sbv[:, cc, :], op0=ALU.add, op1=ALU.mult),
                         reads=[tgB, sbB], writes=[sbB])
            for cc in range(nch):
                ch = c0 // 128 + cc
                P.op("pool", lambda e, cc=cc, ch=ch: e.tensor_tensor(out=mg[:, ch, :], in0=Asb[:, cc, :], in1=Rsb[:, cc, :], op=ALU.add),
                     reads=[AsbB, RsbB], writes=[mgB])
        xi = 0
        for ci, c0 in enumerate(range(0, D, 512)):
            cw = min(512, D - c0)
            banks = [0, 1, 2, 3] if ci % 2 == 0 else [4, 5, 6, 7]
            proj_tm(wb_out, wB["out"], KC, c0, cw, lambda kc, tb: mg[:, kc, tb * 128:(tb + 1) * 128], [mgB], banks)
            for tb in range(4):
                xv = xs[xi % 2]; xB = xsB[xi % 2]; xi += 1
                P.dma("sp", xv[:, 0:cw], xo[k * T + tb * 128:k * T + (tb + 1) * 128, c0:c0 + cw], writes=[xB])
                P.op("dve", lambda e, xv=xv, tb=tb, cw=cw, banks=banks: e.scalar_tensor_tensor(out=xv[:, 0:cw], in0=psum[:, banks[tb], 0:cw], scalar=0.5, in1=xv[:, 0:cw],
                                                                                     op0=ALU.mult, op1=ALU.add),
                     reads=[bankB[banks[tb]], xB], writes=[xB])
                P.op("act", lambda e, xv=xv, tb=tb, ci=ci, cw=cw: e.activation(out=tg[:, 0:cw], in_=xv[:, 0:cw], func=AF.Square, accum_out=ssqP3[:, tb, ci:ci + 1]),
                     reads=[xB], writes=[tgB, ssqB])
                P.dma("sp", x1s[tb * 128:(tb + 1) * 128, c0:c0 + cw], xv[:, 0:cw], reads=[xB], writes=[scrX])
        if k == 0:
            dump('x1', x1s, [scrX]); dump('mg', mg, [mgB])
        P.barrier()

        Ad = Alloc(BASE)
        KH = max(4, KC // 2)
        hm_parts = [v3(Ad.bf16(KH * T), KH) for _ in range((KC + KH - 1) // KH)]
        hmTB = Buf()

        def hm_sl(k0, n):
            return hm_parts[k0 // KH][:, k0 % KH:k0 % KH + n]
        act = v3(Ad.bf16(FC * T), FC); actB = Buf()
        U0 = Ad.o
        hn = Ad.bf16(D); hnB = Buf()
        ysn = [Ad.f32(512) for _ in range(2)]; ysnB = [Buf(), Buf()]
        stt1 = Ad.f32(16); stt1B = Buf()
        g_o = ppoff["gmlp"][0]

        def rstd_from_ssq():
            for tb in range(4):
                P.op("dve", lambda e, tb=tb: e.tensor_reduce(out=stt1[:, tb:tb + 1], in_=ssqP3[:, tb, :], axis=AX.X, op=ALU.add), reads=[ssqB], writes=[stt1B])
            P.op("dve", lambda e: e.tensor_scalar(out=stt1[:, 0:4], in0=stt1[:, 0:4], scalar1=1.0 / D, scalar2=EPS, op0=ALU.mult, op1=ALU.add), reads=[stt1B], writes=[stt1B])
            P.op("pool", lambda e: e.tensor_tensor(out=stt1[:, 4:8], in0=stt1[:, 0:4], in1=cmhalf[:, 0:4], op=ALU.pow), reads=[stt1B, constB], writes=[stt1B])
        rstd_from_ssq()
        yi = 0
        for tb in range(4):
            for ci, c0 in enumerate(range(0, D, 512)):
                cw = min(512, D - c0)
                yv = ysn[yi % 2]; yB = ysnB[yi % 2]; yi += 1
                P.dma("sp", yv[:, 0:cw], x1s[tb * 128:(tb + 1) * 128, c0:c0 + cw], reads=[scrX], writes=[yB])
                P.op("dve", lambda e, yv=yv, tb=tb, c0=c0, cw=cw: e.tensor_scalar(out=hn[:, c0:c0 + cw], in0=yv[:, 0:cw], scalar1=stt1[:, 4 + tb:5 + tb], scalar2=None, op0=ALU.mult),
                     reads=[yB, stt1B], writes=[hnB])
            for k0 in range(0, KC, 4):
                n = min(4, KC - k0)
                pv, bank = transpose_blocks([hn[:, (k0 + j) * 128:(k0 + j + 1) * 128] for j in range(n)], [hnB], None, None)
                gb = pp_sb[:, g_o + k0:g_o + k0 + n].unsqueeze(2).to_broadcast([128, n, 128])
                P.op("dve", lambda e, pv=pv, gb=gb, k0=k0, n=n, tb=tb: e.tensor_tensor(out=hm_sl(k0, n)[:, :, tb * 128:(tb + 1) * 128], in0=pv, in1=gb, op=ALU.mult),
                     reads=[bankB[bank], constB], writes=[hmTB])
        P.barrier()
        Ad2 = Alloc(U0)
        rl = [Ad2.f32(T) for _ in range(2)]; rlB = [Buf(), Buf()]
        ys = [Ad2.f32(512) for _ in range(2)]; ysB = [Buf(), Buf()]
        gfs = Ad2.f32(512); gfB = Buf()
        stt = Ad2.f32(16); sttB = Buf()
        ri = 0
        for ci, c0 in enumerate(range(0, DFF, 512)):
            banks = [0, 1, 2, 3] if ci % 2 == 0 else [4, 5, 6, 7]
            proj_fm(wb_up, wB["up"], KC, c0, 512, lambda kc: hm_sl(kc, 1)[:, 0, :], [hmTB], banks)
            for cc in range(4):
                r = rl[ri % 2]; rB = rlB[ri % 2]; ri += 1
                P.op("act", lambda e, r=r, cc=cc, banks=banks: e.activation(out=r, in_=psum[:, banks[cc], 0:T], func=AF.Relu), reads=[bankB[banks[cc]]], writes=[rB])
                P.op("pool", lambda e, r=r, cc=cc, c0=c0: e.tensor_tensor(out=act[:, c0 // 128 + cc, :], in0=r, in1=r, op=ALU.mult), reads=[rB], writes=[actB])
        yi = 0
        for ci, c0 in enumerate(range(0, D, 512)):
            cw = min(512, D - c0)
            banks = [0, 1, 2, 3] if ci % 2 == 0 else [4, 5, 6, 7]
            proj_tm(wb_dn, wB["dn"], FC, c0, cw, lambda kc, tb: act[:, kc, tb * 128:(tb + 1) * 128], [actB], banks)
            for tb in range(4):
                yv = ys[yi % 2]; yB = ysB[yi % 2]; yi += 1
                P.dma("sp", yv[:, 0:cw], x1s[tb * 128:(tb + 1) * 128, c0:c0 + cw], reads=[scrX], writes=[yB])
                P.op("dve", lambda e, yv=yv, tb=tb, cw=cw, banks=banks: e.tensor_tensor(out=yv[:, 0:cw], in0=psum[:, banks[tb], 0:cw], in1=yv[:, 0:cw], op=ALU.add),
                     reads=[bankB[banks[tb]], yB], writes=[yB])
                P.op("act", lambda e, yv=yv, tb=tb, ci=ci, cw=cw: e.activation(out=rl[0][:, 0:cw], in_=yv[:, 0:cw], func=AF.Square, accum_out=ssqP3[:, tb, ci:ci + 1]),
                     reads=[yB], writes=[rlB[0], ssqB])
                P.dma("sp", x1s[tb * 128:(tb + 1) * 128, c0:c0 + cw], yv[:, 0:cw], reads=[yB], writes=[scrX])

        def rstd2():
            for tb in range(4):
                P.op("dve", lambda e, tb=tb: e.tensor_reduce(out=stt[:, tb:tb + 1], in_=ssqP3[:, tb, :], axis=AX.X, op=ALU.add), reads=[ssqB], writes=[sttB])
            P.op("dve", lambda e: e.tensor_scalar(out=stt[:, 0:4], in0=stt[:, 0:4], scalar1=1.0 / D, scalar2=EPS, op0=ALU.mult, op1=ALU.add), reads=[sttB], writes=[sttB])
            P.op("pool", lambda e: e.tensor_tensor(out=stt[:, 4:8], in0=stt[:, 0:4], in1=cmhalf[:, 0:4], op=ALU.pow), reads=[sttB, constB], writes=[sttB])
        rstd2()
        for ci, c0 in enumerate(range(0, D, 512)):
            cw = min(512, D - c0)
            P.dma("sp", gfs[:, 0:cw], gfin[:, c0:c0 + cw], writes=[gfB])
            for tb in range(4):
                yv = ys[yi % 2]; yB = ysB[yi % 2]; yi += 1
                P.dma("sp", yv[:, 0:cw], x1s[tb * 128:(tb + 1) * 128, c0:c0 + cw], reads=[scrX], writes=[yB])
                P.op("dve", lambda e, yv=yv, tb=tb, cw=cw: e.scalar_tensor_tensor(out=yv[:, 0:cw], in0=yv[:, 0:cw], scalar=stt[:, 4 + tb:5 + tb], in1=gfs[:, 0:cw],
                                                                               op0=ALU.mult, op1=ALU.mult),
                     reads=[yB, sttB, gfB], writes=[yB])
                P.dma("sp", yout[k * T + tb * 128:k * T + (tb + 1) * 128, c0:c0 + cw], yv[:, 0:cw], reads=[yB], writes=[outB])
        P.barrier()

    for k in range(nOwn):
        own_tile(k)
        if cfg.get('stop') == 'own1':
            return finish()

    return finish()


def host_inputs(cfg, inp):
    c = dims(cfg)
    D, SP, SS, KC, NB, DR = c["D"], c["SP"], c["SS"], c["KC"], c["NB"], c["DR"]
    nPo, nSo, nPc, nSc, nOwn = c["nPo"], c["nSo"], c["nPc"], c["nSc"], c["nOwn"]
    f = np.float32
    xp = np.asarray(inp["x_prompt"], f)
    xs = np.asarray(inp["x_sample"], f)
    pp = np.zeros((128, c["NPP"]), f)
    off = c["ppoff"]

    def put(name, arr):
        o, n = off[name]
        assert arr.shape == (128, n), (name, arr.shape, n)
        pp[:, o:o + n] = arr
    put("gmix", np.asarray(inp["norm_mix"], f)[0].reshape(KC, 128).T)
    put("gmlp", np.asarray(inp["norm_mlp"], f)[0].reshape(KC, 128).T)
    cw = np.asarray(inp["conv_w"], f)[0]
    put("convw", cw.reshape(4, NB, 128).transpose(2, 1, 0).reshape(128, NB * 4))
    put("convb", np.asarray(inp["conv_b"], f)[0].reshape(NB, 128).T)
    for nm, key in (("ba", "lru_b_a"), ("bi", "lru_b_i"), ("lam", "lru_lambda")):
        put(nm, np.asarray(inp[key], f)[0].reshape(2, NB, 128).transpose(2, 0, 1).reshape(128, 2 * NB))
    put("bg", np.asarray(inp["b_gate"], f)[0].reshape(2, KC, 128).transpose(2, 0, 1).reshape(128, 2 * KC))
    put("qn", np.broadcast_to(np.asarray(inp["q_norm"], f)[0][None, :], (128, 128)))
    put("kn", np.broadcast_to(np.asarray(inp["k_norm"], f)[0][None, :], (128, 128)))
    put("iota", np.broadcast_to(np.arange(32, dtype=f)[None, :], (128, 32)))
    def rowcol(pos):
        pos = np.asarray(pos, np.int64)
        return np.ascontiguousarray(np.stack([pos // 64, pos % 64], -1).reshape(128, -1).astype(f))
    shared = dict(
        w_in=np.ascontiguousarray(np.asarray(inp["w_in"], f)[0]), w_ao=np.ascontiguousarray(np.asarray(inp["w_attn_out"], f)[0]),
        w_ro=np.ascontiguousarray(np.asarray(inp["w_rnn_out"], f)[0]), w_out=np.ascontiguousarray(np.asarray(inp["w_out"], f)[0]),
        w_up=np.ascontiguousarray(np.asarray(inp["w_up"], f)[0]), w_dn=np.ascontiguousarray(np.asarray(inp["w_down"], f)[0]),
        lru_wa=np.ascontiguousarray(np.asarray(inp["lru_w_a"], f)[0]), lru_wi=np.ascontiguousarray(np.asarray(inp["lru_w_i"], f)[0]),
        pp=pp, gfin=np.ascontiguousarray(np.broadcast_to(np.asarray(inp["norm_final"], f)[None, :], (128, D))),
        ident=np.eye(128, dtype=f),
        pos_p=rowcol(np.arange(SP // 128)[None, :] * 128 + np.arange(128)[:, None]),
        pos_s=rowcol(np.arange(SS // 128)[None, :] * 128 + np.arange(128)[:, None]),
        xc_s=np.ascontiguousarray(xs[0]),
    )
    maps = []
    for core in range(8):
        sp, half = core // 2, core % 2
        p0 = half * (SP // 2)
        s0 = core * (SS // 8)
        xo = np.concatenate([xp[sp, p0:p0 + SP // 2], xs[0, s0:s0 + SS // 8]], 0)
        xh = np.zeros((nOwn * 4, D), f)
        pos_o = np.zeros((128, nOwn * 4), np.int64)
        sel_p = np.zeros((128, nPo * nPc), f)
        sel_s = np.zeros((128, nSo * nSc), f)
        for k in range(nOwn):
            if k < nPo:
                src, st, S_ = xp[sp], p0 + k * T, SP
                sel_p[:, k * nPc + st // T] = 1.0
            else:
                src, st, S_ = xs[0], s0 + (k - nPo) * T, SS
                sel_s[:, (k - nPo) * nSc + st // T] = 1.0
            for j, tpos in enumerate((st - 1, st + T, st + T + 1)):
                if 0 <= tpos < S_:
                    xh[k * 4 + j] = src[tpos]
            for tb in range(4):
                pos_o[:, k * 4 + tb] = st + tb * 128 + np.arange(128)
        m = dict(shared)
        m.update(xc_p=np.ascontiguousarray(xp[sp]), xo=np.ascontiguousarray(xo), xh=xh, pos_o=rowcol(pos_o), sel_p=sel_p, sel_s=sel_s)
        maps.append(m)
    return maps


def assemble(cfg, results, B=4):
    c = dims(cfg)
    D, SP, SS = c["D"], c["SP"], c["SS"]
    yp = np.zeros((B, SP, D), np.float32)
    ys = np.zeros((1, SS, D), np.float32)
    for core in range(8):
        y = np.asarray(results[core]["y"], np.float32)
        sp, half = core // 2, core % 2
        p0 = half * (SP // 2)
        s0 = core * (SS // 8)
        yp[sp, p0:p0 + SP // 2] = y[:SP // 2]
        ys[0, s0:s0 + SS // 8] = y[SP // 2:]
    return yp, ys


_NC_CACHE = {}


def run(cfg, inp, trace=False):
    key = tuple(sorted((k, str(v)) for k, v in cfg.items()))
    if key not in _NC_CACHE:
        _NC_CACHE[key] = build(cfg)
    nc = _NC_CACHE[key]
    maps = host_inputs(cfg, inp)
    res = run_bass_kernel_spmd(nc, maps, core_ids=list(range(8)), **({"trace": True} if trace else {}))
    return assemble(cfg, res.results), res


def kernel(**inputs):
    (yp, ys), _ = run(FULL, inputs)
    return yp, ys
```

```python
import math
import numpy as np
import concourse.bass as bass
import concourse.mybir as mybir
from concourse.bass_utils import run_bass_kernel_spmd

F32 = mybir.dt.float32
BF16 = mybir.dt.bfloat16
ALU = mybir.AluOpType
AF = mybir.ActivationFunctionType
AX = mybir.AxisListType
ENG = ("pe", "act", "dve", "pool", "sp")
T = 512
EPS = 1e-6

FULL = dict(D=4096, NQ=16, NKV=4, SP=4096, SS=8192)


class Buf:
    __slots__ = ("w", "r")

    def __init__(self):
        self.w = None
        self.r = {}


class Prog:
    def __init__(self, psem, dsems):
        self.ops = {e: [] for e in ENG}
        self.cnt = {e: 0 for e in ENG}
        self.psem = psem
        self.waited = {e: {} for e in ENG}
        self.dsems = dsems
        self.dval = [0] * len(dsems)
        self.dnext = 0

    def _deps(self, eng, reads, writes):
        need = {}

        def add(t):
            if t is None:
                return
            k = id(t[0])
            if k not in need or need[k][1] < t[1]:
                need[k] = t

        for b in reads:
            add(b.w)
        for b in writes:
            add(b.w)
            for t in b.r.values():
                add(t)
        waits = []
        for k, (sem, val) in need.items():
            if self.waited[eng].get(k, 0) < val:
                self.waited[eng][k] = val
                waits.append((sem, val))
        return waits

    def _mark(self, tok, reads, writes):
        for b in reads:
            b.r[id(tok[0])] = tok
        for b in writes:
            b.w = tok
            b.r = {}

    def op(self, eng, fn, reads=(), writes=()):
        waits = self._deps(eng, reads, writes)
        self.cnt[eng] += 1
        sem = self.psem[eng]
        tok = (sem, self.cnt[eng])

        def run(e, waits=waits, fn=fn, sem=sem):
            for ws, wv in waits:
                e.wait_ge(ws, wv)
            fn(e).then_inc(sem, 1)

        self.ops[eng].append(run)
        self._mark(tok, reads, writes)
        return tok

    def dma(self, q, out_ap, in_ap, reads=(), writes=()):
        k = self.dnext
        self.dnext = (self.dnext + 1) % len(self.dsems)
        sem = self.dsems[k]
        prev = self.dval[k]
        self.dval[k] += 16
        tok = (sem, self.dval[k])
        waits = self._deps(q, reads, writes)
        if prev > 0 and self.waited[q].get(id(sem), 0) < prev:
            self.waited[q][id(sem)] = prev
            waits.append((sem, prev))

        def run(e, waits=waits, sem=sem, out_ap=out_ap, in_ap=in_ap):
            for ws, wv in waits:
                e.wait_ge(ws, wv)
            e.dma_start(out=out_ap, in_=in_ap).then_inc(sem, 16)

        self.ops[q].append(run)
        self._mark(tok, reads, writes)
        return tok

    def barrier(self):
        toks = [(self.psem[f], self.cnt[f]) for f in ENG if self.cnt[f] > 0]
        toks += [(s, v) for s, v in zip(self.dsems, self.dval) if v > 0]
        for e in ENG:
            waits = []
            for sem, val in toks:
                if sem is self.psem[e]:
                    continue
                if self.waited[e].get(id(sem), 0) < val:
                    self.waited[e][id(sem)] = val
                    waits.append((sem, val))
            if waits:
                def run(eng, waits=waits):
                    for ws, wv in waits:
                        eng.wait_ge(ws, wv)
                self.ops[e].append(run)


def dims(cfg):
    cfg = {k: v for k, v in cfg.items() if k not in ('stop', 'arena', 'nds', 'nocast', 'dbg', 'pad')}
    D, NQ, NKV, SP, SS = cfg["D"], cfg["NQ"], cfg["NKV"], cfg["SP"], cfg["SS"]
    d = dict(cfg)
    d.update(KC=D // 128, DR=D // 2, NB=D // 256, DFF=4 * D, FC=4 * D // 128, QW=NQ * 128, KW=NKV * 128,
             G=NQ // NKV, nPo=SP // 2 // T, nSo=SS // 8 // T, nPc=SP // T, nSc=SS // T)
    d["IN"] = d["QW"] + 2 * d["KW"] + 2 * d["DR"] + 2 * D
    d["nOwn"] = d["nPo"] + d["nSo"]
    off = {}
    o = 0
    for name, n in (("gmix", d["KC"]), ("gmlp", d["KC"]), ("convw", d["NB"] * 4), ("convb", d["NB"]),
                    ("ba", 2 * d["NB"]), ("bi", 2 * d["NB"]), ("lam", 2 * d["NB"]), ("bg", 2 * d["KC"]),
                    ("qn", 128), ("kn", 128), ("iota", 32)):
        off[name] = (o, n)
        o += n
    d["ppoff"] = off
    d["NPP"] = o
    return d


def build(cfg):
    c = dims(cfg)
    D, NQ, NKV, SP, SS = c["D"], c["NQ"], c["NKV"], c["SP"], c["SS"]
    KC, DR, NB, DFF, FC, QW, KW, G, IN = c["KC"], c["DR"], c["NB"], c["DFF"], c["FC"], c["QW"], c["KW"], c["G"], c["IN"]
    nPo, nSo, nPc, nSc, nOwn = c["nPo"], c["nSo"], c["nPc"], c["nSc"], c["nOwn"]
    NPP = c["NPP"]
    ppoff = c["ppoff"]
    nc = bass.Bass("TRN2", target_bir_lowering=False)

    def din(name, shape, dt=F32):
        return nc.dram_tensor(name, list(shape), dt, kind="ExternalInput").ap()

    xc = [din("xc_p", [SP, D]), din("xc_s", [SS, D])]
    xo = din("xo", [nOwn * T, D])
    xh = din("xh", [nOwn * 4, D])
    posc = [din("pos_p", [128, SP // 64]), din("pos_s", [128, SS // 64])]
    poso = din("pos_o", [128, nOwn * 8])
    selc = [din("sel_p", [128, nPo * nPc]), din("sel_s", [128, nSo * nSc])]
    w_in = din("w_in", [D, IN])
    w_ao = din("w_ao", [QW, D])
    w_ro = din("w_ro", [DR, D])
    w_out = din("w_out", [D, D])
    w_up = din("w_up", [D, DFF])
    w_dn = din("w_dn", [DFF, D])
    lwa = din("lru_wa", [2, NB, 128, 128])
    lwi = din("lru_wi", [2, NB, 128, 128])
    pp = din("pp", [128, NPP])
    gfin = din("gfin", [128, D])
    yout = nc.dram_tensor("y", [nOwn * T, D], F32, kind="ExternalOutput").ap()

    def dscr(name, shape, dt):
        return nc.dram_tensor(name, list(shape), dt).ap()

    wb_in = dscr("wb_in", [D, IN], BF16)
    wb_ao = dscr("wb_ao", [QW, D], BF16)
    wb_ro = dscr("wb_ro", [DR, D], BF16)
    wb_out = dscr("wb_out", [D, D], BF16)
    wb_up = dscr("wb_up", [D, DFF], BF16)
    wb_dn = dscr("wb_dn", [DFF, D], BF16)
    Sseq = [SP, SS]
    KTs = [dscr(f"KT{s}", [NKV, 128, Sseq[s]], BF16) for s in range(2)]
    Vs = [dscr(f"V{s}", [NKV, 128, Sseq[s] // 128, 128], BF16) for s in range(2)]
    x1s = dscr("x1s", [T, D], F32)

    ARENA = cfg.get('arena', 53200)
    import contextlib
    es = contextlib.ExitStack()
    arena = es.enter_context(nc.sbuf_tensor("arena", [128, ARENA], F32))
    psum = es.enter_context(nc.psum_tensor("psum", [128, 8, 512], F32))
    sem_objs = {e: es.enter_context(nc.semaphore("p_" + e)) for e in ENG}
    dsems = [es.enter_context(nc.semaphore(f"d{i}")) for i in range(cfg.get('nds', 8))]
    P = Prog(sem_objs, dsems)

    HOLE_LO, HOLE_HI = 10 ** 9, 10 ** 9

    class Alloc:
        def __init__(self, base):
            if isinstance(base, tuple):
                self.o1, self.o2 = base
            else:
                self.o1 = base
                self.o2 = max(base, HOLE_HI)

        @property
        def o(self):
            return (self.o1, self.o2)

        def f32(self, n):
            if self.o1 + n <= HOLE_LO:
                assert self.o1 + n <= ARENA, (self.o1 + n, ARENA)
                ap = arena[:, self.o1:self.o1 + n]
                self.o1 += n
                return ap
            ap = arena[:, self.o2:self.o2 + n]
            self.o2 += n
            assert self.o2 <= ARENA, (self.o2, ARENA)
            return ap

        def bf16(self, n):
            w = (n + 1) // 2
            return self.f32(w).bitcast(BF16)[:, 0:n]

    A0 = Alloc(0)
    pp_sb = A0.f32(NPP)
    ident = A0.bf16(128)
    ones_bf = A0.bf16(128)
    der = A0.f32(10 * NB + 2 * KC)
    NTc = [nPc, nSc]
    csel = A0.f32(2 * NB * nOwn)
    sel_sb = [A0.f32(nPo * nPc), A0.f32(nSo * nSc)]
    inv_sb = A0.f32(32)
    RING_N = 2
    KCP = 8
    ring = [A0.bf16(KCP * 512) for _ in range(RING_N)]
    ringB = [Buf() for _ in range(RING_N)]
    ring_i = [0]
    BASE = A0.o
    bankB = [Buf() for _ in range(8)]
    constB = Buf()

    junk = A0.f32(16)
    BASE = A0.o
    junkB = Buf()
    for apx in [xc[0], xc[1], xo, xh, posc[0], posc[1], poso, selc[0], selc[1], w_in, w_ao, w_ro, w_out, w_up, w_dn, pp, gfin]:
        P.dma("sp", junk[0:1, 0:8], apx[0:1, 0:8], writes=[junkB])
    for apx in [lwa, lwi]:
        P.dma("sp", junk[0:1, 0:8], apx[0, 0, 0:1, 0:8], writes=[junkB])

    def ppv(name, i=0, n=1):
        o, _ = ppoff[name]
        return pp_sb[:, o + i:o + i + n]

    def v3(ap, a):
        return ap.rearrange("p (a b) -> p a b", a=a)

    P.dma("sp", pp_sb, pp, writes=[constB])
    P.op("pool", lambda e: e.memset(ones_bf, 1.0), writes=[constB])
    P.op("pool", lambda e: e.memset(chalf, 0.5), writes=[constB])
    P.op("pool", lambda e: e.memset(cmhalf, -0.5), writes=[constB])
    P.op("pool", lambda e: e.memset(c64, 64.0), writes=[constB])
    P.op("pool", lambda e: e.memset(c2pi, 2 * math.pi), writes=[constB])
    idf = A0.f32(128)
    chalf = A0.f32(512)
    cmhalf = A0.f32(16)
    c64 = A0.f32(4)
    c2pi = A0.f32(256)
    ncg = (D + 511) // 512
    ssqP = A0.f32(4 * ncg)
    ssqP3 = ssqP.rearrange("p (t c) -> p t c", t=4)
    ssqB = Buf()
    scrX = Buf()
    outB = Buf()
    sumB = Buf()
    cselB = Buf()
    identf = idf
    BASE = A0.o
    ident_in = din("ident", [128, 128])
    P.dma("sp", idf, ident_in, writes=[constB])
    P.op("dve", lambda e: e.tensor_copy(out=ident, in_=idf), reads=[constB], writes=[constB])
    for s in range(2):
        P.dma("sp", sel_sb[s], selc[s], writes=[constB])
    def load_lw(al):
        lw_sb = al.bf16(4 * NB * 128)
        lw4 = lw_sb.rearrange("p (g z n m) -> p g z n m", g=2, z=2, n=NB)
        lwB = Buf()
        for gi, src in enumerate((lwa, lwi)):
            for z in range(2):
                P.dma("pool", lw4[:, gi, z], src[z].rearrange("n c m -> c n m"), writes=[lwB])
        return lw4, lwB
    o_hc, o_nhc, o_hba, o_hbi, o_hbg = 0, 2 * NB, 4 * NB, 6 * NB, 8 * NB
    hc = der[:, o_hc:o_hc + 2 * NB]
    nhc = der[:, o_nhc:o_nhc + 2 * NB]
    hba = der[:, o_hba:o_hba + 2 * NB]
    hbi = der[:, o_hbi:o_hbi + 2 * NB]
    hbg = der[:, o_hbg:o_hbg + 2 * KC]
    tmpd = der[:, o_hbg + 2 * KC:o_hbg + 2 * KC + 2 * NB]
    lam = ppv("lam", 0, 2 * NB)
    P.op("act", lambda e: e.activation(out=tmpd, in_=lam, func=AF.Exp, scale=-1.0), reads=[constB], writes=[constB])
    P.op("act", lambda e: e.activation(out=tmpd, in_=tmpd, func=AF.Ln, bias=1.0), reads=[constB], writes=[constB])
    P.op("dve", lambda e: e.tensor_scalar_mul(out=hc, in0=tmpd, scalar1=-4.0), reads=[constB], writes=[constB])
    P.op("dve", lambda e: e.tensor_scalar_mul(out=nhc, in0=tmpd, scalar1=4.0), reads=[constB], writes=[constB])
    P.op("dve", lambda e: e.tensor_scalar_mul(out=hba, in0=ppv("ba", 0, 2 * NB), scalar1=0.5), reads=[constB], writes=[constB])
    P.op("dve", lambda e: e.tensor_scalar_mul(out=hbi, in0=ppv("bi", 0, 2 * NB), scalar1=0.5), reads=[constB], writes=[constB])
    P.op("dve", lambda e: e.tensor_scalar_mul(out=hbg, in0=ppv("bg", 0, 2 * KC), scalar1=0.5), reads=[constB], writes=[constB])
    P.op("act", lambda e: e.activation(out=inv_sb, in_=ppv("iota", 0, 32), func=AF.Exp, scale=-math.log(10000.0) / 32.0),
         reads=[constB], writes=[constB])
    wB = {}
    for name, src, dst in (("in", w_in, wb_in), ("ao", w_ao, wb_ao), ("ro", w_ro, wb_ro), ("out", w_out, wb_out),
                           ("up", w_up, wb_up), ("dn", w_dn, wb_dn)):
        wB[name] = []
        R = src.shape[0]
        step = max(128, min(R, (8 << 20) // src.shape[1] // 128 * 128))
        for r0 in range(0, R, step):
            r1 = min(R, r0 + step)
            b = Buf()
            wB[name].append(b)
            if not cfg.get('nocast'):
                P.dma("pool", dst[r0:r1, :], src[r0:r1, :], writes=[b])

    def finish():
        P.barrier()
        with nc.Block() as block:
            @block.tensor
            def _(e):
                for f in P.ops["pe"]:
                    f(e)

            @block.scalar
            def _(e):
                for f in P.ops["act"]:
                    f(e)

            @block.vector
            def _(e):
                for f in P.ops["dve"]:
                    f(e)

            @block.gpsimd
            def _(e):
                for f in P.ops["pool"]:
                    f(e)

            @block.sync
            def _(e):
                for f in P.ops["sp"]:
                    f(e)
        es.close()
        return nc


    dbgB = Buf()

    def dump(name, ap, bufs, dt=None):
        if not cfg.get('dbg'):
            return
        o = nc.dram_tensor("dbg_" + name, list(ap.shape), dt or ap.dtype, kind="ExternalOutput").ap()
        P.dma("sp", o, ap, reads=list(bufs), writes=[dbgB])

    if cfg.get('stop') == 'setup':
        return finish()
    def load_piece(Wb, wbuf, k0, nk, c0, cw):
        i = ring_i[0] % RING_N
        ring_i[0] += 1
        slot = v3(ring[i], KCP)
        P.dma("sp", slot[:, 0:nk, 0:cw], Wb[k0 * 128:(k0 + nk) * 128, c0:c0 + cw].rearrange("(k p) c -> p k c", p=128),
              reads=list(wbuf), writes=[ringB[i]])
        return slot, ringB[i]

    def proj_fm(Wb, wbuf, Kc, c0, cw, rhs_fn, rbufs, banks, N=T):
        nch = cw // 128
        for k0 in range(0, Kc, KCP):
            nk = min(KCP, Kc - k0)
            slot, sb = load_piece(Wb, wbuf, k0, nk, c0, cw)

            def fn(e, slot=slot, k0=k0, nk=nk):
                ins = None
                for kk in range(nk):
                    for cc in range(nch):
                        ins = e.matmul(psum[:, banks[cc], 0:N], lhsT=slot[:, kk, cc * 128:(cc + 1) * 128], rhs=rhs_fn(k0 + kk),
                                       start=(k0 + kk == 0), stop=(k0 + kk == Kc - 1))
                return ins
            P.op("pe", fn, reads=[sb] + rbufs, writes=[bankB[b] for b in banks[:nch]])

    def proj_tm(Wb, wbuf, Kc, c0, cw, lhs_fn, rbufs, banks, ntb=4, mrows=128):
        for k0 in range(0, Kc, KCP):
            nk = min(KCP, Kc - k0)
            slot, sb = load_piece(Wb, wbuf, k0, nk, c0, cw)

            def fn(e, slot=slot, k0=k0, nk=nk):
                ins = None
                for kk in range(nk):
                    for tb in range(ntb):
                        ins = e.matmul(psum[0:mrows, banks[tb], 0:cw], lhsT=lhs_fn(k0 + kk, tb), rhs=slot[:, kk, 0:cw],
                                       start=(k0 + kk == 0), stop=(k0 + kk == Kc - 1))
                return ins
            P.op("pe", fn, reads=[sb] + rbufs, writes=[bankB[b] for b in banks[:ntb]])

    def psbf(bank):
        return psum[:, bank, :].bitcast(BF16)

    tp_i = [0]

    def transpose_blocks(srcs, src_bufs, dst_ap, dst_buf):
        bank = 6 + (tp_i[0] % 2)
        tp_i[0] += 1
        n = len(srcs)
        pv = v3(psbf(bank)[:, 0:n * 128], n)

        def fn(e):
            ins = None
            for j, sap in enumerate(srcs):
                ins = e.transpose(out=pv[:, j, :], in_=sap, identity=ident)
            return ins
        P.op("pe", fn, reads=src_bufs + [constB], writes=[bankB[bank]])
        return pv, bank

    XH = D if D <= 2048 else D // 2
    NXH = D // XH

    def norm_T(xsrc_rows, gname, hT, hTB, S):
        xst, hn, ss, xstB, hnB = S["xst"], S["hn"], S["ss"], S["xstB"], S["hnB"]
        g_o, _ = ppoff[gname]
        for tb in range(4):
            src = xsrc_rows(tb)
            for hh in range(NXH):
                P.dma("sp", xst, src[:, hh * XH:(hh + 1) * XH], writes=[xstB])
                P.op("act", lambda e, hh=hh: e.activation(out=hn[:, hh * XH:(hh + 1) * XH], in_=xst, func=AF.Square, accum_out=ss[:, 4 + hh:5 + hh]),
                     reads=[xstB], writes=[hnB, S["ssB"]])
            if NXH == 1:
                P.op("dve", lambda e: e.tensor_scalar(out=ss[:, 1:2], in0=ss[:, 4:5], scalar1=1.0 / D, scalar2=EPS, op0=ALU.mult, op1=ALU.add),
                     reads=[S["ssB"]], writes=[S["ssB"]])
            else:
                P.op("dve", lambda e: e.tensor_tensor(out=ss[:, 0:1], in0=ss[:, 4:5], in1=ss[:, 5:6], op=ALU.add), reads=[S["ssB"]], writes=[S["ssB"]])
                P.op("dve", lambda e: e.tensor_scalar(out=ss[:, 1:2], in0=ss[:, 0:1], scalar1=1.0 / D, scalar2=EPS, op0=ALU.mult, op1=ALU.add),
                     reads=[S["ssB"]], writes=[S["ssB"]])
            P.op("pool", lambda e: e.tensor_tensor(out=ss[:, 2:3], in0=ss[:, 1:2], in1=cmhalf[:, 0:1], op=ALU.pow),
                 reads=[S["ssB"], constB], writes=[S["ssB"]])
            for hh in range(NXH):
                if NXH > 1:
                    P.dma("sp", xst, src[:, hh * XH:(hh + 1) * XH], writes=[xstB])
                P.op("dve", lambda e, hh=hh: e.tensor_scalar(out=hn[:, hh * XH:(hh + 1) * XH], in0=xst, scalar1=ss[:, 2:3], scalar2=None, op0=ALU.mult),
                     reads=[xstB, S["ssB"]], writes=[hnB])
            for k0 in range(0, KC, 4):
                n = min(4, KC - k0)
                pv, bank = transpose_blocks([hn[:, (k0 + j) * 128:(k0 + j + 1) * 128] for j in range(n)], [hnB], None, None)
                gb = pp_sb[:, g_o + k0:g_o + k0 + n].unsqueeze(2).to_broadcast([128, n, 128])
                P.op("dve", lambda e, pv=pv, gb=gb, k0=k0, n=n, tb=tb: e.tensor_tensor(
                    out=hT[:, k0:k0 + n, tb * 128:(tb + 1) * 128], in0=pv, in1=gb, op=ALU.mult),
                    reads=[bankB[bank], constB], writes=[hTB])

    def hn_junk(S):
        return S["junk"]

    def rope_tables(pos_ap, S):
        rt, rtB = S["rt"], S["rtB"]
        pos8 = rt[:, 0:8]
        ang = v3(rt[:, 16:16 + 256], 4)
        angf = rt[:, 16:16 + 256]
        sargf = rt[:, 272:272 + 256]
        qi = rt[:, 528:528 + 256].bitcast(mybir.dt.int32)
        sin_t = v3(rt[:, 1040:1040 + 256], 4)
        cos_t = v3(rt[:, 1296:1296 + 256], 4)
        qf = rt[:, 784:784 + 256]
        P.dma("sp", pos8, pos_ap, writes=[rtB])
        for b in range(4):
            P.op("dve", lambda e, b=b: e.tensor_scalar(out=ang[:, b, 0:32], in0=inv_sb, scalar1=pos8[:, 2 * b:2 * b + 1], scalar2=None, op0=ALU.mult),
                 reads=[rtB, constB], writes=[rtB])
            P.op("dve", lambda e, b=b: e.tensor_scalar(out=ang[:, b, 32:64], in0=inv_sb, scalar1=pos8[:, 2 * b + 1:2 * b + 2], scalar2=None, op0=ALU.mult),
                 reads=[rtB, constB], writes=[rtB])
        for shift, dst in ((0.0, rt[:, 1040:1040 + 256]), (0.5 * math.pi, rt[:, 1296:1296 + 256])):
            P.op("dve", lambda e, shift=shift: e.tensor_scalar_add(out=sargf, in0=angf, scalar1=shift), reads=[rtB], writes=[rtB])
            P.op("dve", lambda e: e.tensor_scalar_mul(out=qf, in0=sargf, scalar1=1.0 / (2 * math.pi)), reads=[rtB], writes=[rtB])
            P.op("dve", lambda e: e.tensor_copy(out=qi, in_=qf), reads=[rtB], writes=[rtB])
            P.op("dve", lambda e: e.tensor_copy(out=qf, in_=qi), reads=[rtB], writes=[rtB])
            P.op("dve", lambda e: e.scalar_tensor_tensor(out=sargf, in0=qf, scalar=-2 * math.pi, in1=sargf, op0=ALU.mult, op1=ALU.add), reads=[rtB], writes=[rtB])
            P.op("dve", lambda e: e.tensor_scalar(out=qf, in0=sargf, scalar1=math.pi, scalar2=2 * math.pi, op0=ALU.is_gt, op1=ALU.mult), reads=[rtB], writes=[rtB])
            P.op("dve", lambda e: e.tensor_tensor(out=sargf, in0=sargf, in1=qf, op=ALU.subtract), reads=[rtB], writes=[rtB])
            P.op("act", lambda e, dst=dst: e.activation(out=dst, in_=sargf, func=AF.Sin), reads=[rtB], writes=[rtB])
        return sin_t, cos_t

    def qk_norm_rope(bank, H, gname, sin_b, cos_b, S, out_bf):
        kq, sq, st, B = S["kq"], S["sq"], S["qst"], S["qkB"]
        n = H * 128
        kq3 = v3(kq[:, 0:n], H)
        sq3 = v3(sq[:, 0:n], H)
        g_o, _ = ppoff[gname]
        grow = pp_sb[:, g_o:g_o + 128].unsqueeze(1).to_broadcast([128, H, 128])
        P.op("act", lambda e: e.activation(out=kq[:, 0:n], in_=psum[:, bank, 0:n], func=AF.Identity), reads=[bankB[bank]], writes=[B])
        P.op("act", lambda e: e.activation(out=sq[:, 0:n], in_=psum[:, bank, 0:n], func=AF.Square), reads=[bankB[bank]], writes=[B])
        P.op("dve", lambda e: e.tensor_reduce(out=st[:, 0:H], in_=sq3, axis=AX.X, op=ALU.add), reads=[B], writes=[B])
        P.op("dve", lambda e: e.tensor_scalar(out=st[:, 0:H], in0=st[:, 0:H], scalar1=1.0 / 128, scalar2=EPS, op0=ALU.mult, op1=ALU.add), reads=[B], writes=[B])
        P.op("pool", lambda e: e.tensor_tensor(out=st[:, 0:H], in0=st[:, 0:H], in1=cmhalf[:, 0:H], op=ALU.pow), reads=[B, constB], writes=[B])
        P.op("dve", lambda e: e.tensor_tensor(out=kq3, in0=kq3, in1=st[:, 0:H].unsqueeze(2).to_broadcast([128, H, 128]), op=ALU.mult), reads=[B], writes=[B])
        P.op("dve", lambda e: e.tensor_tensor(out=kq3, in0=kq3, in1=grow, op=ALU.mult), reads=[B, constB], writes=[B])
        x1 = kq3[:, :, 0::2]
        x2 = kq3[:, :, 1::2]
        cb = cos_b.unsqueeze(1).to_broadcast([128, H, 64])
        sb = sin_b.unsqueeze(1).to_broadcast([128, H, 64])
        t1 = v3(sq[:, 0:H * 64], H)
        t2 = v3(sq[:, H * 64:H * 128], H)
        o3 = v3(out_bf[:, 0:n], H)
        oB = S["qoB"]
        P.op("dve", lambda e: e.tensor_tensor(out=t1, in0=x1, in1=cb, op=ALU.mult), reads=[B, S["rtB"]], writes=[B])
        P.op("dve", lambda e: e.tensor_tensor(out=t2, in0=x2, in1=sb, op=ALU.mult), reads=[B, S["rtB"]], writes=[B])
        P.op("dve", lambda e: e.tensor_tensor(out=o3[:, :, 0::2], in0=t1, in1=t2, op=ALU.subtract), reads=[B], writes=[oB])
        P.op("dve", lambda e: e.tensor_tensor(out=t1, in0=x1, in1=sb, op=ALU.mult), reads=[B, S["rtB"]], writes=[B])
        P.op("dve", lambda e: e.tensor_tensor(out=t2, in0=x2, in1=cb, op=ALU.mult), reads=[B, S["rtB"]], writes=[B])
        P.op("dve", lambda e: e.tensor_tensor(out=o3[:, :, 1::2], in0=t1, in1=t2, op=ALU.add), reads=[B], writes=[oB])

    LNAMES = ("xc", "tr", "ti", "a", "th", "a2", "m2", "hs", "hf", "a_1", "th_1")

    def mk_L(al):
        S_L = {}
        for nm in LNAMES:
            S_L[nm] = (al.f32(T), Buf())
        S_L["iu"] = S_L["a2"]
        S_L["u"] = S_L["th"]
        S_L["u_1"] = S_L["th_1"]
        S_L["xcb"] = (al.bf16(T), Buf())
        return S_L

    dbgXE = [None]

    dbgnames = []

    def lru_block(xe, XEBuf, blk, L, lw4, lwB, mode, extra):
        xcf, xcB = L["xc"]
        tr, trB = L["tr"]
        ti, tiB = L["ti"]
        a, aB = L["a"]
        th, thB = L["th"]
        a2, a2B = L["a2"]
        m2, m2B = L["m2"]
        iu, iuB = L["iu"]
        u, uB = L["u"]
        hs, hsB = L["hs"]
        hf, hfB = L["hf"]
        xcb, xcbB = L["xcb"]

        def _dd(i):
            if mode == 'own' and dbgXE[0] is not None and extra[0] == 0 and blk == 0:
                dump('Y%d_%d' % (i, len(dbgnames)), dbgXE[0], [XEBuf] + [L[n][1] for n in L])
                dbgnames.append(i)
        cw_o = ppoff["convw"][0]
        cb_o = ppoff["convb"][0]
        P.op("dve", lambda e: e.tensor_scalar(out=xcf, in0=xe[:, 0:T], scalar1=pp_sb[:, cw_o + blk * 4:cw_o + blk * 4 + 1],
                                              scalar2=pp_sb[:, cb_o + blk:cb_o + blk + 1], op0=ALU.mult, op1=ALU.add),
             reads=[XEBuf, constB], writes=[xcB])
        _dd(0)
        for j in range(1, 4):
            P.op("dve", lambda e, j=j: e.scalar_tensor_tensor(out=xcf, in0=xe[:, j:j + T], scalar=pp_sb[:, cw_o + blk * 4 + j:cw_o + blk * 4 + j + 1],
                                                          in1=xcf, op0=ALU.mult, op1=ALU.add),
                 reads=[XEBuf, constB, xcB], writes=[xcB])
            _dd(1)
        P.op("pool", lambda e: e.tensor_copy(out=xcb, in_=xcf), reads=[xcB], writes=[xcbB])
        _dd(2)
        def _zbody(z):
            zi = z * NB + blk
            banks = (4, 5)
            a, aB = L["a"] if z == 0 else L["a_1"]
            th, thB = L["th"] if z == 0 else L["th_1"]
            u, uB = th, thB

            def fn(e, z=z):
                e.matmul(psum[:, banks[0], 0:T], lhsT=lw4[:, 0, z, blk, :], rhs=xcb, start=True, stop=True)
                return e.matmul(psum[:, banks[1], 0:T], lhsT=lw4[:, 1, z, blk, :], rhs=xcb, start=True, stop=True)
            P.op("pe", fn, reads=[xcbB, lwB], writes=[bankB[banks[0]], bankB[banks[1]]])
            _dd(3)
            P.op("act", lambda e, zi=zi: e.activation(out=tr, in_=psum[:, banks[0], 0:T], func=AF.Tanh, scale=0.5, bias=hba[:, zi:zi + 1]),
                 reads=[bankB[banks[0]], constB], writes=[trB])
            _dd(4)
            P.op("act", lambda e, zi=zi: e.activation(out=ti, in_=psum[:, banks[1], 0:T], func=AF.Tanh, scale=0.5, bias=hbi[:, zi:zi + 1]),
                 reads=[bankB[banks[1]], constB], writes=[tiB])
            _dd(5)
            P.op("act", lambda e, zi=zi: e.activation(out=a, in_=tr, func=AF.Exp, scale=hc[:, zi:zi + 1], bias=hc[:, zi:zi + 1]),
                 reads=[trB, constB], writes=[aB])
            _dd(6)
            P.op("act", lambda e, zi=zi: e.activation(out=th, in_=tr, func=AF.Tanh, scale=nhc[:, zi:zi + 1], bias=nhc[:, zi:zi + 1]),
                 reads=[trB, constB], writes=[thB])
            _dd(7)
            P.op("act", lambda e: e.activation(out=a2, in_=a, func=AF.Square), reads=[aB], writes=[a2B])
            _dd(8)
            P.op("dve", lambda e: e.scalar_tensor_tensor(out=m2, in0=a2, scalar=1.0, in1=th, op0=ALU.add, op1=ALU.mult),
                 reads=[a2B, thB], writes=[m2B])
            _dd(9)
            P.op("dve", lambda e: e.tensor_scalar_max(out=m2, in0=m2, scalar1=0.0), reads=[m2B], writes=[m2B])
            _dd(10)
            P.op("pool", lambda e: e.tensor_tensor(out=m2, in0=m2, in1=chalf, op=ALU.pow), reads=[m2B, constB], writes=[m2B])
            _dd(11)
            P.op("dve", lambda e: e.scalar_tensor_tensor(out=iu, in0=ti, scalar=1.0, in1=xcf, op0=ALU.add, op1=ALU.mult),
                 reads=[tiB, xcB], writes=[iuB])
            _dd(12)
            P.op("dve", lambda e: e.scalar_tensor_tensor(out=u, in0=iu, scalar=0.5, in1=m2, op0=ALU.mult, op1=ALU.mult), reads=[iuB, m2B], writes=[uB])
            _dd(13)
            if mode == "ctx":
                sA4, sE4, t = extra
                if z == 0:
                    P.op("dve", lambda e: e.tensor_tensor_scan(out=hs, data0=a, data1=u, initial=0.0, op0=ALU.mult, op1=ALU.add),
                         reads=[aB, uB], writes=[hsB])
                    P.op("pool", lambda e: e.tensor_copy(out=sE4[:, 0, blk, t:t + 1], in_=hs[:, T - 1:T]), reads=[hsB], writes=[sumB])
                else:
                    P.op("dve", lambda e: e.tensor_tensor_scan(out=hs[:, ::-1], data0=a[:, ::-1], data1=u[:, ::-1], initial=0.0, op0=ALU.mult, op1=ALU.add),
                         reads=[aB, uB], writes=[hsB])
                    P.op("pool", lambda e: e.tensor_copy(out=sE4[:, 1, blk, t:t + 1], in_=hs[:, 0:1]), reads=[hsB], writes=[sumB])
                P.op("dve", lambda e, z=z: e.tensor_reduce(out=sA4[:, z, blk, t:t + 1], in_=a, axis=AX.X, op=ALU.mult),
                     reads=[aB], writes=[sumB])
            else:
                k, gy, gyB, rec, recB = extra
                cs4 = csel.rearrange("p (z n k) -> p z n k", z=2, n=NB)
                dd = (dbgXE[0] is not None and k == 0 and blk == 0)
                if z == 0:
                    if dd:
                        dump('X0', dbgXE[0], [XEBuf, uB, aB])
                    P.op("dve", lambda e: e.tensor_tensor_scan(out=hf, data0=a, data1=u, initial=cs4[:, 0, blk, k:k + 1], op0=ALU.mult, op1=ALU.add),
                         reads=[aB, uB, cselB], writes=[hfB])
                    if dd:
                        dump('X1', dbgXE[0], [XEBuf, hfB])
                else:
                    P.op("dve", lambda e: e.tensor_tensor_scan(out=hs[:, ::-1], data0=a[:, ::-1], data1=u[:, ::-1], initial=cs4[:, 1, blk, k:k + 1],
                                                               op0=ALU.mult, op1=ALU.add),
                         reads=[aB, uB, cselB], writes=[hsB])
                    if dd:
                        dump('X2', dbgXE[0], [XEBuf, hsB])
                    P.op("pool", lambda e: e.tensor_tensor(out=hs, in0=hs, in1=hf, op=ALU.add), reads=[hsB, hfB], writes=[hsB])
                    if dd:
                        dump('X3', dbgXE[0], [XEBuf, hsB])
                    P.op("dve", lambda e: e.tensor_tensor(out=rec[:, blk, :], in0=hs, in1=gy[:, blk, :], op=ALU.mult),
                         reads=[hsB, gyB], writes=[recB])
                    if dd:
                        dump('X4', dbgXE[0], [XEBuf, recB])

        for z in range(2):
            _zbody(z)

    def mk_S(al):
        S = {}
        S["xst"] = al.f32(XH); S["xstB"] = Buf()
        S["hn"] = al.bf16(D); S["hnB"] = Buf()
        S["junk"] = S["hn"]; S["junkB"] = S["hnB"]
        S["ss"] = al.f32(8); S["ssB"] = Buf()
        S["rt"] = al.f32(1552); S["rtB"] = Buf()
        S["kq"] = al.f32(512); S["sq"] = al.f32(512); S["qst"] = al.f32(16); S["qkB"] = Buf(); S["qoB"] = Buf()
        return S

    o_k, o_v, o_xr, o_yr, o_g = QW, QW + KW, QW + 2 * KW, QW + 2 * KW + DR, QW + 2 * KW + 2 * DR
    NTc = [nPc, nSc]
    scrB = [Buf(), Buf()]

    Actx = Alloc(BASE)
    hTc = v3(Actx.bf16(KC * T), KC); hTcB = Buf()
    Sc = mk_S(Actx)
    krb = Actx.bf16(512)
    KT_sb = v3(Actx.bf16(NKV * T), NKV); KT_B = Buf()
    v_sb = Actx.bf16(4 * KW); v_B = Buf()
    XE = [v3(Actx.f32(NB * (T + 4)), NB) for _ in range(2)]
    XEB = [Buf(), Buf()]
    Lc = mk_L(Actx)
    lw4c, lwBc = load_lw(Actx)
    sumA = [Actx.f32(2 * NB * NTc[s]).rearrange("p (z n t) -> p z n t", z=2, n=NB) for s in range(2)]
    sumE = [Actx.f32(2 * NB * NTc[s]).rearrange("p (z n t) -> p z n t", z=2, n=NB) for s in range(2)]
    carC = [Actx.f32(2 * NB * NTc[s]).rearrange("p (z n t) -> p z n t", z=2, n=NB) for s in range(2)]
    tmpc = Actx.f32(2 * NB * max(NTc))

    def ctx_lru(seq, t, last):
        xe = XE[t % 2]
        if t == 0:
            P.op("pool", lambda e: e.memset(xe[:, :, 0:1], 0.0), writes=[XEB[t % 2]])
        if last:
            P.op("pool", lambda e: e.memset(xe[:, :, T + 1:T + 3], 0.0), writes=[XEB[t % 2]])
        for blk in range(NB):
            lru_block(xe[:, blk, :], XEB[t % 2], blk, Lc, lw4c, lwBc, "ctx", (sumA[seq], sumE[seq], t))

    def ctx_tile(seq, t):
        S = Sc
        hT, hTB = hTc, hTcB
        norm_T(lambda tb: xc[seq][t * T + tb * 128:t * T + (tb + 1) * 128, :], "gmix", hT, hTB, S)
        sin_t, cos_t = rope_tables(posc[seq][:, t * 8:(t + 1) * 8], S)
        proj_tm(wb_in, wB["in"], KC, o_k, KW, lambda kc, tb: hT[:, kc, tb * 128:(tb + 1) * 128], [hTB], [0, 1, 2, 3])
        for tb in range(4):
            qk_norm_rope(tb, NKV, "kn", sin_t[:, tb, :], cos_t[:, tb, :], S, krb)
            pv, bank = transpose_blocks([krb[:, g * 128:(g + 1) * 128] for g in range(NKV)], [S["qoB"]], None, None)
            P.op("act", lambda e, pv=pv, tb=tb: e.activation(out=KT_sb[:, :, tb * 128:(tb + 1) * 128], in_=pv, func=AF.Identity),
                 reads=[bankB[bank]], writes=[KT_B])
        for g in range(NKV):
            P.dma("sp", KTs[seq][g][:, t * T:(t + 1) * T], KT_sb[:, g, :], reads=[KT_B], writes=[scrB[seq]])
        proj_tm(wb_in, wB["in"], KC, o_v, KW, lambda kc, tb: hT[:, kc, tb * 128:(tb + 1) * 128], [hTB], [0, 1, 2, 3])
        v3d = v3(v_sb, 4)
        for tb in range(4):
            P.op("act", lambda e, tb=tb: e.activation(out=v3d[:, tb, :], in_=psum[:, tb, 0:KW], func=AF.Identity), reads=[bankB[tb]], writes=[v_B])
        for g in range(NKV):
            P.dma("sp", Vs[seq][g][:, t * 4:(t + 1) * 4, :], v3d[:, :, g * 128:(g + 1) * 128], reads=[v_B], writes=[scrB[seq]])
        xe = XE[t % 2]
        for c0 in range(0, DR, 512):
            cw = min(512, DR - c0)
            proj_fm(wb_in, wB["in"], KC, o_xr + c0, cw, lambda kc: hT[:, kc, :], [hTB], [0, 1, 2, 3])
            for cc in range(cw // 128):
                P.op("act", lambda e, cc=cc, c0=c0: e.activation(out=xe[:, c0 // 128 + cc, 1:T + 1], in_=psum[:, cc, 0:T], func=AF.Identity),
                     reads=[bankB[cc]], writes=[XEB[t % 2]])
        if t > 0:
            xp = XE[(t - 1) % 2]
            P.op("pool", lambda e: e.tensor_copy(out=xp[:, :, T + 1:T + 3], in_=xe[:, :, 1:3]), reads=[XEB[t % 2]], writes=[XEB[(t - 1) % 2]])
            P.op("pool", lambda e: e.tensor_copy(out=xe[:, :, 0:1], in_=xp[:, :, T:T + 1]), reads=[XEB[(t - 1) % 2]], writes=[XEB[t % 2]])
            ctx_lru(seq, t - 1, False)

    for seq in range(2):
        for t in range(NTc[seq]):
            ctx_tile(seq, t)
            if cfg.get('stop') == 'ctx1':
                return finish()
        ctx_lru(seq, NTc[seq] - 1, True)
    if cfg.get('stop') == 'ctx':
        return finish()

    def carries(seq):
        nT = NTc[seq]
        A4, E4, C4 = sumA[seq], sumE[seq], carC[seq]
        P.op("dve", lambda e: e.memset(C4[:, 0, :, 0:1], 0.0), reads=[sumB], writes=[sumB])
        P.op("dve", lambda e: e.memset(C4[:, 1, :, nT - 1:nT], 0.0), reads=[sumB], writes=[sumB])
        for t in range(1, nT):
            P.op("dve", lambda e, t=t: e.tensor_tensor(out=C4[:, 0, :, t:t + 1], in0=A4[:, 0, :, t - 1:t], in1=C4[:, 0, :, t - 1:t], op=ALU.mult),
                 reads=[sumB], writes=[sumB])
            P.op("dve", lambda e, t=t: e.tensor_tensor(out=C4[:, 0, :, t:t + 1], in0=C4[:, 0, :, t:t + 1], in1=E4[:, 0, :, t - 1:t], op=ALU.add),
                 reads=[sumB], writes=[sumB])
        for t in range(nT - 2, -1, -1):
            P.op("dve", lambda e, t=t: e.tensor_tensor(out=C4[:, 1, :, t:t + 1], in0=A4[:, 1, :, t + 1:t + 2], in1=C4[:, 1, :, t + 1:t + 2], op=ALU.mult),
                 reads=[sumB], writes=[sumB])
            P.op("dve", lambda e, t=t: e.tensor_tensor(out=C4[:, 1, :, t:t + 1], in0=C4[:, 1, :, t:t + 1], in1=E4[:, 1, :, t + 1:t + 2], op=ALU.add),
                 reads=[sumB], writes=[sumB])

    def select_carry(k):
        seq = 0 if k < nPo else 1
        kk = k if k < nPo else k - nPo
        nT = NTc[seq]
        C3 = carC[seq].rearrange("p z n t -> p (z n) t")
        selk = sel_sb[seq][:, kk * nT:(kk + 1) * nT].unsqueeze(1).to_broadcast([128, 2 * NB, nT])
        t3 = tmpc[:, 0:2 * NB * nT].rearrange("p (zn t) -> p zn t", t=nT)
        P.op("dve", lambda e: e.tensor_tensor(out=t3, in0=C3, in1=selk, op=ALU.mult), reads=[sumB, constB], writes=[sumB])
        P.op("dve", lambda e: e.tensor_reduce(out=csel.rearrange("p (zn k) -> p zn k", k=nOwn)[:, :, k], in_=t3, axis=AX.X, op=ALU.add),
             reads=[sumB], writes=[cselB])

    for seq in range(2):
        carries(seq)
    for k in range(nOwn):
        select_carry(k)
    P.barrier()

    def own_tile(k):
        seq = 0 if k < nPo else 1
        Sq = Sseq[seq]
        nkb = Sq // 128
        Aa = Alloc(BASE)
        hT = v3(Aa.bf16(KC * T), KC); hTB = Buf()
        QT = v3(Aa.bf16(NQ * T), NQ); QTB = Buf()
        rec = v3(Aa.bf16(NB * T), NB); recB = Buf()
        KEEP = Aa.o
        S = mk_S(Aa)
        qrb = Aa.bf16(512)
        if cfg.get('pad'):
            Aa.f32(cfg['pad'])
        XEo = v3(Aa.f32(NB * (T + 4)), NB); XEoB = Buf()
        gy = v3(Aa.bf16(NB * T), NB); gyB = Buf()
        L = mk_L(Aa)
        lw4, lwB = load_lw(Aa)
        hhn = S["hn"]; hhB = S["hnB"]
        hhT = v3(Aa.bf16(KC * 4), KC); hhTB = Buf()
        xhs = S["xst"]; xhsB = S["xstB"]

        norm_T(lambda tb: xo[k * T + tb * 128:k * T + (tb + 1) * 128, :], "gmix", hT, hTB, S)
        sin_t, cos_t = rope_tables(poso[:, k * 8:(k + 1) * 8], S)
        xst4 = S["xst"][0:4, :]
        ss = S["ss"]
        for hh in range(NXH):
            P.dma("sp", xst4, xh[k * 4:(k + 1) * 4, hh * XH:(hh + 1) * XH], writes=[S["xstB"]])
            P.op("act", lambda e, hh=hh: e.activation(out=hhn[0:4, hh * XH:(hh + 1) * XH], in_=xst4, func=AF.Square, accum_out=ss[0:4, 4 + hh:5 + hh]),
                 reads=[S["xstB"]], writes=[hhB, S["ssB"]])
        if NXH == 1:
            P.op("dve", lambda e: e.tensor_scalar(out=ss[0:4, 1:2], in0=ss[0:4, 4:5], scalar1=1.0 / D, scalar2=EPS, op0=ALU.mult, op1=ALU.add), reads=[S["ssB"]], writes=[S["ssB"]])
        else:
            P.op("dve", lambda e: e.tensor_tensor(out=ss[0:4, 0:1], in0=ss[0:4, 4:5], in1=ss[0:4, 5:6], op=ALU.add), reads=[S["ssB"]], writes=[S["ssB"]])
            P.op("dve", lambda e: e.tensor_scalar(out=ss[0:4, 1:2], in0=ss[0:4, 0:1], scalar1=1.0 / D, scalar2=EPS, op0=ALU.mult, op1=ALU.add), reads=[S["ssB"]], writes=[S["ssB"]])
        P.op("pool", lambda e: e.tensor_tensor(out=ss[0:4, 2:3], in0=ss[0:4, 1:2], in1=cmhalf[0:4, 0:1], op=ALU.pow), reads=[S["ssB"], constB], writes=[S["ssB"]])
        for hh in range(NXH):
            if NXH > 1:
                P.dma("sp", xst4, xh[k * 4:(k + 1) * 4, hh * XH:(hh + 1) * XH], writes=[S["xstB"]])
            P.op("dve", lambda e, hh=hh: e.tensor_scalar(out=hhn[0:4, hh * XH:(hh + 1) * XH], in0=xst4, scalar1=ss[0:4, 2:3], scalar2=None, op0=ALU.mult),
                 reads=[S["xstB"], S["ssB"]], writes=[hhB])
        g_o = ppoff["gmix"][0]
        for k0 in range(0, KC, 4):
            n = min(4, KC - k0)
            bank = 6 + (tp_i[0] % 2)
            tp_i[0] += 1
            pv = v3(psbf(bank)[:, 0:n * 4], n)

            def fn(e, pv=pv, k0=k0, n=n):
                ins = None
                for j in range(n):
                    ins = e.transpose(out=pv[:, j, :], in_=hhn[0:4, (k0 + j) * 128:(k0 + j + 1) * 128], identity=ident[0:4, 0:4])
                return ins
            P.op("pe", fn, reads=[hhB, constB], writes=[bankB[bank]])
            gb = pp_sb[:, g_o + k0:g_o + k0 + n].unsqueeze(2).to_broadcast([128, n, 4])
            P.op("dve", lambda e, pv=pv, gb=gb, k0=k0, n=n: e.tensor_tensor(out=hhT[:, k0:k0 + n, :], in0=pv, in1=gb, op=ALU.mult),
                 reads=[bankB[bank], constB], writes=[hhTB])
        for c0 in range(0, QW, 512):
            cw = min(512, QW - c0)
            H = cw // 128
            proj_tm(wb_in, wB["in"], KC, c0, cw, lambda kc, tb: hT[:, kc, tb * 128:(tb + 1) * 128], [hTB], [0, 1, 2, 3])
            for tb in range(4):
                qk_norm_rope(tb, H, "qn", sin_t[:, tb, :], cos_t[:, tb, :], S, qrb)
                pv, bank = transpose_blocks([qrb[:, h * 128:(h + 1) * 128] for h in range(H)], [S["qoB"]], None, None)
                P.op("act", lambda e, pv=pv, tb=tb, c0=c0, H=H: e.activation(out=QT[:, c0 // 128:c0 // 128 + H, tb * 128:(tb + 1) * 128], in_=pv, func=AF.Identity),
                     reads=[bankB[bank]], writes=[QTB])
        for c0 in range(0, DR, 512):
            cw = min(512, DR - c0)
            proj_fm(wb_in, wB["in"], KC, o_xr + c0, cw, lambda kc: hT[:, kc, :], [hTB], [0, 1, 2, 3])
            for cc in range(cw // 128):
                P.op("act", lambda e, cc=cc, c0=c0: e.activation(out=XEo[:, c0 // 128 + cc, 1:T + 1], in_=psum[:, cc, 0:T], func=AF.Identity),
                     reads=[bankB[cc]], writes=[XEoB])
            proj_tm(wb_in, wB["in"], KC, o_xr + c0, cw, lambda kc, tb: hhT[:, kc, :], [hhTB], [4], ntb=1, mrows=4)
            P.op("act", lambda e, c0=c0, cw=cw: e.activation(out=xhs[0:4, c0:c0 + cw], in_=psum[0:4, 4, 0:cw], func=AF.Identity), reads=[bankB[4]], writes=[xhsB])
        for b0 in range(0, NB, 4):
            n = min(4, NB - b0)
            bank = 6 + (tp_i[0] % 2)
            tp_i[0] += 1
            pvf = v3(psum[:, bank, 0:n * 4], n)

            def fn(e, pvf=pvf, b0=b0, n=n):
                ins = None
                for j in range(n):
                    ins = e.transpose(out=pvf[:, j, :], in_=xhs[0:4, (b0 + j) * 128:(b0 + j + 1) * 128], identity=identf[0:4, 0:4])
                return ins
            P.op("pe", fn, reads=[xhsB, constB], writes=[bankB[bank]])
            P.op("act", lambda e, pvf=pvf, b0=b0, n=n: e.activation(out=XEo[:, b0:b0 + n, 0:1], in_=pvf[:, :, 0:1], func=AF.Identity), reads=[bankB[bank]], writes=[XEoB])
            P.op("act", lambda e, pvf=pvf, b0=b0, n=n: e.activation(out=XEo[:, b0:b0 + n, T + 1:T + 3], in_=pvf[:, :, 1:3], func=AF.Identity), reads=[bankB[bank]], writes=[XEoB])
        if k == 0:
            dump('XE', XEo, [XEoB])
        g_yv, g_yvB = L["xc"]
        g_y2, g_y2B = L["tr"]
        g_wv, g_wvB = L["ti"]
        g_tv, g_tvB = L["a"]
        for c0 in range(0, DR, 512):
            cw = min(512, DR - c0)
            proj_fm(wb_in, wB["in"], KC, o_yr + c0, cw, lambda kc: hT[:, kc, :], [hTB], [0, 1, 2, 3])
            for cc in range(cw // 128):
                blk = c0 // 128 + cc
                P.op("act", lambda e, cc=cc: e.activation(out=g_yv, in_=psum[:, cc, 0:T], func=AF.Identity), reads=[bankB[cc]], writes=[g_yvB])
                if k == 0 and blk == NB - 1:
                    dump('G0', XEo, [XEoB, g_yvB, g_y2B, g_wvB, g_tvB, gyB])
                P.op("dve", lambda e: e.tensor_tensor(out=g_y2, in0=g_yv, in1=g_yv, op=ALU.mult), reads=[g_yvB], writes=[g_y2B])
                if k == 0 and blk == NB - 1:
                    dump('G1', XEo, [XEoB, g_yvB, g_y2B, g_wvB, g_tvB, gyB])
                P.op("dve", lambda e: e.tensor_scalar(out=g_y2, in0=g_y2, scalar1=0.044715, scalar2=1.0, op0=ALU.mult, op1=ALU.add), reads=[g_y2B], writes=[g_y2B])
                if k == 0 and blk == NB - 1:
                    dump('G2', XEo, [XEoB, g_yvB, g_y2B, g_wvB, g_tvB, gyB])
                P.op("dve", lambda e: e.tensor_tensor(out=g_wv, in0=g_y2, in1=g_yv, op=ALU.mult), reads=[g_y2B, g_yvB], writes=[g_wvB])
                if k == 0 and blk == NB - 1:
                    dump('G3', XEo, [XEoB, g_yvB, g_y2B, g_wvB, g_tvB, gyB])
                P.op("act", lambda e: e.activation(out=g_tv, in_=g_wv, func=AF.Tanh, scale=math.sqrt(2.0 / math.pi)), reads=[g_wvB], writes=[g_tvB])
                if k == 0 and blk == NB - 1:
                    dump('G4', XEo, [XEoB, g_yvB, g_y2B, g_wvB, g_tvB, gyB])
                P.op("dve", lambda e: e.scalar_tensor_tensor(out=g_tv, in0=g_tv, scalar=1.0, in1=g_yv, op0=ALU.add, op1=ALU.mult), reads=[g_tvB, g_yvB], writes=[g_tvB])
                if k == 0 and blk == NB - 1:
                    dump('G5', XEo, [XEoB, g_yvB, g_y2B, g_wvB, g_tvB, gyB])
                P.op("dve", lambda e, blk=blk: e.tensor_scalar_mul(out=gy[:, blk, :], in0=g_tv, scalar1=0.5), reads=[g_tvB], writes=[gyB])
                if k == 0 and blk == NB - 1:
                    dump('G6', XEo, [XEoB, g_yvB, g_y2B, g_wvB, g_tvB, gyB])
        if k == 0:
            dump('XEb', XEo, [XEoB])
            dbgXE[0] = XEo
        for blk in range(NB):
            lru_block(XEo[:, blk, :], XEoB, blk, L, lw4, lwB, "own", (k, gy, gyB, rec, recB))
            if k == 0 and blk == 0:
                dump('XEc', XEo, [XEoB, L['hs'][1], recB])
        if k == 0:
            dump('XEd', XEo, [XEoB, recB])
        if k == 0:
            for nm in ('xc', 'tr', 'ti', 'a', 'm2', 'u', 'hs', 'hf'):
                dump('L_' + nm, L[nm][0], [L[nm][1]])
            dump('hT', hT, [hTB]); dump('QT', QT, [QTB]); dump('rec', rec, [recB]); dump('gy', gy, [gyB]); dump('csel', csel, [cselB])
            dump('KT', KTs[0], [scrB[0]]); dump('V', Vs[0], [scrB[0]])
        P.barrier()

        Ab = Alloc(KEEP)
        attn = v3(Ab.bf16(NQ * T), NQ); attnB = Buf()
        KEEP2 = Ab.o
        KTg = [Ab.bf16(Sq) for _ in range(2)]
        Vg = [Ab.bf16(Sq) for _ in range(2)]
        KVB = [Buf(), Buf()]
        PT = [Ab.bf16(T) for _ in range(3)]
        PTB = [Buf() for _ in range(3)]
        rcp = Ab.f32(T); rcpB = Buf()
        pi = 0
        for g in range(NKV):
            kt = KTg[g % 2]
            vg = v3(Vg[g % 2], nkb)
            P.dma("sp", kt, KTs[seq][g], reads=[scrB[seq]], writes=[KVB[g % 2]])
            P.dma("sp", vg, Vs[seq][g], reads=[scrB[seq]], writes=[KVB[g % 2]])
            for j in range(G):
                h = g * G + j
                for kb in range(nkb):
                    sbank = kb % 3
                    P.op("pe", lambda e, kt=kt, kb=kb, h=h, sbank=sbank: e.matmul(psum[:, sbank, 0:T], lhsT=kt[:, kb * 128:(kb + 1) * 128], rhs=QT[:, h, :], start=True, stop=True),
                         reads=[KVB[g % 2], QTB], writes=[bankB[sbank]])
                    pt = PT[pi % 3]
                    ptB = PTB[pi % 3]
                    pi += 1
                    P.op("act", lambda e, pt=pt, sbank=sbank: e.activation(out=pt, in_=psum[:, sbank, 0:T], func=AF.Exp, scale=1.0 / math.sqrt(128.0)),
                         reads=[bankB[sbank]], writes=[ptB])

                    def fn(e, vg=vg, kb=kb, pt=pt):
                        e.matmul(psum[:, 4, 0:T], lhsT=vg[:, kb, :], rhs=pt, start=(kb == 0), stop=(kb == nkb - 1))
                        return e.matmul(psum[:, 5, 0:T], lhsT=ones_bf, rhs=pt, start=(kb == 0), stop=(kb == nkb - 1))
                    P.op("pe", fn, reads=[KVB[g % 2], ptB, constB], writes=[bankB[4], bankB[5]])
                P.op("dve", lambda e: e.reciprocal(out=rcp, in_=psum[:, 5, 0:T]), reads=[bankB[5]], writes=[rcpB])
                P.op("dve", lambda e, h=h: e.tensor_tensor(out=attn[:, h, :], in0=psum[:, 4, 0:T], in1=rcp, op=ALU.mult), reads=[bankB[4], rcpB], writes=[attnB])
        if k == 0:
            dump('attn', attn, [attnB])
        P.barrier()

        Ac = Alloc(KEEP2)
        mg = v3(Ac.bf16(KC * T), KC); mgB = Buf()
        Asb = v3(Ac.f32(4 * T), 4); AsbB = Buf()
        Rsb = v3(Ac.f32(4 * T), 4); RsbB = Buf()
        tg = Ac.f32(T); tgB = Buf()
        xs = [Ac.f32(512) for _ in range(2)]; xsB = [Buf(), Buf()]
        for c0 in range(0, D, 512):
            cw = min(512, D - c0)
            nch = cw // 128
            proj_fm(wb_ao, wB["ao"], QW // 128, c0, cw, lambda kc: attn[:, kc, :], [attnB], [0, 1, 2, 3])
            for cc in range(nch):
                P.op("act", lambda e, cc=cc: e.activation(out=Asb[:, cc, :], in_=psum[:, cc, 0:T], func=AF.Identity), reads=[bankB[cc]], writes=[AsbB])
            proj_fm(wb_ro, wB["ro"], NB, c0, cw, lambda kc: rec[:, kc, :], [recB], [4, 5, 6, 7])
            for cc in range(nch):
                P.op("act", lambda e, cc=cc: e.activation(out=Rsb[:, cc, :], in_=psum[:, 4 + cc, 0:T], func=AF.Identity), reads=[bankB[4 + cc]], writes=[RsbB])
            for gi, (sbv, sbB) in enumerate(((Asb, AsbB), (Rsb, RsbB))):
                banks = [0, 1, 2, 3] if gi == 0 else [4, 5, 6, 7]
                proj_fm(wb_in, wB["in"], KC, o_g + gi * D + c0, cw, lambda kc: hT[:, kc, :], [hTB], banks)
                for cc in range(nch):
                    ch = c0 // 128 + cc
                    P.op("act", lambda e, cc=cc, ch=ch, gi=gi, banks=banks: e.activation(out=tg, in_=psum[:, banks[cc], 0:T], func=AF.Tanh, scale=0.5,
                                                                                      bias=hbg[:, gi * KC + ch:gi * KC + ch + 1]),
                         reads=[bankB[banks[cc]], constB], writes=[tgB])
                    P.op("dve", lambda e, cc=cc, sbv=sbv: e.scalar_tensor_tensor(out=sbv[:, cc, :], in0=tg, scalar=1.0, in1=sbv[:, cc, :], op0=ALU.add, op1=ALU.mult),
                         reads=[tgB, sbB], writes=[sbB])
            for cc in range(nch):
                ch = c0 // 128 + cc
                P.op("pool", lambda e, cc=cc, ch=ch: e.tensor_tensor(out=mg[:, ch, :], in0=Asb[:, cc, :], in1=Rsb[:, cc, :], op=ALU.add),
                     reads=[AsbB, RsbB], writes=[mgB])
        xi_c = [0]

        def evac_wout(ci, c0, cw, banks):
            for tb in range(4):
                xv = xs[xi_c[0] % 2]; xB = xsB[xi_c[0] % 2]; xi_c[0] += 1
                P.dma("sp", xv[:, 0:cw], xo[k * T + tb * 128:k * T + (tb + 1) * 128, c0:c0 + cw], writes=[xB])
                P.op("dve", lambda e, xv=xv, tb=tb, cw=cw, banks=banks: e.scalar_tensor_tensor(out=xv[:, 0:cw], in0=psum[:, banks[tb], 0:cw], scalar=0.5, in1=xv[:, 0:cw],
                                                                                     op0=ALU.mult, op1=ALU.add),
                     reads=[bankB[banks[tb]], xB], writes=[xB])
                P.op("act", lambda e, xv=xv, tb=tb, ci=ci, cw=cw: e.activation(out=tg[:, 0:cw], in_=xv[:, 0:cw], func=AF.Square, accum_out=ssqP3[:, tb, ci:ci + 1]),
                     reads=[xB], writes=[tgB, ssqB])
                P.dma("sp", x1s[tb * 128:(tb + 1) * 128, c0:c0 + cw], xv[:, 0:cw], reads=[xB], writes=[scrX])
        prev_c = None
        for ci, c0 in enumerate(range(0, D, 512)):
            cw = min(512, D - c0)
            banks = [0, 1, 2, 3] if ci % 2 == 0 else [4, 5, 6, 7]
            proj_tm(wb_out, wB["out"], KC, c0, cw, lambda kc, tb: mg[:, kc, tb * 128:(tb + 1) * 128], [mgB], banks)
            if prev_c is not None:
                evac_wout(*prev_c)
            prev_c = (ci, c0, cw, banks)
        evac_wout(*prev_c)
        if k == 0:
            dump('x1', x1s, [scrX]); dump('mg', mg, [mgB])
        P.barrier()

        Ad = Alloc(BASE)
        KH = max(4, KC // 2)
        hm_parts = [v3(Ad.bf16(KH * T), KH) for _ in range((KC + KH - 1) // KH)]
        hmTB = Buf()

        def hm_sl(k0, n):
            return hm_parts[k0 // KH][:, k0 % KH:k0 % KH + n]
        act = v3(Ad.bf16(FC * T), FC); actB = Buf()
        U0 = Ad.o
        hn = Ad.bf16(D); hnB = Buf()
        ysn = [Ad.f32(512) for _ in range(2)]; ysnB = [Buf(), Buf()]
        stt1 = Ad.f32(16); stt1B = Buf()
        g_o = ppoff["gmlp"][0]

        def rstd_from_ssq():
            for tb in range(4):
                P.op("dve", lambda e, tb=tb: e.tensor_reduce(out=stt1[:, tb:tb + 1], in_=ssqP3[:, tb, :], axis=AX.X, op=ALU.add), reads=[ssqB], writes=[stt1B])
            P.op("dve", lambda e: e.tensor_scalar(out=stt1[:, 0:4], in0=stt1[:, 0:4], scalar1=1.0 / D, scalar2=EPS, op0=ALU.mult, op1=ALU.add), reads=[stt1B], writes=[stt1B])
            P.op("pool", lambda e: e.tensor_tensor(out=stt1[:, 4:8], in0=stt1[:, 0:4], in1=cmhalf[:, 0:4], op=ALU.pow), reads=[stt1B, constB], writes=[stt1B])
        rstd_from_ssq()
        yi = 0
        for tb in range(4):
            for ci, c0 in enumerate(range(0, D, 512)):
                cw = min(512, D - c0)
                yv = ysn[yi % 2]; yB = ysnB[yi % 2]; yi += 1
                P.dma("sp", yv[:, 0:cw], x1s[tb * 128:(tb + 1) * 128, c0:c0 + cw], reads=[scrX], writes=[yB])
                P.op("dve", lambda e, yv=yv, tb=tb, c0=c0, cw=cw: e.tensor_scalar(out=hn[:, c0:c0 + cw], in0=yv[:, 0:cw], scalar1=stt1[:, 4 + tb:5 + tb], scalar2=None, op0=ALU.mult),
                     reads=[yB, stt1B], writes=[hnB])
            for k0 in range(0, KC, 4):
                n = min(4, KC - k0)
                pv, bank = transpose_blocks([hn[:, (k0 + j) * 128:(k0 + j + 1) * 128] for j in range(n)], [hnB], None, None)
                gb = pp_sb[:, g_o + k0:g_o + k0 + n].unsqueeze(2).to_broadcast([128, n, 128])
                P.op("dve", lambda e, pv=pv, gb=gb, k0=k0, n=n, tb=tb: e.tensor_tensor(out=hm_sl(k0, n)[:, :, tb * 128:(tb + 1) * 128], in0=pv, in1=gb, op=ALU.mult),
                     reads=[bankB[bank], constB], writes=[hmTB])
        P.barrier()
        Ad2 = Alloc(U0)
        rl = [Ad2.f32(T) for _ in range(2)]; rlB = [Buf(), Buf()]
        ys = [Ad2.f32(512) for _ in range(2)]; ysB = [Buf(), Buf()]
        gfs = Ad2.f32(512); gfB = Buf()
        stt = Ad2.f32(16); sttB = Buf()
        ri = 0
        for ci, c0 in enumerate(range(0, DFF, 512)):
            banks = [0, 1, 2, 3] if ci % 2 == 0 else [4, 5, 6, 7]
            proj_fm(wb_up, wB["up"], KC, c0, 512, lambda kc: hm_sl(kc, 1)[:, 0, :], [hmTB], banks)
            for cc in range(4):
                r = rl[ri % 2]; rB = rlB[ri % 2]; ri += 1
                P.op("act", lambda e, r=r, cc=cc, banks=banks: e.activation(out=r, in_=psum[:, banks[cc], 0:T], func=AF.Relu), reads=[bankB[banks[cc]]], writes=[rB])
                P.op("pool", lambda e, r=r, cc=cc, c0=c0: e.tensor_tensor(out=act[:, c0 // 128 + cc, :], in0=r, in1=r, op=ALU.mult), reads=[rB], writes=[actB])
        yi_c = [0]

        def evac_down(ci, c0, cw, banks):
            for tb in range(4):
                yv = ys[yi_c[0] % 2]; yB = ysB[yi_c[0] % 2]; yi_c[0] += 1
                P.dma("sp", yv[:, 0:cw], x1s[tb * 128:(tb + 1) * 128, c0:c0 + cw], reads=[scrX], writes=[yB])
                P.op("dve", lambda e, yv=yv, tb=tb, cw=cw, banks=banks: e.tensor_tensor(out=yv[:, 0:cw], in0=psum[:, banks[tb], 0:cw], in1=yv[:, 0:cw], op=ALU.add),
                     reads=[bankB[banks[tb]], yB], writes=[yB])
                P.op("act", lambda e, yv=yv, tb=tb, ci=ci, cw=cw: e.activation(out=rl[0][:, 0:cw], in_=yv[:, 0:cw], func=AF.Square, accum_out=ssqP3[:, tb, ci:ci + 1]),
                     reads=[yB], writes=[rlB[0], ssqB])
                P.dma("sp", x1s[tb * 128:(tb + 1) * 128, c0:c0 + cw], yv[:, 0:cw], reads=[yB], writes=[scrX])

        prev_d = None
        for ci, c0 in enumerate(range(0, D, 512)):
            cw = min(512, D - c0)
            banks = [0, 1, 2, 3] if ci % 2 == 0 else [4, 5, 6, 7]
            proj_tm(wb_dn, wB["dn"], FC, c0, cw, lambda kc, tb: act[:, kc, tb * 128:(tb + 1) * 128], [actB], banks)
            if prev_d is not None:
                evac_down(*prev_d)
            prev_d = (ci, c0, cw, banks)
        evac_down(*prev_d)

        def rstd2():
            for tb in range(4):
                P.op("dve", lambda e, tb=tb: e.tensor_reduce(out=stt[:, tb:tb + 1], in_=ssqP3[:, tb, :], axis=AX.X, op=ALU.add), reads=[ssqB], writes=[sttB])
            P.op("dve", lambda e: e.tensor_scalar(out=stt[:, 0:4], in0=stt[:, 0:4], scalar1=1.0 / D, scalar2=EPS, op0=ALU.mult, op1=ALU.add), reads=[sttB], writes=[sttB])
            P.op("pool", lambda e: e.tensor_tensor(out=stt[:, 4:8], in0=stt[:, 0:4], in1=cmhalf[:, 0:4], op=ALU.pow), reads=[sttB, constB], writes=[sttB])
        rstd2()
        for ci, c0 in enumerate(range(0, D, 512)):
            cw = min(512, D - c0)
            P.dma("sp", gfs[:, 0:cw], gfin[:, c0:c0 + cw], writes=[gfB])
            for tb in range(4):
                yv = ys[yi_c[0] % 2]; yB = ysB[yi_c[0] % 2]; yi_c[0] += 1
                P.dma("sp", yv[:, 0:cw], x1s[tb * 128:(tb + 1) * 128, c0:c0 + cw], reads=[scrX], writes=[yB])
                P.op("dve", lambda e, yv=yv, tb=tb, cw=cw: e.scalar_tensor_tensor(out=yv[:, 0:cw], in0=yv[:, 0:cw], scalar=stt[:, 4 + tb:5 + tb], in1=gfs[:, 0:cw],
                                                                               op0=ALU.mult, op1=ALU.mult),
                     reads=[yB, sttB, gfB], writes=[yB])
                P.dma("sp", yout[k * T + tb * 128:k * T + (tb + 1) * 128, c0:c0 + cw], yv[:, 0:cw], reads=[yB], writes=[outB])
        P.barrier()

    for k in range(nOwn):
        own_tile(k)
        if cfg.get('stop') == 'own1':
            return finish()

    return finish()


def host_inputs(cfg, inp):
    c = dims(cfg)
    D, SP, SS, KC, NB, DR = c["D"], c["SP"], c["SS"], c["KC"], c["NB"], c["DR"]
    nPo, nSo, nPc, nSc, nOwn = c["nPo"], c["nSo"], c["nPc"], c["nSc"], c["nOwn"]
    f = np.float32
    xp = np.asarray(inp["x_prompt"], f)
    xs = np.asarray(inp["x_sample"], f)
    pp = np.zeros((128, c["NPP"]), f)
    off = c["ppoff"]

    def put(name, arr):
        o, n = off[name]
        assert arr.shape == (128, n), (name, arr.shape, n)
        pp[:, o:o + n] = arr
    put("gmix", np.asarray(inp["norm_mix"], f)[0].reshape(KC, 128).T)
    put("gmlp", np.asarray(inp["norm_mlp"], f)[0].reshape(KC, 128).T)
    cw = np.asarray(inp["conv_w"], f)[0]
    put("convw", cw.reshape(4, NB, 128).transpose(2, 1, 0).reshape(128, NB * 4))
    put("convb", np.asarray(inp["conv_b"], f)[0].reshape(NB, 128).T)
    for nm, key in (("ba", "lru_b_a"), ("bi", "lru_b_i"), ("lam", "lru_lambda")):
        put(nm, np.asarray(inp[key], f)[0].reshape(2, NB, 128).transpose(2, 0, 1).reshape(128, 2 * NB))
    put("bg", np.asarray(inp["b_gate"], f)[0].reshape(2, KC, 128).transpose(2, 0, 1).reshape(128, 2 * KC))
    put("qn", np.broadcast_to(np.asarray(inp["q_norm"], f)[0][None, :], (128, 128)))
    put("kn", np.broadcast_to(np.asarray(inp["k_norm"], f)[0][None, :], (128, 128)))
    put("iota", np.broadcast_to(np.arange(32, dtype=f)[None, :], (128, 32)))
    def rowcol(pos):
        pos = np.asarray(pos, np.int64)
        return np.ascontiguousarray(np.stack([pos // 64, pos % 64], -1).reshape(128, -1).astype(f))
    shared = dict(
        w_in=np.ascontiguousarray(np.asarray(inp["w_in"], f)[0]), w_ao=np.ascontiguousarray(np.asarray(inp["w_attn_out"], f)[0]),
        w_ro=np.ascontiguousarray(np.asarray(inp["w_rnn_out"], f)[0]), w_out=np.ascontiguousarray(np.asarray(inp["w_out"], f)[0]),
        w_up=np.ascontiguousarray(np.asarray(inp["w_up"], f)[0]), w_dn=np.ascontiguousarray(np.asarray(inp["w_down"], f)[0]),
        lru_wa=np.ascontiguousarray(np.asarray(inp["lru_w_a"], f)[0]), lru_wi=np.ascontiguousarray(np.asarray(inp["lru_w_i"], f)[0]),
        pp=pp, gfin=np.ascontiguousarray(np.broadcast_to(np.asarray(inp["norm_final"], f)[None, :], (128, D))),
        ident=np.eye(128, dtype=f),
        pos_p=rowcol(np.arange(SP // 128)[None, :] * 128 + np.arange(128)[:, None]),
        pos_s=rowcol(np.arange(SS // 128)[None, :] * 128 + np.arange(128)[:, None]),
        xc_s=np.ascontiguousarray(xs[0]),
    )
    maps = []
    for core in range(8):
        sp, half = core // 2, core % 2
        p0 = half * (SP // 2)
        s0 = core * (SS // 8)
        xo = np.concatenate([xp[sp, p0:p0 + SP // 2], xs[0, s0:s0 + SS // 8]], 0)
        xh = np.zeros((nOwn * 4, D), f)
        pos_o = np.zeros((128, nOwn * 4), np.int64)
        sel_p = np.zeros((128, nPo * nPc), f)
        sel_s = np.zeros((128, nSo * nSc), f)
        for k in range(nOwn):
            if k < nPo:
                src, st, S_ = xp[sp], p0 + k * T, SP
                sel_p[:, k * nPc + st // T] = 1.0
            else:
                src, st, S_ = xs[0], s0 + (k - nPo) * T, SS
                sel_s[:, (k - nPo) * nSc + st // T] = 1.0
            for j, tpos in enumerate((st - 1, st + T, st + T + 1)):
                if 0 <= tpos < S_:
                    xh[k * 4 + j] = src[tpos]
            for tb in range(4):
                pos_o[:, k * 4 + tb] = st + tb * 128 + np.arange(128)
        m = dict(shared)
        m.update(xc_p=np.ascontiguousarray(xp[sp]), xo=np.ascontiguousarray(xo), xh=xh, pos_o=rowcol(pos_o), sel_p=sel_p, sel_s=sel_s)
        maps.append(m)
    return maps


def assemble(cfg, results, B=4):
    c = dims(cfg)
    D, SP, SS = c["D"], c["SP"], c["SS"]
    yp = np.zeros((B, SP, D), np.float32)
    ys = np.zeros((1, SS, D), np.float32)
    for core in range(8):
        y = np.asarray(results[core]["y"], np.float32)
        sp, half = core // 2, core % 2
        p0 = half * (SP // 2)
        s0 = core * (SS // 8)
        yp[sp, p0:p0 + SP // 2] = y[:SP // 2]
        ys[0, s0:s0 + SS // 8] = y[SP // 2:]
    return yp, ys


_NC_CACHE = {}


def run(cfg, inp, trace=False):
    key = tuple(sorted((k, str(v)) for k, v in cfg.items()))
    if key not in _NC_CACHE:
        _NC_CACHE[key] = build(cfg)
    nc = _NC_CACHE[key]
    maps = host_inputs(cfg, inp)
    res = run_bass_kernel_spmd(nc, maps, core_ids=list(range(8)), **({"trace": True} if trace else {}))
    return assemble(cfg, res.results), res


def kernel(**inputs):
    (yp, ys), _ = run(FULL, inputs)
    return yp, ys
```

```python
import math
import numpy as np
import concourse.bass as bass
import concourse.mybir as mybir
from concourse.bass_utils import run_bass_kernel_spmd

F32 = mybir.dt.float32
BF16 = mybir.dt.bfloat16
ALU = mybir.AluOpType
AF = mybir.ActivationFunctionType
AX = mybir.AxisListType
ENG = ("pe", "act", "dve", "pool", "sp")
T = 512
EPS = 1e-6

FULL = dict(D=4096, NQ=16, NKV=4, SP=4096, SS=8192)


class Buf:
    __slots__ = ("w", "r")

    def __init__(self):
        self.w = None
        self.r = {}


class Prog:
    def __init__(self, psem, dsems):
        self.ops = {e: [] for e in ENG}
        self.cnt = {e: 0 for e in ENG}
        self.psem = psem
        self.waited = {e: {} for e in ENG}
        self.dsems = dsems
        self.dval = [0] * len(dsems)
        self.dnext = 0

    def _deps(self, eng, reads, writes):
        need = {}

        def add(t):
            if t is None:
                return
            k = id(t[0])
            if k not in need or need[k][1] < t[1]:
                need[k] = t

        for b in reads:
            add(b.w)
        for b in writes:
            add(b.w)
            for t in b.r.values():
                add(t)
        waits = []
        for k, (sem, val) in need.items():
            if self.waited[eng].get(k, 0) < val:
                self.waited[eng][k] = val
                waits.append((sem, val))
        return waits

    def _mark(self, tok, reads, writes):
        for b in reads:
            b.r[id(tok[0])] = tok
        for b in writes:
            b.w = tok
            b.r = {}

    def op(self, eng, fn, reads=(), writes=()):
        waits = self._deps(eng, reads, writes)
        self.cnt[eng] += 1
        sem = self.psem[eng]
        tok = (sem, self.cnt[eng])

        def run(e, waits=waits, fn=fn, sem=sem):
            for ws, wv in waits:
                e.wait_ge(ws, wv)
            fn(e).then_inc(sem, 1)

        self.ops[eng].append(run)
        self._mark(tok, reads, writes)
        return tok

    def dma(self, q, out_ap, in_ap, reads=(), writes=()):
        k = self.dnext
        self.dnext = (self.dnext + 1) % len(self.dsems)
        sem = self.dsems[k]
        prev = self.dval[k]
        self.dval[k] += 16
        tok = (sem, self.dval[k])
        waits = self._deps(q, reads, writes)
        if prev > 0 and self.waited[q].get(id(sem), 0) < prev:
            self.waited[q][id(sem)] = prev
            waits.append((sem, prev))

        def run(e, waits=waits, sem=sem, out_ap=out_ap, in_ap=in_ap):
            for ws, wv in waits:
                e.wait_ge(ws, wv)
            e.dma_start(out=out_ap, in_=in_ap).then_inc(sem, 16)

        self.ops[q].append(run)
        self._mark(tok, reads, writes)
        return tok

    def barrier(self):
        toks = [(self.psem[f], self.cnt[f]) for f in ENG if self.cnt[f] > 0]
        toks += [(s, v) for s, v in zip(self.dsems, self.dval) if v > 0]
        for e in ENG:
            waits = []
            for sem, val in toks:
                if sem is self.psem[e]:
                    continue
                if self.waited[e].get(id(sem), 0) < val:
                    self.waited[e][id(sem)] = val
                    waits.append((sem, val))
            if waits:
                def run(eng, waits=waits):
                    for ws, wv in waits:
                        eng.wait_ge(ws, wv)
                self.ops[e].append(run)


def dims(cfg):
    cfg = {k: v for k, v in cfg.items() if k not in ('stop', 'arena', 'nds', 'nocast', 'dbg', 'pad')}
    D, NQ, NKV, SP, SS = cfg["D"], cfg["NQ"], cfg["NKV"], cfg["SP"], cfg["SS"]
    d = dict(cfg)
    d.update(KC=D // 128, DR=D // 2, NB=D // 256, DFF=4 * D, FC=4 * D // 128, QW=NQ * 128, KW=NKV * 128,
             G=NQ // NKV, nPo=SP // 2 // T, nSo=SS // 8 // T, nPc=SP // T, nSc=SS // T)
    d["IN"] = d["QW"] + 2 * d["KW"] + 2 * d["DR"] + 2 * D
    d["nOwn"] = d["nPo"] + d["nSo"]
    off = {}
    o = 0
    for name, n in (("gmix", d["KC"]), ("gmlp", d["KC"]), ("convw", d["NB"] * 4), ("convb", d["NB"]),
                    ("ba", 2 * d["NB"]), ("bi", 2 * d["NB"]), ("lam", 2 * d["NB"]), ("bg", 2 * d["KC"]),
                    ("qn", 128), ("kn", 128), ("iota", 32)):
        off[name] = (o, n)
        o += n
    d["ppoff"] = off
    d["NPP"] = o
    return d


def build(cfg):
    c = dims(cfg)
    D, NQ, NKV, SP, SS = c["D"], c["NQ"], c["NKV"], c["SP"], c["SS"]
    KC, DR, NB, DFF, FC, QW, KW, G, IN = c["KC"], c["DR"], c["NB"], c["DFF"], c["FC"], c["QW"], c["KW"], c["G"], c["IN"]
    nPo, nSo, nPc, nSc, nOwn = c["nPo"], c["nSo"], c["nPc"], c["nSc"], c["nOwn"]
    NPP = c["NPP"]
    ppoff = c["ppoff"]
    nc = bass.Bass("TRN2", target_bir_lowering=False)

    def din(name, shape, dt=F32):
        return nc.dram_tensor(name, list(shape), dt, kind="ExternalInput").ap()

    xc = [din("xc_p", [SP, D]), din("xc_s", [SS, D])]
    xo = din("xo", [nOwn * T, D])
    xh = din("xh", [nOwn * 4, D])
    posc = [din("pos_p", [128, SP // 64]), din("pos_s", [128, SS // 64])]
    poso = din("pos_o", [128, nOwn * 8])
    selc = [din("sel_p", [128, nPo * nPc]), din("sel_s", [128, nSo * nSc])]
    w_in = din("w_in", [D, IN])
    w_ao = din("w_ao", [QW, D])
    w_ro = din("w_ro", [DR, D])
    w_out = din("w_out", [D, D])
    w_up = din("w_up", [D, DFF])
    w_dn = din("w_dn", [DFF, D])
    lwa = din("lru_wa", [2, NB, 128, 128])
    lwi = din("lru_wi", [2, NB, 128, 128])
    pp = din("pp", [128, NPP])
    gfin = din("gfin", [128, D])
    yout = nc.dram_tensor("y", [nOwn * T, D], F32, kind="ExternalOutput").ap()

    def dscr(name, shape, dt):
        return nc.dram_tensor(name, list(shape), dt).ap()

    wb_in = dscr("wb_in", [D, IN], BF16)
    wb_ao = dscr("wb_ao", [QW, D], BF16)
    wb_ro = dscr("wb_ro", [DR, D], BF16)
    wb_out = dscr("wb_out", [D, D], BF16)
    wb_up = dscr("wb_up", [D, DFF], BF16)
    wb_dn = dscr("wb_dn", [DFF, D], BF16)
    Sseq = [SP, SS]
    KTs = [dscr(f"KT{s}", [NKV, 128, Sseq[s]], BF16) for s in range(2)]
    Vs = [dscr(f"V{s}", [NKV, 128, Sseq[s] // 128, 128], BF16) for s in range(2)]
    x1s = dscr("x1s", [T, D], F32)

    ARENA = cfg.get('arena', 53200)
    import contextlib
    es = contextlib.ExitStack()
    arena = es.enter_context(nc.sbuf_tensor("arena", [128, ARENA], F32))
    psum = es.enter_context(nc.psum_tensor("psum", [128, 8, 512], F32))
    sem_objs = {e: es.enter_context(nc.semaphore("p_" + e)) for e in ENG}
    dsems = [es.enter_context(nc.semaphore(f"d{i}")) for i in range(cfg.get('nds', 8))]
    P = Prog(sem_objs, dsems)

    HOLE_LO, HOLE_HI = 10 ** 9, 10 ** 9

    class Alloc:
        def __init__(self, base):
            if isinstance(base, tuple):
                self.o1, self.o2 = base
            else:
                self.o1 = base
                self.o2 = max(base, HOLE_HI)

        @property
        def o(self):
            return (self.o1, self.o2)

        def f32(self, n):
            if self.o1 + n <= HOLE_LO:
                assert self.o1 + n <= ARENA, (self.o1 + n, ARENA)
                ap = arena[:, self.o1:self.o1 + n]
                self.o1 += n
                return ap
            ap = arena[:, self.o2:self.o2 + n]
            self.o2 += n
            assert self.o2 <= ARENA, (self.o2, ARENA)
            return ap

        def bf16(self, n):
            w = (n + 1) // 2
            return self.f32(w).bitcast(BF16)[:, 0:n]

    A0 = Alloc(0)
    pp_sb = A0.f32(NPP)
    ident = A0.bf16(128)
    ones_bf = A0.bf16(128)
    der = A0.f32(10 * NB + 2 * KC)
    NTc = [nPc, nSc]
    csel = A0.f32(2 * NB * nOwn)
    sel_sb = [A0.f32(nPo * nPc), A0.f32(nSo * nSc)]
    inv_sb = A0.f32(32)
    RING_N = 2
    KCP = 8
    ring = [A0.bf16(KCP * 512) for _ in range(RING_N)]
    ringB = [Buf() for _ in range(RING_N)]
    ring_i = [0]
    BASE = A0.o
    bankB = [Buf() for _ in range(8)]
    constB = Buf()

    junk = A0.f32(16)
    BASE = A0.o
    junkB = Buf()
    for apx in [xc[0], xc[1], xo, xh, posc[0], posc[1], poso, selc[0], selc[1], w_in, w_ao, w_ro, w_out, w_up, w_dn, pp, gfin]:
        P.dma("sp", junk[0:1, 0:8], apx[0:1, 0:8], writes=[junkB])
    for apx in [lwa, lwi]:
        P.dma("sp", junk[0:1, 0:8], apx[0, 0, 0:1, 0:8], writes=[junkB])

    def ppv(name, i=0, n=1):
        o, _ = ppoff[name]
        return pp_sb[:, o + i:o + i + n]

    def v3(ap, a):
        return ap.rearrange("p (a b) -> p a b", a=a)

    P.dma("sp", pp_sb, pp, writes=[constB])
    P.op("pool", lambda e: e.memset(ones_bf, 1.0), writes=[constB])
    P.op("pool", lambda e: e.memset(chalf, 0.5), writes=[constB])
    P.op("pool", lambda e: e.memset(cmhalf, -0.5), writes=[constB])
    P.op("pool", lambda e: e.memset(c64, 64.0), writes=[constB])
    P.op("pool", lambda e: e.memset(c2pi, 2 * math.pi), writes=[constB])
    idf = A0.f32(128)
    chalf = A0.f32(512)
    cmhalf = A0.f32(16)
    c64 = A0.f32(4)
    c2pi = A0.f32(256)
    ncg = (D + 511) // 512
    ssqP = A0.f32(4 * ncg)
    ssqP3 = ssqP.rearrange("p (t c) -> p t c", t=4)
    ssqB = Buf()
    scrX = Buf()
    outB = Buf()
    sumB = Buf()
    cselB = Buf()
    identf = idf
    BASE = A0.o
    ident_in = din("ident", [128, 128])
    P.dma("sp", idf, ident_in, writes=[constB])
    P.op("dve", lambda e: e.tensor_copy(out=ident, in_=idf), reads=[constB], writes=[constB])
    for s in range(2):
        P.dma("sp", sel_sb[s], selc[s], writes=[constB])
    def load_lw(al):
        lw_sb = al.bf16(4 * NB * 128)
        lw4 = lw_sb.rearrange("p (g z n m) -> p g z n m", g=2, z=2, n=NB)
        lwB = Buf()
        for gi, src in enumerate((lwa, lwi)):
            for z in range(2):
                P.dma("pool", lw4[:, gi, z], src[z].rearrange("n c m -> c n m"), writes=[lwB])
        return lw4, lwB
    o_hc, o_nhc, o_hba, o_hbi, o_hbg = 0, 2 * NB, 4 * NB, 6 * NB, 8 * NB
    hc = der[:, o_hc:o_hc + 2 * NB]
    nhc = der[:, o_nhc:o_nhc + 2 * NB]
    hba = der[:, o_hba:o_hba + 2 * NB]
    hbi = der[:, o_hbi:o_hbi + 2 * NB]
    hbg = der[:, o_hbg:o_hbg + 2 * KC]
    tmpd = der[:, o_hbg + 2 * KC:o_hbg + 2 * KC + 2 * NB]
    lam = ppv("lam", 0, 2 * NB)
    P.op("act", lambda e: e.activation(out=tmpd, in_=lam, func=AF.Exp, scale=-1.0), reads=[constB], writes=[constB])
    P.op("act", lambda e: e.activation(out=tmpd, in_=tmpd, func=AF.Ln, bias=1.0), reads=[constB], writes=[constB])
    P.op("dve", lambda e: e.tensor_scalar_mul(out=hc, in0=tmpd, scalar1=-4.0), reads=[constB], writes=[constB])
    P.op("dve", lambda e: e.tensor_scalar_mul(out=nhc, in0=tmpd, scalar1=4.0), reads=[constB], writes=[constB])
    P.op("dve", lambda e: e.tensor_scalar_mul(out=hba, in0=ppv("ba", 0, 2 * NB), scalar1=0.5), reads=[constB], writes=[constB])
    P.op("dve", lambda e: e.tensor_scalar_mul(out=hbi, in0=ppv("bi", 0, 2 * NB), scalar1=0.5), reads=[constB], writes=[constB])
    P.op("dve", lambda e: e.tensor_scalar_mul(out=hbg, in0=ppv("bg", 0, 2 * KC), scalar1=0.5), reads=[constB], writes=[constB])
    P.op("act", lambda e: e.activation(out=inv_sb, in_=ppv("iota", 0, 32), func=AF.Exp, scale=-math.log(10000.0) / 32.0),
         reads=[constB], writes=[constB])
    wB = {}
    for name, src, dst in (("in", w_in, wb_in), ("ao", w_ao, wb_ao), ("ro", w_ro, wb_ro), ("out", w_out, wb_out),
                           ("up", w_up, wb_up), ("dn", w_dn, wb_dn)):
        wB[name] = []
        R = src.shape[0]
        step = max(128, min(R, (8 << 20) // src.shape[1] // 128 * 128))
        for r0 in range(0, R, step):
            r1 = min(R, r0 + step)
            b = Buf()
            wB[name].append(b)
            if not cfg.get('nocast'):
                P.dma("pool", dst[r0:r1, :], src[r0:r1, :], writes=[b])

    def finish():
        P.barrier()
        with nc.Block() as block:
            @block.tensor
            def _(e):
                for f in P.ops["pe"]:
                    f(e)

            @block.scalar
            def _(e):
                for f in P.ops["act"]:
                    f(e)

            @block.vector
            def _(e):
                for f in P.ops["dve"]:
                    f(e)

            @block.gpsimd
            def _(e):
                for f in P.ops["pool"]:
                    f(e)

            @block.sync
            def _(e):
                for f in P.ops["sp"]:
                    f(e)
        es.close()
        return nc


    dbgB = Buf()

    def dump(name, ap, bufs, dt=None):
        if not cfg.get('dbg'):
            return
        o = nc.dram_tensor("dbg_" + name, list(ap.shape), dt or ap.dtype, kind="ExternalOutput").ap()
        P.dma("sp", o, ap, reads=list(bufs), writes=[dbgB])

    if cfg.get('stop') == 'setup':
        return finish()
    def load_piece(Wb, wbuf, k0, nk, c0, cw):
        i = ring_i[0] % RING_N
        ring_i[0] += 1
        slot = v3(ring[i], KCP)
        P.dma("sp", slot[:, 0:nk, 0:cw], Wb[k0 * 128:(k0 + nk) * 128, c0:c0 + cw].rearrange("(k p) c -> p k c", p=128),
              reads=list(wbuf), writes=[ringB[i]])
        return slot, ringB[i]

    def proj_fm(Wb, wbuf, Kc, c0, cw, rhs_fn, rbufs, banks, N=T):
        nch = cw // 128
        for k0 in range(0, Kc, KCP):
            nk = min(KCP, Kc - k0)
            slot, sb = load_piece(Wb, wbuf, k0, nk, c0, cw)

            def fn(e, slot=slot, k0=k0, nk=nk):
                ins = None
                for kk in range(nk):
                    for cc in range(nch):
                        ins = e.matmul(psum[:, banks[cc], 0:N], lhsT=slot[:, kk, cc * 128:(cc + 1) * 128], rhs=rhs_fn(k0 + kk),
                                       start=(k0 + kk == 0), stop=(k0 + kk == Kc - 1))
                return ins
            P.op("pe", fn, reads=[sb] + rbufs, writes=[bankB[b] for b in banks[:nch]])

    def proj_tm(Wb, wbuf, Kc, c0, cw, lhs_fn, rbufs, banks, ntb=4, mrows=128):
        for k0 in range(0, Kc, KCP):
            nk = min(KCP, Kc - k0)
            slot, sb = load_piece(Wb, wbuf, k0, nk, c0, cw)

            def fn(e, slot=slot, k0=k0, nk=nk):
                ins = None
                for kk in range(nk):
                    for tb in range(ntb):
                        ins = e.matmul(psum[0:mrows, banks[tb], 0:cw], lhsT=lhs_fn(k0 + kk, tb), rhs=slot[:, kk, 0:cw],
                                       start=(k0 + kk == 0), stop=(k0 + kk == Kc - 1))
                return ins
            P.op("pe", fn, reads=[sb] + rbufs, writes=[bankB[b] for b in banks[:ntb]])

    def psbf(bank):
        return psum[:, bank, :].bitcast(BF16)

    tp_i = [0]

    def transpose_blocks(srcs, src_bufs, dst_ap, dst_buf):
        bank = 6 + (tp_i[0] % 2)
        tp_i[0] += 1
        n = len(srcs)
        pv = v3(psbf(bank)[:, 0:n * 128], n)

        def fn(e):
            ins = None
            for j, sap in enumerate(srcs):
                ins = e.transpose(out=pv[:, j, :], in_=sap, identity=ident)
            return ins
        P.op("pe", fn, reads=src_bufs + [constB], writes=[bankB[bank]])
        return pv, bank

    XH = D if D <= 2048 else D // 2
    NXH = D // XH

    def norm_T(xsrc_rows, gname, hT, hTB, S):
        xst, hn, ss, xstB, hnB = S["xst"], S["hn"], S["ss"], S["xstB"], S["hnB"]
        g_o, _ = ppoff[gname]
        for tb in range(4):
            src = xsrc_rows(tb)
            for hh in range(NXH):
                P.dma("sp", xst, src[:, hh * XH:(hh + 1) * XH], writes=[xstB])
                P.op("act", lambda e, hh=hh: e.activation(out=hn[:, hh * XH:(hh + 1) * XH], in_=xst, func=AF.Square, accum_out=ss[:, 4 + hh:5 + hh]),
                     reads=[xstB], writes=[hnB, S["ssB"]])
            if NXH == 1:
                P.op("dve", lambda e: e.tensor_scalar(out=ss[:, 1:2], in0=ss[:, 4:5], scalar1=1.0 / D, scalar2=EPS, op0=ALU.mult, op1=ALU.add),
                     reads=[S["ssB"]], writes=[S["ssB"]])
            else:
                P.op("dve", lambda e: e.tensor_tensor(out=ss[:, 0:1], in0=ss[:, 4:5], in1=ss[:, 5:6], op=ALU.add), reads=[S["ssB"]], writes=[S["ssB"]])
                P.op("dve", lambda e: e.tensor_scalar(out=ss[:, 1:2], in0=ss[:, 0:1], scalar1=1.0 / D, scalar2=EPS, op0=ALU.mult, op1=ALU.add),
                     reads=[S["ssB"]], writes=[S["ssB"]])
            P.op("act", lambda e: e.activation(out=ss[:, 3:4], in_=ss[:, 1:2], func=AF.Sqrt), reads=[S["ssB"]], writes=[S["ssB"]])
            P.op("dve", lambda e: e.reciprocal(out=ss[:, 2:3], in_=ss[:, 3:4]), reads=[S["ssB"]], writes=[S["ssB"]])
            for hh in range(NXH):
                if NXH > 1:
                    P.dma("sp", xst, src[:, hh * XH:(hh + 1) * XH], writes=[xstB])
                P.op("dve", lambda e, hh=hh: e.tensor_scalar(out=hn[:, hh * XH:(hh + 1) * XH], in0=xst, scalar1=ss[:, 2:3], scalar2=None, op0=ALU.mult),
                     reads=[xstB, S["ssB"]], writes=[hnB])
            for k0 in range(0, KC, 4):
                n = min(4, KC - k0)
                pv, bank = transpose_blocks([hn[:, (k0 + j) * 128:(k0 + j + 1) * 128] for j in range(n)], [hnB], None, None)
                gb = pp_sb[:, g_o + k0:g_o + k0 + n].unsqueeze(2).to_broadcast([128, n, 128])
                P.op("dve", lambda e, pv=pv, gb=gb, k0=k0, n=n, tb=tb: e.tensor_tensor(
                    out=hT[:, k0:k0 + n, tb * 128:(tb + 1) * 128], in0=pv, in1=gb, op=ALU.mult),
                    reads=[bankB[bank], constB], writes=[hTB])

    def hn_junk(S):
        return S["junk"]

    def rope_tables(pos_ap, S):
        rt, rtB = S["rt"], S["rtB"]
        pos8 = rt[:, 0:8]
        ang = v3(rt[:, 16:16 + 256], 4)
        angf = rt[:, 16:16 + 256]
        sargf = rt[:, 272:272 + 256]
        qi = rt[:, 528:528 + 256].bitcast(mybir.dt.int32)
        sin_t = v3(rt[:, 1040:1040 + 256], 4)
        cos_t = v3(rt[:, 1296:1296 + 256], 4)
        qf = rt[:, 784:784 + 256]
        P.dma("sp", pos8, pos_ap, writes=[rtB])
        for b in range(4):
            P.op("dve", lambda e, b=b: e.tensor_scalar(out=ang[:, b, 0:32], in0=inv_sb, scalar1=pos8[:, 2 * b:2 * b + 1], scalar2=None, op0=ALU.mult),
                 reads=[rtB, constB], writes=[rtB])
            P.op("dve", lambda e, b=b: e.tensor_scalar(out=ang[:, b, 32:64], in0=inv_sb, scalar1=pos8[:, 2 * b + 1:2 * b + 2], scalar2=None, op0=ALU.mult),
                 reads=[rtB, constB], writes=[rtB])
        for shift, dst in ((0.0, rt[:, 1040:1040 + 256]), (0.5 * math.pi, rt[:, 1296:1296 + 256])):
            P.op("dve", lambda e, shift=shift: e.tensor_scalar_add(out=sargf, in0=angf, scalar1=shift), reads=[rtB], writes=[rtB])
            P.op("dve", lambda e: e.tensor_scalar_mul(out=qf, in0=sargf, scalar1=1.0 / (2 * math.pi)), reads=[rtB], writes=[rtB])
            P.op("dve", lambda e: e.tensor_copy(out=qi, in_=qf), reads=[rtB], writes=[rtB])
            P.op("dve", lambda e: e.tensor_copy(out=qf, in_=qi), reads=[rtB], writes=[rtB])
            P.op("dve", lambda e: e.scalar_tensor_tensor(out=sargf, in0=qf, scalar=-2 * math.pi, in1=sargf, op0=ALU.mult, op1=ALU.add), reads=[rtB], writes=[rtB])
            P.op("dve", lambda e: e.tensor_scalar(out=qf, in0=sargf, scalar1=math.pi, scalar2=2 * math.pi, op0=ALU.is_gt, op1=ALU.mult), reads=[rtB], writes=[rtB])
            P.op("dve", lambda e: e.tensor_tensor(out=sargf, in0=sargf, in1=qf, op=ALU.subtract), reads=[rtB], writes=[rtB])
            P.op("act", lambda e, dst=dst: e.activation(out=dst, in_=sargf, func=AF.Sin), reads=[rtB], writes=[rtB])
        return sin_t, cos_t

    def qk_norm_rope(bank, H, gname, sin_b, cos_b, S, out_bf):
        kq, sq, st, B = S["kq"], S["sq"], S["qst"], S["qkB"]
        n = H * 128
        kq3 = v3(kq[:, 0:n], H)
        sq3 = v3(sq[:, 0:n], H)
        g_o, _ = ppoff[gname]
        grow = pp_sb[:, g_o:g_o + 128].unsqueeze(1).to_broadcast([128, H, 128])
        P.op("act", lambda e: e.activation(out=kq[:, 0:n], in_=psum[:, bank, 0:n], func=AF.Identity), reads=[bankB[bank]], writes=[B])
        P.op("act", lambda e: e.activation(out=sq[:, 0:n], in_=psum[:, bank, 0:n], func=AF.Square), reads=[bankB[bank]], writes=[B])
        P.op("dve", lambda e: e.tensor_reduce(out=st[:, 0:H], in_=sq3, axis=AX.X, op=ALU.add), reads=[B], writes=[B])
        P.op("dve", lambda e: e.tensor_scalar(out=st[:, 0:H], in0=st[:, 0:H], scalar1=1.0 / 128, scalar2=EPS, op0=ALU.mult, op1=ALU.add), reads=[B], writes=[B])
        P.op("act", lambda e: e.activation(out=st[:, 8:8 + H], in_=st[:, 0:H], func=AF.Sqrt), reads=[B], writes=[B])
        P.op("dve", lambda e: e.reciprocal(out=st[:, 0:H], in_=st[:, 8:8 + H]), reads=[B], writes=[B])
        P.op("dve", lambda e: e.tensor_tensor(out=kq3, in0=kq3, in1=st[:, 0:H].unsqueeze(2).to_broadcast([128, H, 128]), op=ALU.mult), reads=[B], writes=[B])
        P.op("dve", lambda e: e.tensor_tensor(out=kq3, in0=kq3, in1=grow, op=ALU.mult), reads=[B, constB], writes=[B])
        x1 = kq3[:, :, 0::2]
        x2 = kq3[:, :, 1::2]
        cb = cos_b.unsqueeze(1).to_broadcast([128, H, 64])
        sb = sin_b.unsqueeze(1).to_broadcast([128, H, 64])
        t1 = v3(sq[:, 0:H * 64], H)
        t2 = v3(sq[:, H * 64:H * 128], H)
        o3 = v3(out_bf[:, 0:n], H)
        oB = S["qoB"]
        P.op("dve", lambda e: e.tensor_tensor(out=t1, in0=x1, in1=cb, op=ALU.mult), reads=[B, S["rtB"]], writes=[B])
        P.op("dve", lambda e: e.tensor_tensor(out=t2, in0=x2, in1=sb, op=ALU.mult), reads=[B, S["rtB"]], writes=[B])
        P.op("dve", lambda e: e.tensor_tensor(out=o3[:, :, 0::2], in0=t1, in1=t2, op=ALU.subtract), reads=[B], writes=[oB])
        P.op("dve", lambda e: e.tensor_tensor(out=t1, in0=x1, in1=sb, op=ALU.mult), reads=[B, S["rtB"]], writes=[B])
        P.op("dve", lambda e: e.tensor_tensor(out=t2, in0=x2, in1=cb, op=ALU.mult), reads=[B, S["rtB"]], writes=[B])
        P.op("dve", lambda e: e.tensor_tensor(out=o3[:, :, 1::2], in0=t1, in1=t2, op=ALU.add), reads=[B], writes=[oB])

    LNAMES = ("xc", "tr", "ti", "a", "th", "a2", "m2", "hs", "hf", "a_1", "th_1")

    def mk_L(al):
        S_L = {}
        for nm in LNAMES:
            S_L[nm] = (al.f32(T), Buf())
        S_L["iu"] = S_L["a2"]
        S_L["u"] = S_L["th"]
        S_L["u_1"] = S_L["th_1"]
        S_L["xcb"] = (al.bf16(T), Buf())
        return S_L

    dbgXE = [None]

    dbgnames = []

    def lru_block(xe, XEBuf, blk, L, lw4, lwB, mode, extra):
        xcf, xcB = L["xc"]
        tr, trB = L["tr"]
        ti, tiB = L["ti"]
        a, aB = L["a"]
        th, thB = L["th"]
        a2, a2B = L["a2"]
        m2, m2B = L["m2"]
        iu, iuB = L["iu"]
        u, uB = L["u"]
        hs, hsB = L["hs"]
        hf, hfB = L["hf"]
        xcb, xcbB = L["xcb"]

        def _dd(i):
            if mode == 'own' and dbgXE[0] is not None and extra[0] == 0 and blk == 0:
                dump('Y%d_%d' % (i, len(dbgnames)), dbgXE[0], [XEBuf] + [L[n][1] for n in L])
                dbgnames.append(i)
        cw_o = ppoff["convw"][0]
        cb_o = ppoff["convb"][0]
        P.op("dve", lambda e: e.tensor_scalar(out=xcf, in0=xe[:, 0:T], scalar1=pp_sb[:, cw_o + blk * 4:cw_o + blk * 4 + 1],
                                              scalar2=pp_sb[:, cb_o + blk:cb_o + blk + 1], op0=ALU.mult, op1=ALU.add),
             reads=[XEBuf, constB], writes=[xcB])
        _dd(0)
        for j in range(1, 4):
            P.op("dve", lambda e, j=j: e.scalar_tensor_tensor(out=xcf, in0=xe[:, j:j + T], scalar=pp_sb[:, cw_o + blk * 4 + j:cw_o + blk * 4 + j + 1],
                                                          in1=xcf, op0=ALU.mult, op1=ALU.add),
                 reads=[XEBuf, constB, xcB], writes=[xcB])
            _dd(1)
        P.op("pool", lambda e: e.tensor_copy(out=xcb, in_=xcf), reads=[xcB], writes=[xcbB])
        _dd(2)
        def _zbody(z):
            zi = z * NB + blk
            banks = (4, 5)
            a, aB = L["a"] if z == 0 else L["a_1"]
            th, thB = L["th"] if z == 0 else L["th_1"]
            u, uB = th, thB

            def fn(e, z=z):
                e.matmul(psum[:, banks[0], 0:T], lhsT=lw4[:, 0, z, blk, :], rhs=xcb, start=True, stop=True)
                return e.matmul(psum[:, banks[1], 0:T], lhsT=lw4[:, 1, z, blk, :], rhs=xcb, start=True, stop=True)
            P.op("pe", fn, reads=[xcbB, lwB], writes=[bankB[banks[0]], bankB[banks[1]]])
            _dd(3)
            P.op("act", lambda e, zi=zi: e.activation(out=tr, in_=psum[:, banks[0], 0:T], func=AF.Tanh, scale=0.5, bias=hba[:, zi:zi + 1]),
                 reads=[bankB[banks[0]], constB], writes=[trB])
            _dd(4)
            P.op("act", lambda e, zi=zi: e.activation(out=ti, in_=psum[:, banks[1], 0:T], func=AF.Tanh, scale=0.5, bias=hbi[:, zi:zi + 1]),
                 reads=[bankB[banks[1]], constB], writes=[tiB])
            _dd(5)
            P.op("act", lambda e, zi=zi: e.activation(out=a, in_=tr, func=AF.Exp, scale=hc[:, zi:zi + 1], bias=hc[:, zi:zi + 1]),
                 reads=[trB, constB], writes=[aB])
            _dd(6)
            P.op("act", lambda e, zi=zi: e.activation(out=th, in_=tr, func=AF.Tanh, scale=nhc[:, zi:zi + 1], bias=nhc[:, zi:zi + 1]),
                 reads=[trB, constB], writes=[thB])
            _dd(7)
            P.op("act", lambda e: e.activation(out=a2, in_=a, func=AF.Square), reads=[aB], writes=[a2B])
            _dd(8)
            P.op("dve", lambda e: e.scalar_tensor_tensor(out=m2, in0=a2, scalar=1.0, in1=th, op0=ALU.add, op1=ALU.mult),
                 reads=[a2B, thB], writes=[m2B])
            _dd(9)
            P.op("dve", lambda e: e.tensor_scalar_max(out=m2, in0=m2, scalar1=1e-30), reads=[m2B], writes=[m2B])
            _dd(10)
            P.op("act", lambda e: e.activation(out=m2, in_=m2, func=AF.Sqrt), reads=[m2B], writes=[m2B])
            _dd(11)
            P.op("dve", lambda e: e.scalar_tensor_tensor(out=iu, in0=ti, scalar=1.0, in1=xcf, op0=ALU.add, op1=ALU.mult),
                 reads=[tiB, xcB], writes=[iuB])
            _dd(12)
            P.op("dve", lambda e: e.scalar_tensor_tensor(out=u, in0=iu, scalar=0.5, in1=m2, op0=ALU.mult, op1=ALU.mult), reads=[iuB, m2B], writes=[uB])
            _dd(13)
            if mode == "ctx":
                sA4, sE4, t = extra
                if z == 0:
                    P.op("dve", lambda e: e.tensor_tensor_scan(out=hs, data0=a, data1=u, initial=0.0, op0=ALU.mult, op1=ALU.add),
                         reads=[aB, uB], writes=[hsB])
                    P.op("pool", lambda e: e.tensor_copy(out=sE4[:, 0, blk, t:t + 1], in_=hs[:, T - 1:T]), reads=[hsB], writes=[sumB])
                else:
                    P.op("dve", lambda e: e.tensor_tensor_scan(out=hs[:, ::-1], data0=a[:, ::-1], data1=u[:, ::-1], initial=0.0, op0=ALU.mult, op1=ALU.add),
                         reads=[aB, uB], writes=[hsB])
                    P.op("pool", lambda e: e.tensor_copy(out=sE4[:, 1, blk, t:t + 1], in_=hs[:, 0:1]), reads=[hsB], writes=[sumB])
                P.op("dve", lambda e, z=z: e.tensor_reduce(out=sA4[:, z, blk, t:t + 1], in_=a, axis=AX.X, op=ALU.mult),
                     reads=[aB], writes=[sumB])
            else:
                k, gy, gyB, rec, recB = extra
                cs4 = csel.rearrange("p (z n k) -> p z n k", z=2, n=NB)
                dd = (dbgXE[0] is not None and k == 0 and blk == 0)
                if z == 0:
                    if dd:
                        dump('X0', dbgXE[0], [XEBuf, uB, aB])
                    P.op("dve", lambda e: e.tensor_tensor_scan(out=hf, data0=a, data1=u, initial=cs4[:, 0, blk, k:k + 1], op0=ALU.mult, op1=ALU.add),
                         reads=[aB, uB, cselB], writes=[hfB])
                    if dd:
                        dump('X1', dbgXE[0], [XEBuf, hfB])
                else:
                    P.op("dve", lambda e: e.tensor_tensor_scan(out=hs[:, ::-1], data0=a[:, ::-1], data1=u[:, ::-1], initial=cs4[:, 1, blk, k:k + 1],
                                                               op0=ALU.mult, op1=ALU.add),
                         reads=[aB, uB, cselB], writes=[hsB])
                    if dd:
                        dump('X2', dbgXE[0], [XEBuf, hsB])
                    P.op("pool", lambda e: e.tensor_tensor(out=hs, in0=hs, in1=hf, op=ALU.add), reads=[hsB, hfB], writes=[hsB])
                    if dd:
                        dump('X3', dbgXE[0], [XEBuf, hsB])
                    P.op("dve", lambda e: e.tensor_tensor(out=rec[:, blk, :], in0=hs, in1=gy[:, blk, :], op=ALU.mult),
                         reads=[hsB, gyB], writes=[recB])
                    if dd:
                        dump('X4', dbgXE[0], [XEBuf, recB])

        for z in range(2):
            _zbody(z)

    def mk_S(al):
        S = {}
        S["xst"] = al.f32(XH); S["xstB"] = Buf()
        S["hn"] = al.bf16(D); S["hnB"] = Buf()
        S["junk"] = S["hn"]; S["junkB"] = S["hnB"]
        S["ss"] = al.f32(8); S["ssB"] = Buf()
        S["rt"] = al.f32(1552); S["rtB"] = Buf()
        S["kq"] = al.f32(512); S["sq"] = al.f32(512); S["qst"] = al.f32(16); S["qkB"] = Buf(); S["qoB"] = Buf()
        return S

    o_k, o_v, o_xr, o_yr, o_g = QW, QW + KW, QW + 2 * KW, QW + 2 * KW + DR, QW + 2 * KW + 2 * DR
    NTc = [nPc, nSc]
    scrB = [Buf(), Buf()]

    Actx = Alloc(BASE)
    hTc = v3(Actx.bf16(KC * T), KC); hTcB = Buf()
    Sc = mk_S(Actx)
    krb = Actx.bf16(512)
    KT_sb = v3(Actx.bf16(NKV * T), NKV); KT_B = Buf()
    v_sb = Actx.bf16(4 * KW); v_B = Buf()
    XE = [v3(Actx.f32(NB * (T + 4)), NB) for _ in range(2)]
    XEB = [Buf(), Buf()]
    Lc = mk_L(Actx)
    lw4c, lwBc = load_lw(Actx)
    sumA = [Actx.f32(2 * NB * NTc[s]).rearrange("p (z n t) -> p z n t", z=2, n=NB) for s in range(2)]
    sumE = [Actx.f32(2 * NB * NTc[s]).rearrange("p (z n t) -> p z n t", z=2, n=NB) for s in range(2)]
    carC = [Actx.f32(2 * NB * NTc[s]).rearrange("p (z n t) -> p z n t", z=2, n=NB) for s in range(2)]
    tmpc = Actx.f32(2 * NB * max(NTc))

    def ctx_lru(seq, t, last):
        xe = XE[t % 2]
        if t == 0:
            P.op("pool", lambda e: e.memset(xe[:, :, 0:1], 0.0), writes=[XEB[t % 2]])
        if last:
            P.op("pool", lambda e: e.memset(xe[:, :, T + 1:T + 3], 0.0), writes=[XEB[t % 2]])
        for blk in range(NB):
            lru_block(xe[:, blk, :], XEB[t % 2], blk, Lc, lw4c, lwBc, "ctx", (sumA[seq], sumE[seq], t))

    def ctx_tile(seq, t):
        S = Sc
        hT, hTB = hTc, hTcB
        norm_T(lambda tb: xc[seq][t * T + tb * 128:t * T + (tb + 1) * 128, :], "gmix", hT, hTB, S)
        sin_t, cos_t = rope_tables(posc[seq][:, t * 8:(t + 1) * 8], S)
        proj_tm(wb_in, wB["in"], KC, o_k, KW, lambda kc, tb: hT[:, kc, tb * 128:(tb + 1) * 128], [hTB], [0, 1, 2, 3])
        for tb in range(4):
            qk_norm_rope(tb, NKV, "kn", sin_t[:, tb, :], cos_t[:, tb, :], S, krb)
            pv, bank = transpose_blocks([krb[:, g * 128:(g + 1) * 128] for g in range(NKV)], [S["qoB"]], None, None)
            P.op("act", lambda e, pv=pv, tb=tb: e.activation(out=KT_sb[:, :, tb * 128:(tb + 1) * 128], in_=pv, func=AF.Identity),
                 reads=[bankB[bank]], writes=[KT_B])
        for g in range(NKV):
            P.dma("sp", KTs[seq][g][:, t * T:(t + 1) * T], KT_sb[:, g, :], reads=[KT_B], writes=[scrB[seq]])
        proj_tm(wb_in, wB["in"], KC, o_v, KW, lambda kc, tb: hT[:, kc, tb * 128:(tb + 1) * 128], [hTB], [0, 1, 2, 3])
        v3d = v3(v_sb, 4)
        for tb in range(4):
            P.op("act", lambda e, tb=tb: e.activation(out=v3d[:, tb, :], in_=psum[:, tb, 0:KW], func=AF.Identity), reads=[bankB[tb]], writes=[v_B])
        for g in range(NKV):
            P.dma("sp", Vs[seq][g][:, t * 4:(t + 1) * 4, :], v3d[:, :, g * 128:(g + 1) * 128], reads=[v_B], writes=[scrB[seq]])
        xe = XE[t % 2]
        for c0 in range(0, DR, 512):
            cw = min(512, DR - c0)
            proj_fm(wb_in, wB["in"], KC, o_xr + c0, cw, lambda kc: hT[:, kc, :], [hTB], [0, 1, 2, 3])
            for cc in range(cw // 128):
                P.op("act", lambda e, cc=cc, c0=c0: e.activation(out=xe[:, c0 // 128 + cc, 1:T + 1], in_=psum[:, cc, 0:T], func=AF.Identity),
                     reads=[bankB[cc]], writes=[XEB[t % 2]])
        if t > 0:
            xp = XE[(t - 1) % 2]
            P.op("pool", lambda e: e.tensor_copy(out=xp[:, :, T + 1:T + 3], in_=xe[:, :, 1:3]), reads=[XEB[t % 2]], writes=[XEB[(t - 1) % 2]])
            P.op("pool", lambda e: e.tensor_copy(out=xe[:, :, 0:1], in_=xp[:, :, T:T + 1]), reads=[XEB[(t - 1) % 2]], writes=[XEB[t % 2]])
            ctx_lru(seq, t - 1, False)

    for seq in range(2):
        for t in range(NTc[seq]):
            ctx_tile(seq, t)
            if cfg.get('stop') == 'ctx1':
                return finish()
        ctx_lru(seq, NTc[seq] - 1, True)
    if cfg.get('stop') == 'ctx':
        return finish()

    def carries(seq):
        nT = NTc[seq]
        A4, E4, C4 = sumA[seq], sumE[seq], carC[seq]
        P.op("dve", lambda e: e.memset(C4[:, 0, :, 0:1], 0.0), reads=[sumB], writes=[sumB])
        P.op("dve", lambda e: e.memset(C4[:, 1, :, nT - 1:nT], 0.0), reads=[sumB], writes=[sumB])
        for t in range(1, nT):
            P.op("dve", lambda e, t=t: e.tensor_tensor(out=C4[:, 0, :, t:t + 1], in0=A4[:, 0, :, t - 1:t], in1=C4[:, 0, :, t - 1:t], op=ALU.mult),
                 reads=[sumB], writes=[sumB])
            P.op("dve", lambda e, t=t: e.tensor_tensor(out=C4[:, 0, :, t:t + 1], in0=C4[:, 0, :, t:t + 1], in1=E4[:, 0, :, t - 1:t], op=ALU.add),
                 reads=[sumB], writes=[sumB])
        for t in range(nT - 2, -1, -1):
            P.op("dve", lambda e, t=t: e.tensor_tensor(out=C4[:, 1, :, t:t + 1], in0=A4[:, 1, :, t + 1:t + 2], in1=C4[:, 1, :, t + 1:t + 2], op=ALU.mult),
                 reads=[sumB], writes=[sumB])
            P.op("dve", lambda e, t=t: e.tensor_tensor(out=C4[:, 1, :, t:t + 1], in0=C4[:, 1, :, t:t + 1], in1=E4[:, 1, :, t + 1:t + 2], op=ALU.add),
                 reads=[sumB], writes=[sumB])

    def select_carry(k):
        seq = 0 if k < nPo else 1
        kk = k if k < nPo else k - nPo
        nT = NTc[seq]
        C3 = carC[seq].rearrange("p z n t -> p (z n) t")
        selk = sel_sb[seq][:, kk * nT:(kk + 1) * nT].unsqueeze(1).to_broadcast([128, 2 * NB, nT])
        t3 = tmpc[:, 0:2 * NB * nT].rearrange("p (zn t) -> p zn t", t=nT)
        P.op("dve", lambda e: e.tensor_tensor(out=t3, in0=C3, in1=selk, op=ALU.mult), reads=[sumB, constB], writes=[sumB])
        P.op("dve", lambda e: e.tensor_reduce(out=csel.rearrange("p (zn k) -> p zn k", k=nOwn)[:, :, k], in_=t3, axis=AX.X, op=ALU.add),
             reads=[sumB], writes=[cselB])

    for seq in range(2):
        carries(seq)
    for k in range(nOwn):
        select_carry(k)
    P.barrier()

    def own_tile(k):
        seq = 0 if k < nPo else 1
        Sq = Sseq[seq]
        nkb = Sq // 128
        Aa = Alloc(BASE)
        hT = v3(Aa.bf16(KC * T), KC); hTB = Buf()
        QT = v3(Aa.bf16(NQ * T), NQ); QTB = Buf()
        rec = v3(Aa.bf16(NB * T), NB); recB = Buf()
        KEEP = Aa.o
        S = mk_S(Aa)
        qrb = Aa.bf16(512)
        if cfg.get('pad'):
            Aa.f32(cfg['pad'])
        XEo = v3(Aa.f32(NB * (T + 4)), NB); XEoB = Buf()
        gy = v3(Aa.bf16(NB * T), NB); gyB = Buf()
        L = mk_L(Aa)
        lw4, lwB = load_lw(Aa)
        hhn = S["hn"]; hhB = S["hnB"]
        hhT = v3(Aa.bf16(KC * 4), KC); hhTB = Buf()
        xhs = S["xst"]; xhsB = S["xstB"]

        norm_T(lambda tb: xo[k * T + tb * 128:k * T + (tb + 1) * 128, :], "gmix", hT, hTB, S)
        sin_t, cos_t = rope_tables(poso[:, k * 8:(k + 1) * 8], S)
        xst4 = S["xst"][0:4, :]
        ss = S["ss"]
        for hh in range(NXH):
            P.dma("sp", xst4, xh[k * 4:(k + 1) * 4, hh * XH:(hh + 1) * XH], writes=[S["xstB"]])
            P.op("act", lambda e, hh=hh: e.activation(out=hhn[0:4, hh * XH:(hh + 1) * XH], in_=xst4, func=AF.Square, accum_out=ss[0:4, 4 + hh:5 + hh]),
                 reads=[S["xstB"]], writes=[hhB, S["ssB"]])
        if NXH == 1:
            P.op("dve", lambda e: e.tensor_scalar(out=ss[0:4, 1:2], in0=ss[0:4, 4:5], scalar1=1.0 / D, scalar2=EPS, op0=ALU.mult, op1=ALU.add), reads=[S["ssB"]], writes=[S["ssB"]])
        else:
            P.op("dve", lambda e: e.tensor_tensor(out=ss[0:4, 0:1], in0=ss[0:4, 4:5], in1=ss[0:4, 5:6], op=ALU.add), reads=[S["ssB"]], writes=[S["ssB"]])
            P.op("dve", lambda e: e.tensor_scalar(out=ss[0:4, 1:2], in0=ss[0:4, 0:1], scalar1=1.0 / D, scalar2=EPS, op0=ALU.mult, op1=ALU.add), reads=[S["ssB"]], writes=[S["ssB"]])
        P.op("act", lambda e: e.activation(out=ss[0:4, 3:4], in_=ss[0:4, 1:2], func=AF.Sqrt), reads=[S["ssB"]], writes=[S["ssB"]])
        P.op("dve", lambda e: e.reciprocal(out=ss[0:4, 2:3], in_=ss[0:4, 3:4]), reads=[S["ssB"]], writes=[S["ssB"]])
        for hh in range(NXH):
            if NXH > 1:
                P.dma("sp", xst4, xh[k * 4:(k + 1) * 4, hh * XH:(hh + 1) * XH], writes=[S["xstB"]])
            P.op("dve", lambda e, hh=hh: e.tensor_scalar(out=hhn[0:4, hh * XH:(hh + 1) * XH], in0=xst4, scalar1=ss[0:4, 2:3], scalar2=None, op0=ALU.mult),
                 reads=[S["xstB"], S["ssB"]], writes=[hhB])
        g_o = ppoff["gmix"][0]
        for k0 in range(0, KC, 4):
            n = min(4, KC - k0)
            bank = 6 + (tp_i[0] % 2)
            tp_i[0] += 1
            pv = v3(psbf(bank)[:, 0:n * 4], n)

            def fn(e, pv=pv, k0=k0, n=n):
                ins = None
                for j in range(n):
                    ins = e.transpose(out=pv[:, j, :], in_=hhn[0:4, (k0 + j) * 128:(k0 + j + 1) * 128], identity=ident[0:4, 0:4])
                return ins
            P.op("pe", fn, reads=[hhB, constB], writes=[bankB[bank]])
            gb = pp_sb[:, g_o + k0:g_o + k0 + n].unsqueeze(2).to_broadcast([128, n, 4])
            P.op("dve", lambda e, pv=pv, gb=gb, k0=k0, n=n: e.tensor_tensor(out=hhT[:, k0:k0 + n, :], in0=pv, in1=gb, op=ALU.mult),
                 reads=[bankB[bank], constB], writes=[hhTB])
        for c0 in range(0, QW, 512):
            cw = min(512, QW - c0)
            H = cw // 128
            proj_tm(wb_in, wB["in"], KC, c0, cw, lambda kc, tb: hT[:, kc, tb * 128:(tb + 1) * 128], [hTB], [0, 1, 2, 3])
            for tb in range(4):
                qk_norm_rope(tb, H, "qn", sin_t[:, tb, :], cos_t[:, tb, :], S, qrb)
                pv, bank = transpose_blocks([qrb[:, h * 128:(h + 1) * 128] for h in range(H)], [S["qoB"]], None, None)
                P.op("act", lambda e, pv=pv, tb=tb, c0=c0, H=H: e.activation(out=QT[:, c0 // 128:c0 // 128 + H, tb * 128:(tb + 1) * 128], in_=pv, func=AF.Identity),
                     reads=[bankB[bank]], writes=[QTB])
        for c0 in range(0, DR, 512):
            cw = min(512, DR - c0)
            proj_fm(wb_in, wB["in"], KC, o_xr + c0, cw, lambda kc: hT[:, kc, :], [hTB], [0, 1, 2, 3])
            for cc in range(cw // 128):
                P.op("act", lambda e, cc=cc, c0=c0: e.activation(out=XEo[:, c0 // 128 + cc, 1:T + 1], in_=psum[:, cc, 0:T], func=AF.Identity),
                     reads=[bankB[cc]], writes=[XEoB])
            proj_tm(wb_in, wB["in"], KC, o_xr + c0, cw, lambda kc, tb: hhT[:, kc, :], [hhTB], [4], ntb=1, mrows=4)
            P.op("act", lambda e, c0=c0, cw=cw: e.activation(out=xhs[0:4, c0:c0 + cw], in_=psum[0:4, 4, 0:cw], func=AF.Identity), reads=[bankB[4]], writes=[xhsB])
        for b0 in range(0, NB, 4):
            n = min(4, NB - b0)
            bank = 6 + (tp_i[0] % 2)
            tp_i[0] += 1
            pvf = v3(psum[:, bank, 0:n * 4], n)

            def fn(e, pvf=pvf, b0=b0, n=n):
                ins = None
                for j in range(n):
                    ins = e.transpose(out=pvf[:, j, :], in_=xhs[0:4, (b0 + j) * 128:(b0 + j + 1) * 128], identity=identf[0:4, 0:4])
                return ins
            P.op("pe", fn, reads=[xhsB, constB], writes=[bankB[bank]])
            P.op("act", lambda e, pvf=pvf, b0=b0, n=n: e.activation(out=XEo[:, b0:b0 + n, 0:1], in_=pvf[:, :, 0:1], func=AF.Identity), reads=[bankB[bank]], writes=[XEoB])
            P.op("act", lambda e, pvf=pvf, b0=b0, n=n: e.activation(out=XEo[:, b0:b0 + n, T + 1:T + 3], in_=pvf[:, :, 1:3], func=AF.Identity), reads=[bankB[bank]], writes=[XEoB])
        if k == 0:
            dump('XE', XEo, [XEoB])
        g_yv, g_yvB = L["xc"]
        g_y2, g_y2B = L["tr"]
        g_wv, g_wvB = L["ti"]
        g_tv, g_tvB = L["a"]
        for c0 in range(0, DR, 512):
            cw = min(512, DR - c0)
            proj_fm(wb_in, wB["in"], KC, o_yr + c0, cw, lambda kc: hT[:, kc, :], [hTB], [0, 1, 2, 3])
            for cc in range(cw // 128):
                blk = c0 // 128 + cc
                P.op("act", lambda e, cc=cc: e.activation(out=g_yv, in_=psum[:, cc, 0:T], func=AF.Identity), reads=[bankB[cc]], writes=[g_yvB])
                if k == 0 and blk == NB - 1:
                    dump('G0', XEo, [XEoB, g_yvB, g_y2B, g_wvB, g_tvB, gyB])
                P.op("dve", lambda e: e.tensor_tensor(out=g_y2, in0=g_yv, in1=g_yv, op=ALU.mult), reads=[g_yvB], writes=[g_y2B])
                if k == 0 and blk == NB - 1:
                    dump('G1', XEo, [XEoB, g_yvB, g_y2B, g_wvB, g_tvB, gyB])
                P.op("dve", lambda e: e.tensor_scalar(out=g_y2, in0=g_y2, scalar1=0.044715, scalar2=1.0, op0=ALU.mult, op1=ALU.add), reads=[g_y2B], writes=[g_y2B])
                if k == 0 and blk == NB - 1:
                    dump('G2', XEo, [XEoB, g_yvB, g_y2B, g_wvB, g_tvB, gyB])
                P.op("dve", lambda e: e.tensor_tensor(out=g_wv, in0=g_y2, in1=g_yv, op=ALU.mult), reads=[g_y2B, g_yvB], writes=[g_wvB])
                if k == 0 and blk == NB - 1:
                    dump('G3', XEo, [XEoB, g_yvB, g_y2B, g_wvB, g_tvB, gyB])
                P.op("act", lambda e: e.activation(out=g_tv, in_=g_wv, func=AF.Tanh, scale=math.sqrt(2.0 / math.pi)), reads=[g_wvB], writes=[g_tvB])
                if k == 0 and blk == NB - 1:
                    dump('G4', XEo, [XEoB, g_yvB, g_y2B, g_wvB, g_tvB, gyB])
                P.op("dve", lambda e: e.scalar_tensor_tensor(out=g_tv, in0=g_tv, scalar=1.0, in1=g_yv, op0=ALU.add, op1=ALU.mult), reads=[g_tvB, g_yvB], writes=[g_tvB])
                if k == 0 and blk == NB - 1:
                    dump('G5', XEo, [XEoB, g_yvB, g_y2B, g_wvB, g_tvB, gyB])
                P.op("dve", lambda e, blk=blk: e.tensor_scalar_mul(out=gy[:, blk, :], in0=g_tv, scalar1=0.5), reads=[g_tvB], writes=[gyB])
                if k == 0 and blk == NB - 1:
                    dump('G6', XEo, [XEoB, g_yvB, g_y2B, g_wvB, g_tvB, gyB])
        if k == 0:
            dump('XEb', XEo, [XEoB])
            dbgXE[0] = XEo
        for blk in range(NB):
            lru_block(XEo[:, blk, :], XEoB, blk, L, lw4, lwB, "own", (k, gy, gyB, rec, recB))
            if k == 0 and blk == 0:
                dump('XEc', XEo, [XEoB, L['hs'][1], recB])
        if k == 0:
            dump('XEd', XEo, [XEoB, recB])
        if k == 0:
            for nm in ('xc', 'tr', 'ti', 'a', 'm2', 'u', 'hs', 'hf'):
                dump('L_' + nm, L[nm][0], [L[nm][1]])
            dump('hT', hT, [hTB]); dump('QT', QT, [QTB]); dump('rec', rec, [recB]); dump('gy', gy, [gyB]); dump('csel', csel, [cselB])
            dump('KT', KTs[0], [scrB[0]]); dump('V', Vs[0], [scrB[0]])
        P.barrier()

        Ab = Alloc(KEEP)
        attn = v3(Ab.bf16(NQ * T), NQ); attnB = Buf()
        KEEP2 = Ab.o
        KTg = [Ab.bf16(Sq) for _ in range(2)]
        Vg = [Ab.bf16(Sq) for _ in range(2)]
        KVB = [Buf(), Buf()]
        PT = [Ab.bf16(T) for _ in range(3)]
        PTB = [Buf() for _ in range(3)]
        rcp = Ab.f32(T); rcpB = Buf()
        pi = 0
        for g in range(NKV):
            kt = KTg[g % 2]
            vg = v3(Vg[g % 2], nkb)
            P.dma("sp", kt, KTs[seq][g], reads=[scrB[seq]], writes=[KVB[g % 2]])
            P.dma("sp", vg, Vs[seq][g], reads=[scrB[seq]], writes=[KVB[g % 2]])
            for j in range(G):
                h = g * G + j
                for kb in range(nkb):
                    sbank = kb % 3
                    P.op("pe", lambda e, kt=kt, kb=kb, h=h, sbank=sbank: e.matmul(psum[:, sbank, 0:T], lhsT=kt[:, kb * 128:(kb + 1) * 128], rhs=QT[:, h, :], start=True, stop=True),
                         reads=[KVB[g % 2], QTB], writes=[bankB[sbank]])
                    pt = PT[pi % 3]
                    ptB = PTB[pi % 3]
                    pi += 1
                    P.op("act", lambda e, pt=pt, sbank=sbank: e.activation(out=pt, in_=psum[:, sbank, 0:T], func=AF.Exp, scale=1.0 / math.sqrt(128.0)),
                         reads=[bankB[sbank]], writes=[ptB])

                    def fn(e, vg=vg, kb=kb, pt=pt):
                        e.matmul(psum[:, 4, 0:T], lhsT=vg[:, kb, :], rhs=pt, start=(kb == 0), stop=(kb == nkb - 1))
                        return e.matmul(psum[:, 5, 0:T], lhsT=ones_bf, rhs=pt, start=(kb == 0), stop=(kb == nkb - 1))
                    P.op("pe", fn, reads=[KVB[g % 2], ptB, constB], writes=[bankB[4], bankB[5]])
                P.op("dve", lambda e: e.reciprocal(out=rcp, in_=psum[:, 5, 0:T]), reads=[bankB[5]], writes=[rcpB])
                P.op("dve", lambda e, h=h: e.tensor_tensor(out=attn[:, h, :], in0=psum[:, 4, 0:T], in1=rcp, op=ALU.mult), reads=[bankB[4], rcpB], writes=[attnB])
        if k == 0:
            dump('attn', attn, [attnB])
        P.barrier()

        Ac = Alloc(KEEP2)
        mg = v3(Ac.bf16(KC * T), KC); mgB = Buf()
        Asb = v3(Ac.f32(4 * T), 4); AsbB = Buf()
        Rsb = v3(Ac.f32(4 * T), 4); RsbB = Buf()
        tg = Ac.f32(T); tgB = Buf()
        xs = [Ac.f32(512) for _ in range(2)]; xsB = [Buf(), Buf()]
        for c0 in range(0, D, 512):
            cw = min(512, D - c0)
            nch = cw // 128
            proj_fm(wb_ao, wB["ao"], QW // 128, c0, cw, lambda kc: attn[:, kc, :], [attnB], [0, 1, 2, 3])
            for cc in range(nch):
                P.op("act", lambda e, cc=cc: e.activation(out=Asb[:, cc, :], in_=psum[:, cc, 0:T], func=AF.Identity), reads=[bankB[cc]], writes=[AsbB])
            proj_fm(wb_ro, wB["ro"], NB, c0, cw, lambda kc: rec[:, kc, :], [recB], [4, 5, 6, 7])
            for cc in range(nch):
                P.op("act", lambda e, cc=cc: e.activation(out=Rsb[:, cc, :], in_=psum[:, 4 + cc, 0:T], func=AF.Identity), reads=[bankB[4 + cc]], writes=[RsbB])
            for gi, (sbv, sbB) in enumerate(((Asb, AsbB), (Rsb, RsbB))):
                banks = [0, 1, 2, 3] if gi == 0 else [4, 5, 6, 7]
                proj_fm(wb_in, wB["in"], KC, o_g + gi * D + c0, cw, lambda kc: hT[:, kc, :], [hTB], banks)
                for cc in range(nch):
                    ch = c0 // 128 + cc
                    P.op("act", lambda e, cc=cc, ch=ch, gi=gi, banks=banks: e.activation(out=tg, in_=psum[:, banks[cc], 0:T], func=AF.Tanh, scale=0.5,
                                                                                      bias=hbg[:, gi * KC + ch:gi * KC + ch + 1]),
                         reads=[bankB[banks[cc]], constB], writes=[tgB])
                    P.op("dve", lambda e, cc=cc, sbv=sbv: e.scalar_tensor_tensor(out=sbv[:, cc, :], in0=tg, scalar=1.0, in1=sbv[:, cc, :], op0=ALU.add, op1=ALU.mult),
                         reads=[tgB, sbB], writes=[sbB])
            for cc in range(nch):
                ch = c0 // 128 + cc
                P.op("pool", lambda e, cc=cc, ch=ch: e.tensor_tensor(out=mg[:, ch, :], in0=Asb[:, cc, :], in1=Rsb[:, cc, :], op=ALU.add),
                     reads=[AsbB, RsbB], writes=[mgB])
        xi_c = [0]

        def evac_wout(ci, c0, cw, banks):
            for tb in range(4):
                xv = xs[xi_c[0] % 2]; xB = xsB[xi_c[0] % 2]; xi_c[0] += 1
                P.dma("sp", xv[:, 0:cw], xo[k * T + tb * 128:k * T + (tb + 1) * 128, c0:c0 + cw], writes=[xB])
                P.op("dve", lambda e, xv=xv, tb=tb, cw=cw, banks=banks: e.scalar_tensor_tensor(out=xv[:, 0:cw], in0=psum[:, banks[tb], 0:cw], scalar=0.5, in1=xv[:, 0:cw],
                                                                                     op0=ALU.mult, op1=ALU.add),
                     reads=[bankB[banks[tb]], xB], writes=[xB])
                P.op("act", lambda e, xv=xv, tb=tb, ci=ci, cw=cw: e.activation(out=tg[:, 0:cw], in_=xv[:, 0:cw], func=AF.Square, accum_out=ssqP3[:, tb, ci:ci + 1]),
                     reads=[xB], writes=[tgB, ssqB])
                P.dma("sp", x1s[tb * 128:(tb + 1) * 128, c0:c0 + cw], xv[:, 0:cw], reads=[xB], writes=[scrX])
        prev_c = None
        for ci, c0 in enumerate(range(0, D, 512)):
            cw = min(512, D - c0)
            banks = [0, 1, 2, 3] if ci % 2 == 0 else [4, 5, 6, 7]
            proj_tm(wb_out, wB["out"], KC, c0, cw, lambda kc, tb: mg[:, kc, tb * 128:(tb + 1) * 128], [mgB], banks)
            if prev_c is not None:
                evac_wout(*prev_c)
            prev_c = (ci, c0, cw, banks)
        evac_wout(*prev_c)
        if k == 0:
            dump('x1', x1s, [scrX]); dump('mg', mg, [mgB])
        P.barrier()

        Ad = Alloc(BASE)
        KH = max(4, KC // 2)
        hm_parts = [v3(Ad.bf16(KH * T), KH) for _ in range((KC + KH - 1) // KH)]
        hmTB = Buf()

        def hm_sl(k0, n):
            return hm_parts[k0 // KH][:, k0 % KH:k0 % KH + n]
        act = v3(Ad.bf16(FC * T), FC); actB = Buf()
        U0 = Ad.o
        hn = Ad.bf16(D); hnB = Buf()
        ysn = [Ad.f32(512) for _ in range(2)]; ysnB = [Buf(), Buf()]
        stt1 = Ad.f32(16); stt1B = Buf()
        g_o = ppoff["gmlp"][0]

        def rstd_from_ssq():
            for tb in range(4):
                P.op("dve", lambda e, tb=tb: e.tensor_reduce(out=stt1[:, tb:tb + 1], in_=ssqP3[:, tb, :], axis=AX.X, op=ALU.add), reads=[ssqB], writes=[stt1B])
            P.op("dve", lambda e: e.tensor_scalar(out=stt1[:, 0:4], in0=stt1[:, 0:4], scalar1=1.0 / D, scalar2=EPS, op0=ALU.mult, op1=ALU.add), reads=[stt1B], writes=[stt1B])
            P.op("act", lambda e: e.activation(out=stt1[:, 8:12], in_=stt1[:, 0:4], func=AF.Sqrt), reads=[stt1B], writes=[stt1B])
            P.op("dve", lambda e: e.reciprocal(out=stt1[:, 4:8], in_=stt1[:, 8:12]), reads=[stt1B], writes=[stt1B])
        rstd_from_ssq()
        yi = 0
        for tb in range(4):
            for ci, c0 in enumerate(range(0, D, 512)):
                cw = min(512, D - c0)
                yv = ysn[yi % 2]; yB = ysnB[yi % 2]; yi += 1
                P.dma("sp", yv[:, 0:cw], x1s[tb * 128:(tb + 1) * 128, c0:c0 + cw], reads=[scrX], writes=[yB])
                P.op("dve", lambda e, yv=yv, tb=tb, c0=c0, cw=cw: e.tensor_scalar(out=hn[:, c0:c0 + cw], in0=yv[:, 0:cw], scalar1=stt1[:, 4 + tb:5 + tb], scalar2=None, op0=ALU.mult),
                     reads=[yB, stt1B], writes=[hnB])
            for k0 in range(0, KC, 4):
                n = min(4, KC - k0)
                pv, bank = transpose_blocks([hn[:, (k0 + j) * 128:(k0 + j + 1) * 128] for j in range(n)], [hnB], None, None)
                gb = pp_sb[:, g_o + k0:g_o + k0 + n].unsqueeze(2).to_broadcast([128, n, 128])
                P.op("dve", lambda e, pv=pv, gb=gb, k0=k0, n=n, tb=tb: e.tensor_tensor(out=hm_sl(k0, n)[:, :, tb * 128:(tb + 1) * 128], in0=pv, in1=gb, op=ALU.mult),
                     reads=[bankB[bank], constB], writes=[hmTB])
        P.barrier()
        Ad2 = Alloc(U0)
        rl = [Ad2.f32(T) for _ in range(2)]; rlB = [Buf(), Buf()]
        ys = [Ad2.f32(512) for _ in range(2)]; ysB = [Buf(), Buf()]
        gfs = Ad2.f32(512); gfB = Buf()
        stt = Ad2.f32(16); sttB = Buf()
        ri = 0
        for ci, c0 in enumerate(range(0, DFF, 512)):
            banks = [0, 1, 2, 3] if ci % 2 == 0 else [4, 5, 6, 7]
            proj_fm(wb_up, wB["up"], KC, c0, 512, lambda kc: hm_sl(kc, 1)[:, 0, :], [hmTB], banks)
            for cc in range(4):
                r = rl[ri % 2]; rB = rlB[ri % 2]; ri += 1
                P.op("act", lambda e, r=r, cc=cc, banks=banks: e.activation(out=r, in_=psum[:, banks[cc], 0:T], func=AF.Relu), reads=[bankB[banks[cc]]], writes=[rB])
                P.op("pool", lambda e, r=r, cc=cc, c0=c0: e.tensor_tensor(out=act[:, c0 // 128 + cc, :], in0=r, in1=r, op=ALU.mult), reads=[rB], writes=[actB])
        yi_c = [0]

        def evac_down(ci, c0, cw, banks):
            for tb in range(4):
                yv = ys[yi_c[0] % 2]; yB = ysB[yi_c[0] % 2]; yi_c[0] += 1
                P.dma("sp", yv[:, 0:cw], x1s[tb * 128:(tb + 1) * 128, c0:c0 + cw], reads=[scrX], writes=[yB])
                P.op("dve", lambda e, yv=yv, tb=tb, cw=cw, banks=banks: e.tensor_tensor(out=yv[:, 0:cw], in0=psum[:, banks[tb], 0:cw], in1=yv[:, 0:cw], op=ALU.add),
                     reads=[bankB[banks[tb]], yB], writes=[yB])
                P.op("act", lambda e, yv=yv, tb=tb, ci=ci, cw=cw: e.activation(out=rl[0][:, 0:cw], in_=yv[:, 0:cw], func=AF.Square, accum_out=ssqP3[:, tb, ci:ci + 1]),
                     reads=[yB], writes=[rlB[0], ssqB])
                P.dma("sp", x1s[tb * 128:(tb + 1) * 128, c0:c0 + cw], yv[:, 0:cw], reads=[yB], writes=[scrX])

        prev_d = None
        for ci, c0 in enumerate(range(0, D, 512)):
            cw = min(512, D - c0)
            banks = [0, 1, 2, 3] if ci % 2 == 0 else [4, 5, 6, 7]
            proj_tm(wb_dn, wB["dn"], FC, c0, cw, lambda kc, tb: act[:, kc, tb * 128:(tb + 1) * 128], [actB], banks)
            if prev_d is not None:
                evac_down(*prev_d)
            prev_d = (ci, c0, cw, banks)
        evac_down(*prev_d)

        def rstd2():
            for tb in range(4):
                P.op("dve", lambda e, tb=tb: e.tensor_reduce(out=stt[:, tb:tb + 1], in_=ssqP3[:, tb, :], axis=AX.X, op=ALU.add), reads=[ssqB], writes=[sttB])
            P.op("dve", lambda e: e.tensor_scalar(out=stt[:, 0:4], in0=stt[:, 0:4], scalar1=1.0 / D, scalar2=EPS, op0=ALU.mult, op1=ALU.add), reads=[sttB], writes=[sttB])
            P.op("act", lambda e: e.activation(out=stt[:, 8:12], in_=stt[:, 0:4], func=AF.Sqrt), reads=[sttB], writes=[sttB])
            P.op("dve", lambda e: e.reciprocal(out=stt[:, 4:8], in_=stt[:, 8:12]), reads=[sttB], writes=[sttB])
        rstd2()
        for ci, c0 in enumerate(range(0, D, 512)):
            cw = min(512, D - c0)
            P.dma("sp", gfs[:, 0:cw], gfin[:, c0:c0 + cw], writes=[gfB])
            for tb in range(4):
                yv = ys[yi_c[0] % 2]; yB = ysB[yi_c[0] % 2]; yi_c[0] += 1
                P.dma("sp", yv[:, 0:cw], x1s[tb * 128:(tb + 1) * 128, c0:c0 + cw], reads=[scrX], writes=[yB])
                P.op("dve", lambda e, yv=yv, tb=tb, cw=cw: e.scalar_tensor_tensor(out=yv[:, 0:cw], in0=yv[:, 0:cw], scalar=stt[:, 4 + tb:5 + tb], in1=gfs[:, 0:cw],
                                                                               op0=ALU.mult, op1=ALU.mult),
                     reads=[yB, sttB, gfB], writes=[yB])
                P.dma("sp", yout[k * T + tb * 128:k * T + (tb + 1) * 128, c0:c0 + cw], yv[:, 0:cw], reads=[yB], writes=[outB])
        P.barrier()

    for k in range(nOwn):
        own_tile(k)
        if cfg.get('stop') == 'own1':
            return finish()

    return finish()


def host_inputs(cfg, inp):
    c = dims(cfg)
    D, SP, SS, KC, NB, DR = c["D"], c["SP"], c["SS"], c["KC"], c["NB"], c["DR"]
    nPo, nSo, nPc, nSc, nOwn = c["nPo"], c["nSo"], c["nPc"], c["nSc"], c["nOwn"]
    f = np.float32
    xp = np.asarray(inp["x_prompt"], f)
    xs = np.asarray(inp["x_sample"], f)
    pp = np.zeros((128, c["NPP"]), f)
    off = c["ppoff"]

    def put(name, arr):
        o, n = off[name]
        assert arr.shape == (128, n), (name, arr.shape, n)
        pp[:, o:o + n] = arr
    put("gmix", np.asarray(inp["norm_mix"], f)[0].reshape(KC, 128).T)
    put("gmlp", np.asarray(inp["norm_mlp"], f)[0].reshape(KC, 128).T)
    cw = np.asarray(inp["conv_w"], f)[0]
    put("convw", cw.reshape(4, NB, 128).transpose(2, 1, 0).reshape(128, NB * 4))
    put("convb", np.asarray(inp["conv_b"], f)[0].reshape(NB, 128).T)
    for nm, key in (("ba", "lru_b_a"), ("bi", "lru_b_i"), ("lam", "lru_lambda")):
        put(nm, np.asarray(inp[key], f)[0].reshape(2, NB, 128).transpose(2, 0, 1).reshape(128, 2 * NB))
    put("bg", np.asarray(inp["b_gate"], f)[0].reshape(2, KC, 128).transpose(2, 0, 1).reshape(128, 2 * KC))
    put("qn", np.broadcast_to(np.asarray(inp["q_norm"], f)[0][None, :], (128, 128)))
    put("kn", np.broadcast_to(np.asarray(inp["k_norm"], f)[0][None, :], (128, 128)))
    put("iota", np.broadcast_to(np.arange(32, dtype=f)[None, :], (128, 32)))
    def rowcol(pos):
        pos = np.asarray(pos, np.int64)
        return np.ascontiguousarray(np.stack([pos // 64, pos % 64], -1).reshape(128, -1).astype(f))
    shared = dict(
        w_in=np.ascontiguousarray(np.asarray(inp["w_in"], f)[0]), w_ao=np.ascontiguousarray(np.asarray(inp["w_attn_out"], f)[0]),
        w_ro=np.ascontiguousarray(np.asarray(inp["w_rnn_out"], f)[0]), w_out=np.ascontiguousarray(np.asarray(inp["w_out"], f)[0]),
        w_up=np.ascontiguousarray(np.asarray(inp["w_up"], f)[0]), w_dn=np.ascontiguousarray(np.asarray(inp["w_down"], f)[0]),
        lru_wa=np.ascontiguousarray(np.asarray(inp["lru_w_a"], f)[0]), lru_wi=np.ascontiguousarray(np.asarray(inp["lru_w_i"], f)[0]),
        pp=pp, gfin=np.ascontiguousarray(np.broadcast_to(np.asarray(inp["norm_final"], f)[None, :], (128, D))),
        ident=np.eye(128, dtype=f),
        pos_p=rowcol(np.arange(SP // 128)[None, :] * 128 + np.arange(128)[:, None]),
        pos_s=rowcol(np.arange(SS // 128)[None, :] * 128 + np.arange(128)[:, None]),
        xc_s=np.ascontiguousarray(xs[0]),
    )
    maps = []
    for core in range(8):
        sp, half = core // 2, core % 2
        p0 = half * (SP // 2)
        s0 = core * (SS // 8)
        xo = np.concatenate([xp[sp, p0:p0 + SP // 2], xs[0, s0:s0 + SS // 8]], 0)
        xh = np.zeros((nOwn * 4, D), f)
        pos_o = np.zeros((128, nOwn * 4), np.int64)
        sel_p = np.zeros((128, nPo * nPc), f)
        sel_s = np.zeros((128, nSo * nSc), f)
        for k in range(nOwn):
            if k < nPo:
                src, st, S_ = xp[sp], p0 + k * T, SP
                sel_p[:, k * nPc + st // T] = 1.0
            else:
                src, st, S_ = xs[0], s0 + (k - nPo) * T, SS
                sel_s[:, (k - nPo) * nSc + st // T] = 1.0
            for j, tpos in enumerate((st - 1, st + T, st + T + 1)):
                if 0 <= tpos < S_:
                    xh[k * 4 + j] = src[tpos]
            for tb in range(4):
                pos_o[:, k * 4 + tb] = st + tb * 128 + np.arange(128)
        m = dict(shared)
        m.update(xc_p=np.ascontiguousarray(xp[sp]), xo=np.ascontiguousarray(xo), xh=xh, pos_o=rowcol(pos_o), sel_p=sel_p, sel_s=sel_s)
        maps.append(m)
    return maps


def assemble(cfg, results, B=4):
    c = dims(cfg)
    D, SP, SS = c["D"], c["SP"], c["SS"]
    yp = np.zeros((B, SP, D), np.float32)
    ys = np.zeros((1, SS, D), np.float32)
    for core in range(8):
        y = np.asarray(results[core]["y"], np.float32)
        sp, half = core // 2, core % 2
        p0 = half * (SP // 2)
        s0 = core * (SS // 8)
        yp[sp, p0:p0 + SP // 2] = y[:SP // 2]
        ys[0, s0:s0 + SS // 8] = y[SP // 2:]
    return yp, ys


_NC_CACHE = {}


def run(cfg, inp, trace=False):
    key = tuple(sorted((k, str(v)) for k, v in cfg.items()))
    if key not in _NC_CACHE:
        _NC_CACHE[key] = build(cfg)
    nc = _NC_CACHE[key]
    maps = host_inputs(cfg, inp)
    res = run_bass_kernel_spmd(nc, maps, core_ids=list(range(8)), **({"trace": True} if trace else {}))
    return assemble(cfg, res.results), res


def kernel(**inputs):
    (yp, ys), _ = run(FULL, inputs)
    return yp, ys
```
